# Optimizing a Trainium2 kernel written in Bass

```python
import jax
import jax.numpy as jnp
from jax import lax
import numpy as np


D_MODEL = 1024
BATCH = 32
SEQ = 2048
DEPTH = 4

GRID_W = 64
CTX_LEN = 256
HEAD_DIM = 128
N_HEADS = D_MODEL // HEAD_DIM
N_KV_HEADS = N_HEADS // 4
Q_GROUP = N_HEADS // N_KV_HEADS
Q_BLOCK = 128
ATTN_SCALE = HEAD_DIM ** -0.5
ROPE_THETA = 10000.0
ROPE_PAIRS = HEAD_DIM // 4
CONV_WIDTH = 31
CONV_CH = D_MODEL
POOL_WINDOWS = (2, 4, 8, 16)
POOL_GROUPS = len(POOL_WINDOWS)
POOL_CH = D_MODEL
POOL_GROUP_CH = POOL_CH // POOL_GROUPS
N_BRANCH = 3
D_FF = 2816
FFN_CONV_WIDTH = 3
N_MOD = 6
ATTN_W = N_HEADS * HEAD_DIM
KV_W = N_KV_HEADS * HEAD_DIM
Q_END = ATTN_W
K_END = Q_END + KV_W
V_END = K_END + KV_W
CONV_END = V_END + 2 * CONV_CH
POOL_END = CONV_END + POOL_CH
D_IN = POOL_END + N_BRANCH * D_MODEL
DEEPNORM_ALPHA = (2 * DEPTH) ** 0.25
DEEPNORM_BETA = (8 * DEPTH) ** -0.25
LN_EPS = 1e-5
RMS_EPS = 1e-6

kernel_name = "hybrid_gated_parallel_diffusion_trunk"


def layer_norm(t, g, b):
    tf = t.astype(jnp.float32)
    mu = jnp.mean(tf, axis=-1, keepdims=True)
    var = jnp.mean(jnp.square(tf - mu), axis=-1, keepdims=True)
    return ((tf - mu) * lax.rsqrt(var + LN_EPS)).astype(t.dtype) * g + b


def rms_norm(t, g):
    tf = t.astype(jnp.float32)
    ms = jnp.mean(jnp.square(tf), axis=-1, keepdims=True)
    return (tf * lax.rsqrt(ms + RMS_EPS)).astype(t.dtype) * g


def ada_mod(cond, w, b, n):
    m = jax.nn.silu(cond) @ w[:, :n * D_MODEL] + b[:n * D_MODEL]
    return m.reshape(m.shape[:-1] + (n, D_MODEL))


def modulate(h, shift, scale):
    return h * (1.0 + scale) + shift


def axial_rope_tables(n_tokens):
    n_rows = n_tokens // GRID_W
    row = jnp.repeat(jnp.arange(n_rows, dtype=jnp.float32), GRID_W)
    col = jnp.tile(jnp.arange(GRID_W, dtype=jnp.float32), n_rows)
    inv_freq = ROPE_THETA ** (-jnp.arange(ROPE_PAIRS, dtype=jnp.float32) / ROPE_PAIRS)
    ang = jnp.stack([row, col], axis=-1)[..., None] * inv_freq
    return jnp.cos(ang), jnp.sin(ang)


def apply_rope(t, cos, sin):
    b, n, h, _ = t.shape
    tf = t.astype(jnp.float32).reshape(b, n, h, 2, 2, ROPE_PAIRS)
    t1, t2 = tf[..., 0, :], tf[..., 1, :]
    c = cos[None, :, None]
    s = sin[None, :, None]
    out = jnp.stack([t1 * c - t2 * s, t1 * s + t2 * c], axis=-2)
    return out.reshape(t.shape).astype(t.dtype)


def kv_heads(z_kv, k_gain):
    lead = z_kv.shape[:-1]
    k = rms_norm(z_kv[..., :KV_W].reshape(lead + (N_KV_HEADS, HEAD_DIM)), k_gain)
    v = z_kv[..., KV_W:].reshape(lead + (N_KV_HEADS, HEAD_DIM))
    return k, v


def qkv_heads(z, q_gain, k_gain):
    q = rms_norm(z[..., :Q_END].reshape(z.shape[:-1] + (N_HEADS, HEAD_DIM)), q_gain)
    k, v = kv_heads(z[..., Q_END:V_END], k_gain)
    return q, k, v


def latent_attention(q, k_lat, v_lat, k_ctx, v_ctx):
    b, n = q.shape[0], q.shape[1]
    k_all = jnp.concatenate([k_ctx, k_lat], axis=1)
    v_all = jnp.concatenate([v_ctx, v_lat], axis=1)
    n_blk = n // Q_BLOCK
    qb = jnp.moveaxis(q.reshape(b, n_blk, Q_BLOCK, N_KV_HEADS, Q_GROUP, HEAD_DIM), 1, 0)

    def block(qi):
        s = jnp.einsum('bqhgd,bkhd->bhgqk', qi, k_all, preferred_element_type=jnp.float32) * ATTN_SCALE
        p = jax.nn.softmax(s, axis=-1).astype(v_all.dtype)
        return jnp.einsum('bhgqk,bkhd->bqhgd', p, v_all)

    o = lax.map(block, qb)
    return jnp.moveaxis(o, 0, 1).reshape(b, n, ATTN_W)


def context_attention(q, k, v):
    b, n = q.shape[0], q.shape[1]
    qg = q.reshape(b, n, N_KV_HEADS, Q_GROUP, HEAD_DIM)
    s = jnp.einsum('bqhgd,bkhd->bhgqk', qg, k, preferred_element_type=jnp.float32) * ATTN_SCALE
    p = jax.nn.softmax(s, axis=-1).astype(v.dtype)
    return jnp.einsum('bhgqk,bkhd->bqhgd', p, v).reshape(b, n, ATTN_W)


def depthwise_conv(t, w, b):
    k = w.shape[0]
    y = lax.conv_general_dilated(
        t, w[:, None, :], window_strides=(1,), padding=[((k - 1) // 2, k // 2)],
        dimension_numbers=('NWC', 'WIO', 'NWC'), feature_group_count=t.shape[-1])
    return y + b


def conformer_conv(u, dw_w, dw_b, ln_g, ln_b, pw_w, pw_b):
    a, gt = jnp.split(u, 2, axis=-1)
    h = depthwise_conv(a * jax.nn.sigmoid(gt), dw_w, dw_b)
    h = jax.nn.silu(layer_norm(h, ln_g, ln_b))
    return h @ pw_w + pw_b


def multiscale_pool(u, pool_w, pool_scale):
    b, n, _ = u.shape
    uf = u.astype(jnp.float32)
    cs = jnp.pad(jnp.cumsum(uf, axis=1), ((0, 0), (1, 0), (0, 0)))
    t = jnp.arange(n)
    outs = []
    for g, w in enumerate(POOL_WINDOWS):
        lo = jnp.clip(t - w // 2, 0, n)
        hi = jnp.clip(t - w // 2 + w, 0, n)
        sl = slice(g * POOL_GROUP_CH, (g + 1) * POOL_GROUP_CH)
        seg = cs[:, :, sl]
        win_sum = jnp.take(seg, hi, axis=1) - jnp.take(seg, lo, axis=1)
        cnt = (hi - lo).astype(jnp.float32)[None, :, None]
        outs.append(win_sum / cnt - uf[:, :, sl])
    pooled = jnp.stack(outs, axis=2).astype(u.dtype)
    mixed = jnp.einsum('blgi,gio->blgo', pooled, pool_w)
    return mixed.reshape(b, n, POOL_CH) * pool_scale


def merge_branches(z, attn_o, conv_dw_w, conv_dw_b, conv_ln_g, conv_ln_b, conv_pw_w, conv_pw_b,
                   pool_w, pool_scale, w_out, b_out):
    conv_o = conformer_conv(z[..., V_END:CONV_END], conv_dw_w, conv_dw_b, conv_ln_g, conv_ln_b,
                            conv_pw_w, conv_pw_b)
    pool_o = multiscale_pool(z[..., CONV_END:POOL_END], pool_w, pool_scale)
    gates = jax.nn.sigmoid(z[..., POOL_END:].reshape(z.shape[:-1] + (N_BRANCH, D_MODEL)))
    m = gates[..., 0, :] * attn_o + gates[..., 1, :] * conv_o + gates[..., 2, :] * pool_o
    return m @ w_out + b_out


def conv_ffn(h, w_up, dw_w, dw_b, w_down):
    a, u = jnp.split(h @ w_up, 2, axis=-1)
    a = depthwise_conv(a, dw_w, dw_b)
    return (jax.nn.silu(a) * u) @ w_down


def setup_inputs(seed: int = 0) -> dict:
    key = jax.random.key(seed)
    ks = iter(jax.random.split(key, 32))

    def nrm(shape, scale):
        return jax.random.normal(next(ks), shape, jnp.float32) * scale

    L = DEPTH
    D = D_MODEL
    return {
        'x': nrm((BATCH, SEQ, D), 1.0),
        'c': nrm((BATCH, D), 1.0),
        'ctx': nrm((BATCH, CTX_LEN, D), 1.0),
        'c_ctx': nrm((D,), 1.0),
        'w_ada': nrm((L, D, N_MOD * D), 0.5 * D ** -0.5),
        'b_ada': nrm((L, N_MOD * D), 0.02),
        'w_in': nrm((L, D, D_IN), D ** -0.5),
        'b_in': nrm((L, D_IN), 0.02),
        'q_gain': 1.0 + nrm((L, HEAD_DIM), 0.02),
        'k_gain': 1.0 + nrm((L, HEAD_DIM), 0.02),
        'conv_dw_w': nrm((L, CONV_WIDTH, CONV_CH), CONV_WIDTH ** -0.5),
        'conv_dw_b': nrm((L, CONV_CH), 0.02),
        'conv_ln_g': 1.0 + nrm((L, CONV_CH), 0.02),
        'conv_ln_b': nrm((L, CONV_CH), 0.02),
        'conv_pw_w': nrm((L, CONV_CH, D), CONV_CH ** -0.5),
        'conv_pw_b': nrm((L, D), 0.02),
        'pool_w': nrm((L, POOL_GROUPS, POOL_GROUP_CH, POOL_GROUP_CH), POOL_GROUP_CH ** -0.5),
        'pool_scale': 1.0 + nrm((L, POOL_CH), 0.1),
        'w_out': nrm((L, D, D), DEEPNORM_BETA * D ** -0.5),
        'b_out': nrm((L, D), 0.02),
        'ln1_g': 1.0 + nrm((L, D), 0.02),
        'ln1_b': nrm((L, D), 0.02),
        'ln2_g': 1.0 + nrm((L, D), 0.02),
        'ln2_b': nrm((L, D), 0.02),
        'w_up': nrm((L, D, 2 * D_FF), D ** -0.5),
        'ffn_dw_w': nrm((L, FFN_CONV_WIDTH, D_FF), FFN_CONV_WIDTH ** -0.5),
        'ffn_dw_b': nrm((L, D_FF), 0.02),
        'w_down': nrm((L, D_FF, D), DEEPNORM_BETA * D_FF ** -0.5),
    }


def reference(x, c, ctx, c_ctx, w_ada, b_ada, w_in, b_in, q_gain, k_gain,
              conv_dw_w, conv_dw_b, conv_ln_g, conv_ln_b, conv_pw_w, conv_pw_b,
              pool_w, pool_scale, w_out, b_out, ln1_g, ln1_b, ln2_g, ln2_b,
              w_up, ffn_dw_w, ffn_dw_b, w_down):
    cos, sin = axial_rope_tables(x.shape[1])
    for l in range(DEPTH):
        last = l == DEPTH - 1

        def mix(z, attn_o):
            return merge_branches(z, attn_o, conv_dw_w[l], conv_dw_b[l], conv_ln_g[l], conv_ln_b[l],
                                  conv_pw_w[l], conv_pw_b[l], pool_w[l], pool_scale[l],
                                  w_out[l], b_out[l])

        def ffn(h):
            return conv_ffn(h, w_up[l], ffn_dw_w[l], ffn_dw_b[l], w_down[l])

        ml = ada_mod(c, w_ada[l], b_ada[l], N_MOD)[:, None]
        mc = ada_mod(c_ctx, w_ada[l], b_ada[l], 2 if last else N_MOD)

        hl = modulate(x, ml[..., 0, :], ml[..., 1, :])
        hc = modulate(ctx, mc[0], mc[1])
        zl = hl @ w_in[l] + b_in[l]
        q_l, k_l, v_l = qkv_heads(zl, q_gain[l], k_gain[l])
        q_l = apply_rope(q_l, cos, sin)
        k_l = apply_rope(k_l, cos, sin)
        if last:
            k_c, v_c = kv_heads(hc @ w_in[l][:, Q_END:V_END] + b_in[l][Q_END:V_END], k_gain[l])
        else:
            zc = hc @ w_in[l] + b_in[l]
            q_c, k_c, v_c = qkv_heads(zc, q_gain[l], k_gain[l])
        attn_l = latent_attention(q_l, k_l, v_l, k_c, v_c)
        x = layer_norm(DEEPNORM_ALPHA * x + ml[..., 2, :] * mix(zl, attn_l), ln1_g[l], ln1_b[l])

        x = layer_norm(DEEPNORM_ALPHA * x + ml[..., 5, :] * ffn(modulate(x, ml[..., 3, :], ml[..., 4, :])),
                       ln2_g[l], ln2_b[l])

        if not last:
            attn_c = context_attention(q_c, k_c, v_c)
            ctx = layer_norm(DEEPNORM_ALPHA * ctx + mc[2] * mix(zc, attn_c), ln1_g[l], ln1_b[l])
            ctx = layer_norm(DEEPNORM_ALPHA * ctx + mc[5] * ffn(modulate(ctx, mc[3], mc[4])),
                             ln2_g[l], ln2_b[l])
    return x
```

```python
import contextlib
import numpy as np
import concourse.bass as bass
import concourse.mybir as mybir
from concourse.bass_utils import run_bass_kernel_spmd

F32 = mybir.dt.float32
BF16 = mybir.dt.bfloat16
AF = mybir.ActivationFunctionType
ALU = mybir.AluOpType

D = 1024
KC = 8
T_LAT = 2048
T_CTX = 256
T_ALL = T_LAT + T_CTX
CH = 512
DEPTH = 4
BATCH = 32
D_FF = 2816
NJF = D_FF // 128
D_IN = 7680
Q_END, K_END, V_END, CONV_END, POOL_END = 1024, 1280, 1536, 3584, 4608
POOL_WINDOWS = (2, 4, 8, 16)
ALPHA = float((2 * DEPTH) ** 0.25)
LN_EPS = 1e-5
RMS_EPS = 1e-6
ATTN_SCALE = float(128 ** -0.5)
HL = 15
EXT = 544


def param_layout(depth):
    lay = {}
    col = 0
    spec = [("b_ada", 48), ("b_in", 60), ("q_gain", 1), ("k_gain", 1), ("conv_dw_w", 31 * 8),
            ("conv_dw_b", 8), ("conv_ln_g", 8), ("conv_ln_b", 8), ("conv_pw_b", 8), ("pool_scale", 8),
            ("b_out", 8), ("ln1_g", 8), ("ln1_b", 8), ("ln2_g", 8), ("ln2_b", 8),
            ("ffn_dw_w", 3 * NJF), ("ffn_dw_b", NJF)]
    for l in range(depth):
        for name, n in spec:
            lay[(name, l)] = col
            col += n
    return lay, col


def pack_params(inputs, depth):
    lay, ncol = param_layout(depth)
    pp = np.zeros((128, ncol), np.float32)

    def put(name, l, arr2d):
        a = np.asarray(arr2d, np.float32)
        m = a.shape[0]
        nch = a.shape[1] // 128
        blk = a.reshape(m, nch, 128).transpose(2, 0, 1).reshape(128, m * nch)
        c0 = lay[(name, l)]
        pp[:, c0:c0 + m * nch] = blk

    for l in range(depth):
        for name in ("b_ada", "b_in", "q_gain", "k_gain", "conv_dw_b", "conv_ln_g", "conv_ln_b",
                     "conv_pw_b", "pool_scale", "b_out", "ln1_g", "ln1_b", "ln2_g", "ln2_b", "ffn_dw_b"):
            put(name, l, np.asarray(inputs[name][l])[None, :])
        put("conv_dw_w", l, inputs["conv_dw_w"][l])
        put("ffn_dw_w", l, inputs["ffn_dw_w"][l])
    return pp


def const_tables():
    cb = np.zeros((128, 5, 128), np.float32)
    cb[:, 0, :] = np.eye(128, dtype=np.float32)
    cb[:, 1, :] = 1.0 / 1024.0
    cb[:, 2, :] = 1.0 / 128.0
    cb[:, 3, :] = 1.0
    rot = np.zeros((128, 128), np.float32)
    for a in range(2):
        for i in range(32):
            rot[a * 64 + 32 + i, a * 64 + i] = -1.0
            rot[a * 64 + i, a * 64 + 32 + i] = 1.0
    cb[:, 4, :] = rot
    t = np.arange(T_LAT)
    row = (t // 64).astype(np.float32)
    colp = (t % 64).astype(np.float32)
    inv_freq = (np.float32(10000.0) ** (-np.arange(32, dtype=np.float32) / np.float32(32))).astype(np.float32)
    cs = np.zeros((2, 128, T_LAT), np.float32)
    for p in range(128):
        pos = row if p < 64 else colp
        ang = (pos * inv_freq[p % 32]).astype(np.float32)
        cs[0, p] = np.cos(ang)
        cs[1, p] = np.sin(ang)
    et = np.ones((128, 4, 2, 8), np.float32)
    n = 4096
    for wi, w in enumerate(POOL_WINDOWS):
        for i in range(8):
            tt = i
            lo = max(tt - w // 2, 0)
            hi = min(tt - w // 2 + w, n)
            et[:, wi, 0, i] = np.float32(w) / np.float32(hi - lo)
            tt = n - 8 + i
            lo = max(tt - w // 2, 0)
            hi = min(tt - w // 2 + w, n)
            et[:, wi, 1, i] = np.float32(w) / np.float32(hi - lo)
    return cb, cs, et


class TL:
    __slots__ = ("ap", "keys")

    def __init__(self, ap, keys):
        self.ap = ap
        self.keys = tuple(keys)

    def v(self, ap):
        return TL(ap, self.keys)


def _ap(x):
    return x.ap if isinstance(x, TL) else x


class _Op:
    __slots__ = ("eng", "fn", "deps", "dma", "sig", "signo", "dsem", "dval", "tag")

    def __init__(self, eng, fn, deps, dma):
        self.eng = eng
        self.fn = fn
        self.deps = deps
        self.dma = dma
        self.sig = False
        self.signo = 0
        self.dsem = 0
        self.dval = 0


EPOCH = 30000
FAST_RECIP = False
import os
KOPT = int(os.environ.get('KOPT', '7'))
ND = 16


class Sched:
    def __init__(self, nc):
        self.nc = nc
        self.ops = []
        self.lastw = {}
        self.readers = {}
        self.dmas = []
        self.tag = None
        self.use_tags = False

    def op(self, eng, fn, ins=(), outs=(), dma=False):
        deps = set()
        for x in ins:
            if isinstance(x, TL):
                for k in x.keys:
                    w = self.lastw.get(k)
                    if w is not None:
                        deps.add(w)
        for x in outs:
            if isinstance(x, TL):
                for k in x.keys:
                    w = self.lastw.get(k)
                    if w is not None:
                        deps.add(w)
                    for r in self.readers.get(k, ()):
                        deps.add(r)
        idx = len(self.ops)
        if dma:
            k = len(self.dmas)
            if k >= ND:
                deps.add(self.dmas[k - ND])
            self.dmas.append(idx)
        o = _Op(eng, fn, sorted(deps), dma)
        o.tag = self.tag
        if dma:
            k = len(self.dmas) - 1
            o.dsem = k % ND
            o.dval = 16 * (k // ND + 1)
        for d in o.deps:
            self.ops[d].sig = True
        self.ops.append(o)
        for x in ins:
            if isinstance(x, TL):
                for k in x.keys:
                    self.readers.setdefault(k, []).append(idx)
        for x in outs:
            if isinstance(x, TL):
                for k in x.keys:
                    self.lastw[k] = idx
                    self.readers[k] = []
        return idx

    def act(self, out, in_, func, bias=0.0, scale=1.0, eng="act"):
        o, i, b, s = _ap(out), _ap(in_), _ap(bias), _ap(scale)
        self.op("act", lambda e: e.activation(out=o, in_=i, func=func, bias=b, scale=s),
                (in_, bias, scale), (out,))

    def tt(self, eng, out, in0, in1, op):
        o, a, b = _ap(out), _ap(in0), _ap(in1)
        self.op(eng, lambda e: e.tensor_tensor(out=o, in0=a, in1=b, op=op), (in0, in1), (out,))

    def ts(self, eng, out, in0, s1, s2, op0, op1=None):
        o, a, x1, x2 = _ap(out), _ap(in0), _ap(s1), _ap(s2)
        if op1 is None:
            self.op(eng, lambda e: e.tensor_scalar(out=o, in0=a, scalar1=x1, scalar2=None, op0=op0),
                    (in0, s1), (out,))
        else:
            self.op(eng, lambda e: e.tensor_scalar(out=o, in0=a, scalar1=x1, scalar2=x2, op0=op0, op1=op1),
                    (in0, s1, s2), (out,))

    def stt(self, eng, out, in0, sc, in1, op0, op1):
        o, a, s, b = _ap(out), _ap(in0), _ap(sc), _ap(in1)
        self.op(eng, lambda e: e.scalar_tensor_tensor(out=o, in0=a, scalar=s, in1=b, op0=op0, op1=op1),
                (in0, sc, in1), (out,))

    def copy(self, eng, out, in_):
        o, i = _ap(out), _ap(in_)
        if eng == "act":
            self.op(eng, lambda e: e.copy(out=o, in_=i), (in_,), (out,))
        else:
            self.op(eng, lambda e: e.tensor_copy(out=o, in_=i), (in_,), (out,))

    def memset(self, eng, out, val):
        o = _ap(out)
        self.op(eng, lambda e: e.memset(o, val), (), (out,))

    def recip(self, out, in_):
        o, i = _ap(out), _ap(in_)
        if FAST_RECIP:
            self.op("dve", lambda e: e.reciprocal_approx_fast(out=o, in_=i), (in_,), (out,))
        else:
            self.op("dve", lambda e: e.reciprocal(out=o, in_=i), (in_,), (out,))

    def mm(self, out, pairs, start=True, stop=True):
        o = _ap(out)
        ps = [(_ap(a), _ap(b)) for a, b in pairs]
        n = len(ps)

        def fn(e):
            r = None
            for i, (a, b) in enumerate(ps):
                r = e.matmul(o, a, b, start=(start and i == 0), stop=(stop and i == n - 1))
            return r
        ins = [a for a, _ in pairs] + [b for _, b in pairs]
        if not start:
            ins.append(out)
        self.op("pe", fn, ins, (out,))

    def dma(self, eng, out, in_):
        o, i = _ap(out), _ap(in_)
        return self.op(eng, lambda e: e.dma_start(out=o, in_=i), (in_,), (out,), dma=True)

    def emit(self, final_deps):
        nc = self.nc
        engs = ["pe", "act", "dve", "pool", "sp"]
        fin = _Op("sp", None, sorted(final_deps), False)
        fin.tag = None
        for d in fin.deps:
            self.ops[d].sig = True
        self.ops.append(fin)
        cnt = {e: 0 for e in engs}
        for o in self.ops:
            if o.dma or not o.sig:
                continue
            cnt[o.eng] += 1
            o.signo = cnt[o.eng]
        with contextlib.ExitStack() as st:
            esems = {}
            for e in engs:
                nep = max(1, (cnt[e] + EPOCH - 1) // EPOCH)
                esems[e] = [st.enter_context(nc.semaphore(f"s_{e}{i}")) for i in range(nep)]
            dsems = [st.enter_context(nc.semaphore(f"s_dma{i}")) for i in range(ND)]
            block = st.enter_context(nc.Block())
            per = {e: [o for o in self.ops if o.eng == e] for e in engs}
            ops = self.ops

            def run(e, eng):
                seen_e = {}
                seen_d = {}
                for o in per[e]:
                    for d in o.deps:
                        p = ops[d]
                        if p.dma:
                            if seen_d.get(p.dsem, 0) >= p.dval:
                                continue
                            seen_d[p.dsem] = p.dval
                            eng.wait_ge(dsems[p.dsem], p.dval)
                        else:
                            if p.eng == e and e == "pe":
                                continue
                            if seen_e.get(p.eng, 0) >= p.signo:
                                continue
                            seen_e[p.eng] = p.signo
                            ep = (p.signo - 1) // EPOCH
                            eng.wait_ge(esems[p.eng][ep], p.signo - ep * EPOCH)
                    if o.fn is None:
                        continue
                    if self.use_tags and o.tag:
                        with nc.named_scope(o.tag):
                            r = o.fn(eng)
                    else:
                        r = o.fn(eng)
                    if o.dma:
                        r.then_inc(dsems[o.dsem], 16)
                    elif o.sig:
                        ep = (o.signo - 1) // EPOCH
                        r.then_inc(esems[e][ep], 1)

            block.tensor(lambda eng: run("pe", eng))
            block.scalar(lambda eng: run("act", eng))
            block.vector(lambda eng: run("dve", eng))
            block.gpsimd(lambda eng: run("pool", eng))
            block.sync(lambda eng: run("sp", eng))


def build_program(nb, depth, debug=False):
    nc = bass.Bass("TRN2", target_bir_lowering=False)
    lay, npcol = param_layout(depth)
    NS = nb + 1

    def din(name, shape, dt=F32):
        return nc.dram_tensor(name, list(shape), dt, kind="ExternalInput").ap()

    xT = din("xT", [nb, D, T_LAT])
    ctxT = din("ctxT", [nb, D, T_CTX])
    csT = din("csT", [128, KC, NS])
    pp_d = din("pp", [128, npcol])
    bvb_d = din("bvb", [128, depth, 256])
    cb_d = din("cb", [128, 5, 128])
    cs_d = din("cs", [2, 128, T_LAT])
    et_d = din("et", [128, 4, 2, 8])
    w_ada = din("w_ada", [depth, D, 6 * D])
    w_in = din("w_in", [depth, D, D_IN])
    conv_pw_w = din("conv_pw_w", [depth, D, D])
    pool_w = din("pool_w", [depth, 4, 256, 256])
    w_out = din("w_out", [depth, D, D])
    w_up = din("w_up", [depth, D, 2 * D_FF])
    w_down = din("w_down", [depth, D_FF, D])
    outT = nc.dram_tensor("outT", [nb, D, T_LAT], F32, kind="ExternalOutput").ap()

    win_b = nc.dram_tensor("win_b", [depth, 60, 128, KC, 128], BF16).ap()
    pw_b = nc.dram_tensor("pw_b", [depth, 8, 128, KC, 128], BF16).ap()
    wo_b = nc.dram_tensor("wo_b", [depth, 8, 128, KC, 128], BF16).ap()
    wup_b = nc.dram_tensor("wup_b", [depth, 44, 128, KC, 128], BF16).ap()
    plw_b = nc.dram_tensor("plw_b", [depth, 4, 128, 2, 256], BF16).ap()
    wd_b = nc.dram_tensor("wd_b", [depth, 2, NJF, 128, 512], BF16).ap()

    S = Sched(nc)
    S.use_tags = (debug == "prof")
    st = contextlib.ExitStack()

    def sb(name, shape, dt):
        return st.enter_context(nc.sbuf_tensor(name, list(shape), dt))

    def pst(name):
        return st.enter_context(nc.psum_tensor(name, [128, 512], F32))

    X = sb("X", [128, KC, T_ALL], F32)
    KT = sb("KT", [128, 2, T_ALL], BF16)
    V = sb("V", [128, T_ALL // 128, 256], BF16)
    HC = sb("HC", [128, KC, EXT], BF16)
    HS = sb("HS", [128, KC, 16], BF16)
    PP = sb("PP", [128, npcol], F32)
    ADA = sb("ADA", [128, depth, 6, KC, NS], F32)
    CST = sb("CST", [128, KC, NS], F32)
    CB = sb("CB", [128, 5, 128], BF16)
    ET = sb("ET", [128, 4, 2, 8], F32)
    COS = sb("COS", [128, CH], F32)
    SIN = sb("SIN", [128, CH], F32)
    NWS = 9
    WS = sb("WS", [128, NWS, KC * 128], BF16)
    ARENA = sb("ARENA", [128, 4, 8, EXT], BF16)
    NPT = 5
    PT = sb("PT", [128, NPT, CH], BF16)
    NT32 = 7
    T32 = sb("T32", [128, NT32, EXT], F32)
    MURS = sb("MURS", [128, 2, CH], F32)
    NT16 = 4
    T16 = sb("T16", [128, NT16, CH], BF16)
    NDG = 16
    DG = sb("DG", [128, NDG, 128], BF16)
    PL = sb("PL", [128, 2, 2, CH], BF16)
    NPG = 4 if (KOPT & 2) else 2
    PG = [pst(f"pg{i}") for i in range(NPG)]
    PD = [pst(f"pd{i}") for i in range(4)]

    ctr = {"pg": 0, "ws": 0, "pt": 0, "t32": 0, "t16": 0, "dg": 0}

    def nxt(name, n):
        i = ctr[name] % n
        ctr[name] += 1
        return i

    def pg():
        i = nxt("pg", NPG)
        return TL(PG[i][:, :], [("PG", i)])

    def pd(i):
        return TL(PD[i][:, :], [("PD", i)])

    def pstat(i):
        return TL(PD[2 + i][:, :], [("PD", 2 + i)])

    def t32():
        i = nxt("t32", NT32)
        return TL(T32[:, i, :], [("T32", i)])

    def t16():
        i = nxt("t16", NT16)
        return TL(T16[:, i, :], [("T16", i)])

    def ptile():
        i = nxt("pt", NPT)
        return TL(PT[:, i, :], [("PT", i)])

    def dgt():
        i = nxt("dg", NDG)
        return TL(DG[:, i, :], [("DG", i)])

    def wslot(src_ap, src_key, width=KC * 128):
        i = nxt("ws", NWS)
        t = TL(WS[:, i, 0:width], [("WS", i)])
        S.dma("sp", t, TL(src_ap, [src_key]))
        return i, t

    def pcol(name, l, c):
        c0 = lay[(name, l)] + c
        return TL(PP[:, c0:c0 + 1], [("PP",)])

    IDENT = TL(CB[:, 0, :], [("CB",)])
    ONESD = TL(CB[:, 1, :], [("CB",)])
    ONESH = TL(CB[:, 2, :], [("CB",)])
    ONES1 = TL(CB[:, 3, :], [("CB",)])
    ROT = TL(CB[:, 4, :], [("CB",)])

    def Xt(j, c, a, b):
        return TL(X[:, j, a:b], [("X", j, c)])

    F32T = []
    for i4 in range(4):
        apv = ARENA[:, 1, 2 * i4:2 * i4 + 2, :].rearrange("p a b -> p (a b)").bitcast(F32)
        F32T.append(TL(apv, [("AR", 1, 2 * i4), ("AR", 1, 2 * i4 + 1)]))
    F32M = [TL(MURS[:, 0, :], [("MURS", 0)]), TL(MURS[:, 1, :], [("MURS", 1)])]

    def ar(blk, j, a=0, b=CH):
        return TL(ARENA[:, blk, j, a:b], [("AR", blk, j)])

    S.dma("sp", TL(PP[:, :], [("PP",)]), pp_d)
    S.dma("sp", TL(CST[:, :, :], [("CST",)]), csT)
    S.dma("sp", TL(ET[:, :, :, :], [("ET",)]), et_d)
    S.dma("pool", TL(CB[:, :, :], [("CB",)]), cb_d)
    cst = TL(CST[:, :, :], [("CST",)])
    S.act(cst, cst, AF.Silu)

    for l in range(depth):
        for g in range(48):
            i = nxt("ws", NWS)
            wt = TL(WS[:, i, :].bitcast(F32), [("WS", i)])
            half = []
            ps = pg()
            for hh in range(2):
                if hh == 1:
                    i = nxt("ws", NWS)
                    wt = TL(WS[:, i, :].bitcast(F32), [("WS", i)])
                src = w_ada[l, hh * 512:(hh + 1) * 512, g * 128:(g + 1) * 128].rearrange("(k p) c -> p k c", p=128)
                S.dma("sp", TL(wt.ap.rearrange("p (k c) -> p k c", k=4), wt.keys), src)
                half.append(wt)
            pairs = []
            for hh in range(2):
                wv = half[hh].ap.rearrange("p (k c) -> p k c", k=4)
                for k4 in range(4):
                    pairs.append((TL(wv[:, k4, :], half[hh].keys), TL(CST[:, hh * 4 + k4, :], [("CST",)])))
            S.mm(TL(ps.ap[:, 0:NS], ps.keys), pairs)
            v, kc = g // 8, g % 8
            addone = 1.0 if v in (1, 4) else 0.0
            S.ts("dve", TL(ADA[:, l, v, kc, :], [("ADA",)]), TL(ps.ap[:, 0:NS], ps.keys),
                 pcol("b_ada", l, g), addone, ALU.add, ALU.add)

    for l in range(depth):
        for g in range(60):
            S.dma("pool", TL(win_b[l, g], [("win", l, g)]),
                  w_in[l, :, g * 128:(g + 1) * 128].rearrange("(k p) c -> p k c", p=128))
        for g in range(8):
            S.dma("pool", TL(pw_b[l, g], [("pw", l, g)]),
                  conv_pw_w[l, :, g * 128:(g + 1) * 128].rearrange("(k p) c -> p k c", p=128))
            S.dma("pool", TL(wo_b[l, g], [("wo", l, g)]),
                  w_out[l, :, g * 128:(g + 1) * 128].rearrange("(k p) c -> p k c", p=128))
        for g in range(4):
            S.dma("pool", TL(plw_b[l, g], [("plw", l, g)]),
                  pool_w[l, g].rearrange("(i p) o -> p i o", p=128))
        for g in range(44):
            S.dma("pool", TL(wup_b[l, g], [("wup", l, g)]),
                  w_up[l, :, g * 128:(g + 1) * 128].rearrange("(k p) c -> p k c", p=128))
        for hf in range(2):
            for j in range(NJF):
                S.dma("pool", TL(wd_b[l, hf, j], [("wd", l, hf, j)]),
                      w_down[l, j * 128:(j + 1) * 128, hf * 512:(hf + 1) * 512])

    chunks = [(i * CH, CH, False, i == 0, i == T_LAT // CH - 1, i) for i in range(T_LAT // CH)]
    chunks.append((T_LAT, T_CTX, True, True, True, T_LAT // CH))

    def load_w(kind, l, g):
        src = {"win": win_b, "pw": pw_b, "wo": wo_b, "wup": wup_b}[kind]
        i, t = wslot(src[l, g].rearrange("p k c -> p (k c)"), (kind, l, g))
        return TL(WS[:, i, :].rearrange("p (k c) -> p k c", k=KC), t.keys)

    def hc_main(n):
        return [TL(HC[:, kc, HL:HL + n], [("HC", kc)]) for kc in range(KC)]

    def proj(wt, rhs_list, n):
        ps = pg()
        o = TL(ps.ap[:, 0:n], ps.keys)
        S.mm(o, [(TL(wt.ap[:, kc, :], wt.keys), rhs_list[kc]) for kc in range(KC)])
        return o

    def layer_stats(n, eps):
        mu_v = TL(MURS[:, 0, 0:n], [("MURS", 0)])
        rs_v = TL(MURS[:, 1, 0:n], [("MURS", 1)])
        p0 = pstat(0)
        p1 = pstat(1)
        S.copy("act", mu_v, TL(p0.ap[:, 0:n], p0.keys))
        S.tt("pool", rs_v, mu_v, mu_v, ALU.mult)
        S.tt("dve", rs_v, TL(p1.ap[:, 0:n], p1.keys), rs_v, ALU.subtract)
        S.act(rs_v, rs_v, AF.Sqrt, bias=eps)
        S.recip(rs_v, rs_v)
        return mu_v, rs_v

    def qk_norm_rope(ps, n, bias, gain, out, rope):
        qf = t32()
        sq = t16()
        qf_v = TL(qf.ap[:, 0:n], qf.keys)
        sq_v = TL(sq.ap[:, 0:n], sq.keys)
        S.act(qf_v, ps, AF.Identity, bias=bias)
        S.act(sq_v, ps, AF.Square, bias=bias)
        ms = pg()
        ms_v = TL(ms.ap[:, 0:n], ms.keys)
        S.mm(ms_v, [(ONESH, sq_v)])
        rs = t32()
        rs_v = TL(rs.ap[:, 0:n], rs.keys)
        S.act(rs_v, ms_v, AF.Sqrt, bias=RMS_EPS)
        S.recip(rs_v, rs_v)
        S.stt("dve", qf_v, qf_v, gain, rs_v, ALU.mult, ALU.mult)
        if not rope:
            S.copy("pool", out, qf_v)
            return
        qb = t16()
        qb_v = TL(qb.ap[:, 0:n], qb.keys)
        S.copy("pool", qb_v, qf_v)
        rt = pg()
        rt_v = TL(rt.ap[:, 0:n], rt.keys)
        S.mm(rt_v, [(ROT, qb_v)])
        S.tt("dve", rs_v, rt_v, TL(SIN[:, 0:n], [("SIN",)]), ALU.mult)
        S.tt("pool", qf_v, qf_v, TL(COS[:, 0:n], [("COS",)]), ALU.mult)
        S.tt("pool", out, qf_v, rs_v, ALU.add)

    def load_rope(c0, n):
        S.dma("sp", TL(COS[:, 0:n], [("COS",)]), cs_d[0, :, c0:c0 + n])
        S.dma("sp", TL(SIN[:, 0:n], [("SIN",)]), cs_d[1, :, c0:c0 + n])

    def modvec(l, v, kc, src):
        return TL(ADA[:, l, v, kc, src:src + 1], [("ADA",)])

    def make_hc(l, vsh, chunk, src, halo, use_halo):
        c0, n, is_ctx, first, last, ci = chunk
        lh = halo if (use_halo and not first) else 0
        rh = halo if (use_halo and not last) else 0
        if lh:
            S.copy("pool", TL(HC[:, :, HL - lh:HL], [("HC", kc) for kc in range(KC)]),
                   TL(HS[:, :, 0:lh], [("HS",)]))
        for kc in range(KC):
            keys = [("X", kc, ci)] + ([("X", kc, ci + 1)] if rh else [])
            S.ts("pool", TL(HC[:, kc, HL:HL + n + rh], [("HC", kc)]),
                 TL(X[:, kc, c0:c0 + n + rh], keys),
                 modvec(l, vsh + 1, kc, src), modvec(l, vsh, kc, src), ALU.mult, ALU.add)
        if rh:
            S.copy("pool", TL(HS[:, :, 0:halo], [("HS",)]),
                   TL(HC[:, :, HL + n - halo:HL + n], [("HC", kc) for kc in range(KC)]))
        return lh, rh

    def ln_apply(l, gname, bname, chunk):
        c0, n, is_ctx, first, last, ci = chunk
        for j in range(KC):
            xj = Xt(j, ci, c0, c0 + n)
            rb = t16()
            rq = t16()
            rb_v = TL(rb.ap[:, 0:n], rb.keys)
            rq_v = TL(rq.ap[:, 0:n], rq.keys)
            S.copy("act", rb_v, xj)
            S.act(rq_v, xj, AF.Square)
            p0, p1 = pstat(0), pstat(1)
            S.mm(TL(p0.ap[:, 0:n], p0.keys), [(ONESD, rb_v)], start=(j == 0), stop=(j == KC - 1))
            S.mm(TL(p1.ap[:, 0:n], p1.keys), [(ONESD, rq_v)], start=(j == 0), stop=(j == KC - 1))
        mu, rs = layer_stats(n, LN_EPS)
        for j in range(KC):
            xj = Xt(j, ci, c0, c0 + n)
            t = t32()
            tv = TL(t.ap[:, 0:n], t.keys)
            S.tt("dve", tv, xj, mu, ALU.subtract)
            S.tt("pool", tv, tv, rs, ALU.mult)
            S.act(xj, tv, AF.Identity, bias=pcol(bname, l, j), scale=pcol(gname, l, j))

    out_dmas = []

    def dbg(name, tl, shape, dt):
        if not debug:
            return
        dd = nc.dram_tensor("dbg_" + name, list(shape), dt, kind="ExternalOutput").ap()
        out_dmas.append(S.dma("sp", dd, tl))

    dbg("ada", TL(ADA[:, :, :, :, :], [("ADA",)]), [128, depth, 6, KC, NS], F32)
    for b in range(nb):
        allx = [("X", j, c) for j in range(KC) for c in range(len(chunks))]
        S.dma("sp", TL(X[:, :, 0:T_LAT], [("X", j, c) for j in range(KC) for c in range(4)]),
              xT[b].rearrange("(k p) t -> p k t", p=128))
        S.dma("sp", TL(X[:, :, T_LAT:T_ALL], [("X", j, 4) for j in range(KC)]),
              ctxT[b].rearrange("(k p) t -> p k t", p=128))
        for l in range(depth):
            lastl = (l == depth - 1)
            for chunk in chunks:
                c0, n, is_ctx, first, last, ci = chunk
                src = nb if is_ctx else b
                S.tag = f"l{l}c{ci}_00kv"
                make_hc(l, 0, chunk, src, 0, False)
                hm = hc_main(n)
                if not is_ctx:
                    load_rope(c0, n)
                for hk in range(2):
                    wt = load_w("win", l, Q_END // 128 + hk)
                    ps = proj(wt, hm, n)
                    qk_norm_rope(ps, n, pcol("b_in", l, Q_END // 128 + hk), pcol("k_gain", l, 0),
                                 TL(KT[:, hk, c0:c0 + n], [("KT", hk, ci)]), not is_ctx)
                wv = [load_w("win", l, K_END // 128 + i) for i in range(2)]
                for tb in range(n // 128):
                    ps = pg()
                    for i in range(2):
                        S.mm(TL(ps.ap[:, i * 128:(i + 1) * 128], ps.keys),
                             [(TL(HC[:, kc, HL + tb * 128:HL + (tb + 1) * 128], [("HC", kc)]),
                               TL(wv[i].ap[:, kc, :], wv[i].keys)) for kc in range(KC)])
                    kb = c0 // 128 + tb
                    S.copy("dve" if tb % 2 else "act", TL(V[:, kb, :], [("V", kb)]), TL(ps.ap[:, 0:256], ps.keys))
            if b == 0 and l == 0:
                dbg("kt", TL(KT[:, :, :], [("KT", hk, c) for hk in range(2) for c in range(5)]), [128, 2, T_ALL], BF16)
                dbg("v", TL(V[:, :, :], [("V", kb) for kb in range(18)]), [128, 18, 256], BF16)
            for chunk in chunks:
                c0, n, is_ctx, first, last, ci = chunk
                if is_ctx and lastl:
                    continue
                src = nb if is_ctx else b
                S.tag = f"l{l}c{ci}_01hc"
                lh, rh = make_hc(l, 0, chunk, src, HL, True)
                hm = hc_main(n)
                if not is_ctx:
                    load_rope(c0, n)
                S.tag = f"l{l}c{ci}_02glu"
                for j in range(KC):
                    wa = load_w("win", l, V_END // 128 + j)
                    wg = load_w("win", l, V_END // 128 + 8 + j)
                    pa = proj(wa, hm, n)
                    pgt = proj(wg, hm, n)
                    ba = pcol("b_in", l, V_END // 128 + j)
                    bg = pcol("b_in", l, V_END // 128 + 8 + j)
                    sg = t32()
                    sg_v = TL(sg.ap[:, 0:n], sg.keys)
                    S.act(sg_v, pgt, AF.Sigmoid, bias=bg)
                    S.stt("dve", ar(0, j, HL, HL + n), pa, ba, sg_v, ALU.add, ALU.mult)
                    sides = []
                    if lh:
                        sides.append((0, HL, 0))
                    if rh:
                        sides.append((HL + n, HL + n + HL, 1))
                    if sides:
                        ph = pg()
                        for (a0, a1, si) in sides:
                            for (wt, off) in ((wa, 0), (wg, 32)):
                                S.mm(TL(ph.ap[:, off + si * HL: off + (si + 1) * HL], ph.keys),
                                     [(TL(wt.ap[:, kc, :], wt.keys), TL(HC[:, kc, a0:a1], [("HC", kc)]))
                                      for kc in range(KC)])
                        sh = t32()
                        for (a0, a1, si) in sides:
                            S.act(TL(sh.ap[:, si * HL:(si + 1) * HL], sh.keys),
                                  TL(ph.ap[:, 32 + si * HL:32 + (si + 1) * HL], ph.keys), AF.Sigmoid, bias=bg)
                            S.stt("dve", ar(0, j, a0, a1), TL(ph.ap[:, si * HL:(si + 1) * HL], ph.keys), ba,
                                  TL(sh.ap[:, si * HL:(si + 1) * HL], sh.keys), ALU.add, ALU.mult)
                    if not lh:
                        S.memset("pool", ar(0, j, 0, HL), 0.0)
                    if not rh:
                        S.memset("pool", ar(0, j, HL + n, HL + n + HL), 0.0)
                S.tag = f"l{l}c{ci}_03dw"
                for j in range(KC):
                    ps = pg()
                    ps_v = TL(ps.ap[:, 0:n], ps.keys)
                    taps = list(range(31))
                    for s0 in range(0, 31, 8):
                        grp = taps[s0:s0 + 8]
                        pairs = []
                        for k in grp:
                            dg = dgt()
                            if KOPT & 1:
                                S.ts("pool" if k % 3 else "dve", dg, IDENT, pcol("conv_dw_w", l, k * 8 + j), 1.0, ALU.mult, ALU.mult)
                            else:
                                S.ts("pool", dg, IDENT, pcol("conv_dw_w", l, k * 8 + j), None, ALU.mult)
                            pairs.append((dg, ar(0, j, k, k + n)))
                        S.mm(ps_v, pairs, start=(s0 == 0), stop=(s0 + 8 >= 31))
                    bd = pcol("conv_dw_b", l, j)
                    S.act(ar(1, j, 0, n), ps_v, AF.Identity, bias=bd)
                    sq = t16()
                    sq_v = TL(sq.ap[:, 0:n], sq.keys)
                    S.act(sq_v, ps_v, AF.Square, bias=bd)
                    p0, p1 = pstat(0), pstat(1)
                    S.mm(TL(p0.ap[:, 0:n], p0.keys), [(ONESD, ar(1, j, 0, n))], start=(j == 0), stop=(j == KC - 1))
                    S.mm(TL(p1.ap[:, 0:n], p1.keys), [(ONESD, sq_v)], start=(j == 0), stop=(j == KC - 1))
                S.tag = f"l{l}c{ci}_04cln"
                mu, rs = layer_stats(n, LN_EPS)
                for j in range(KC):
                    t = t32()
                    tv = TL(t.ap[:, 0:n], t.keys)
                    S.tt("dve", tv, ar(1, j, 0, n), mu, ALU.subtract)
                    S.tt("pool", tv, tv, rs, ALU.mult)
                    S.act(ar(2, j, 0, n), tv, AF.Silu, bias=pcol("conv_ln_b", l, j), scale=pcol("conv_ln_g", l, j))
                convh = [ar(2, kc, 0, n) for kc in range(KC)]
                kbs = list(range(T_LAT // 128, T_ALL // 128)) if is_ctx else list(range(T_ALL // 128))
                rope = not is_ctx
                tagp = f"l{l}c{ci}_"

                def q_chain(j):
                    p = j % 2
                    S.tag = tagp + "06q"
                    qf_v = TL(F32T[2 * p].ap[:, 0:n], F32T[2 * p].keys)
                    rs_v = TL(F32T[2 * p + 1].ap[:, 0:n], F32T[2 * p + 1].keys)
                    sq_v = TL(T16[:, p, 0:n], [("T16", p)])
                    out = ar(0, j, 0, n)
                    bias = pcol("b_in", l, j)
                    wq = load_w("win", l, j)
                    pq = proj(wq, hm, n)
                    S.act(qf_v, pq, AF.Identity, bias=bias)
                    S.act(sq_v, pq, AF.Square, bias=bias)
                    yield
                    S.tag = tagp + "06q"
                    ms = pg()
                    ms_v = TL(ms.ap[:, 0:n], ms.keys)
                    S.mm(ms_v, [(ONESH, sq_v)])
                    S.act(rs_v, ms_v, AF.Sqrt, bias=RMS_EPS)
                    S.recip(rs_v, rs_v)
                    S.stt("dve", qf_v, qf_v, pcol("q_gain", l, 0), rs_v, ALU.mult, ALU.mult)
                    if not rope:
                        S.copy("pool", out, qf_v)
                        return
                    S.copy("pool", sq_v, qf_v)
                    yield
                    S.tag = tagp + "06q"
                    rt = pg()
                    rt_v = TL(rt.ap[:, 0:n], rt.keys)
                    S.mm(rt_v, [(ROT, sq_v)])
                    S.tt("dve", rs_v, rt_v, TL(SIN[:, 0:n], [("SIN",)]), ALU.mult)
                    S.tt("pool", qf_v, qf_v, TL(COS[:, 0:n], [("COS",)]), ALU.mult)
                    S.tt("pool", out, qf_v, rs_v, ALU.add)

                def poolf_chain(g):
                    w = POOL_WINDOWS[g]
                    plh = 8 if not first else 0
                    prh = 8 if not last else 0
                    gp = g % 2
                    for jj in range(2):
                        S.tag = tagp + "05poolf"
                        jc = 2 * g + jj
                        wp = load_w("win", l, CONV_END // 128 + jc)
                        bp = pcol("b_in", l, CONV_END // 128 + jc)
                        pu = proj(wp, hm, n)
                        u = TL(T32[:, 4, :], [("T32", 4)])
                        S.act(TL(u.ap[:, 8:8 + n], u.keys), pu, AF.Identity, bias=bp)
                        hs = []
                        if plh:
                            hs.append((HL - 8, HL, 0, 0))
                        if prh:
                            hs.append((HL + n, HL + n + 8, 8 + n, 1))
                        if hs:
                            yield
                            S.tag = tagp + "05poolf"
                            ph = pg()
                            for (a0, a1, d0, si) in hs:
                                S.mm(TL(ph.ap[:, si * 8:(si + 1) * 8], ph.keys),
                                     [(TL(wp.ap[:, kc, :], wp.keys), TL(HC[:, kc, a0:a1], [("HC", kc)]))
                                      for kc in range(KC)])
                            for (a0, a1, d0, si) in hs:
                                S.act(TL(u.ap[:, d0:d0 + 8], u.keys), TL(ph.ap[:, si * 8:(si + 1) * 8], ph.keys),
                                      AF.Identity, bias=bp)
                        if not plh:
                            S.memset("pool", TL(u.ap[:, 0:8], u.keys), 0.0)
                        if not prh:
                            S.memset("pool", TL(u.ap[:, 8 + n:16 + n], u.keys), 0.0)
                        ln_ = 16 + n
                        cur = u
                        step = 1
                        ab = [TL(T32[:, 5, :], [("T32", 5)]), TL(T32[:, 6, :], [("T32", 6)])]
                        si_ = 0
                        while step < w:
                            nx = ab[si_ % 2]
                            si_ += 1
                            S.tt("pool" if step % 2 else "dve", TL(nx.ap[:, 0:ln_ - step], nx.keys),
                                 TL(cur.ap[:, 0:ln_ - step], cur.keys), TL(cur.ap[:, step:ln_], cur.keys), ALU.add)
                            ln_ -= step
                            cur = nx
                            step *= 2
                        o0 = 8 - w // 2
                        mean = ab[si_ % 2]
                        S.ts("dve", TL(mean.ap[:, 0:n], mean.keys), TL(cur.ap[:, o0:o0 + n], cur.keys),
                             1.0 / w, None, ALU.mult)
                        if first:
                            S.tt("dve", TL(mean.ap[:, 0:8], mean.keys), TL(mean.ap[:, 0:8], mean.keys),
                                 TL(ET[:, g, 0, :], [("ET",)]), ALU.mult)
                        if last:
                            S.tt("dve", TL(mean.ap[:, n - 8:n], mean.keys), TL(mean.ap[:, n - 8:n], mean.keys),
                                 TL(ET[:, g, 1, :], [("ET",)]), ALU.mult)
                        S.tt("pool", TL(PL[:, gp, jj, 0:n], [("PL", gp, jj)]), TL(mean.ap[:, 0:n], mean.keys),
                             TL(u.ap[:, 8:8 + n], u.keys), ALU.subtract)
                        yield

                def merge_chain(j):
                    p = j % 2
                    g = j // 2
                    gp = g % 2
                    S.tag = tagp + "08merge"
                    po = pd(2 * p)
                    pden = pd(2 * p + 1)
                    po_v = TL(po.ap[:, 0:n], po.keys)
                    pden_v = TL(pden.ap[:, 0:n], pden.keys)
                    rd_v = TL(T32[:, 2 * p, 0:n], [("T32", 2 * p)])
                    sg_v = TL(T32[:, 2 * p + 1, 0:n], [("T32", 2 * p + 1)])
                    S.recip(rd_v, pden_v)
                    S.tt("dve", rd_v, po_v, rd_v, ALU.mult)
                    yield
                    S.tag = tagp + "08merge"
                    wg0 = load_w("win", l, POOL_END // 128 + j)
                    pz = proj(wg0, hm, n)
                    S.act(sg_v, pz, AF.Sigmoid, bias=pcol("b_in", l, POOL_END // 128 + j))
                    S.stt("dve", rd_v, rd_v, pcol("b_in", l, K_END // 128 + j // 4), sg_v, ALU.add, ALU.mult)
                    yield
                    S.tag = tagp + "08merge"
                    wg1 = load_w("win", l, POOL_END // 128 + 8 + j)
                    pz = proj(wg1, hm, n)
                    sg1 = sg_v
                    S.act(sg1, pz, AF.Sigmoid, bias=pcol("b_in", l, POOL_END // 128 + 8 + j))
                    yield
                    S.tag = tagp + "08merge"
                    wpw = load_w("pw", l, j)
                    pc = proj(wpw, convh, n)
                    S.stt("dve", sg1, pc, pcol("conv_pw_b", l, j), sg1, ALU.add, ALU.mult)
                    S.tt("pool", rd_v, rd_v, sg1, ALU.add)
                    yield
                    S.tag = tagp + "08merge"
                    wg2 = load_w("win", l, POOL_END // 128 + 16 + j)
                    pz = proj(wg2, hm, n)
                    S.act(sg_v, pz, AF.Sigmoid, bias=pcol("b_in", l, POOL_END // 128 + 16 + j))
                    yield
                    S.tag = tagp + "08merge"
                    i, pwt = wslot(plw_b[l, g].rearrange("p i o -> p (i o)"), ("plw", l, g), 512)
                    plw_t = TL(WS[:, i, 0:512].rearrange("p (i o) -> p i o", i=2), pwt.keys)
                    ppo = pg()
                    ppo_v = TL(ppo.ap[:, 0:n], ppo.keys)
                    oc = j % 2
                    S.mm(ppo_v, [(TL(plw_t.ap[:, ic, oc * 128:(oc + 1) * 128], plw_t.keys),
                                  TL(PL[:, gp, ic, 0:n], [("PL", gp, ic)])) for ic in range(2)])
                    S.stt("dve", sg_v, ppo_v, pcol("pool_scale", l, j), sg_v, ALU.mult, ALU.mult)
                    S.tt("dve", ar(3, j, 0, n), rd_v, sg_v, ALU.add)

                def drain(tasks):
                    for t in tasks:
                        for _ in t:
                            pass

                def attention(j, tasks):
                    kv = j // 4
                    p = j % 2
                    qt_v = ar(0, j, 0, n)
                    po = pd(2 * p)
                    pden = pd(2 * p + 1)
                    po_v = TL(po.ap[:, 0:n], po.keys)
                    pden_v = TL(pden.ap[:, 0:n], pden.keys)
                    sts = {}
                    live = list(tasks)

                    def issue_st(ki_):
                        kb_ = kbs[ki_]
                        stp = pg()
                        sv = TL(stp.ap[:, 0:n], stp.keys)
                        S.mm(sv, [(TL(KT[:, kv, kb_ * 128:(kb_ + 1) * 128], [("KT", kv, min(kb_ // 4, 4))]), qt_v)])
                        sts[ki_] = sv
                    S.tag = tagp + "07attn"
                    issue_st(0)
                    rr = 0
                    for ki, kb in enumerate(kbs):
                        S.tag = tagp + "07attn"
                        if ki + 1 < len(kbs):
                            issue_st(ki + 1)
                        st_v = sts.pop(ki)
                        pt = ptile()
                        pt_v = TL(pt.ap[:, 0:n], pt.keys)
                        S.act(pt_v, st_v, AF.Exp, scale=ATTN_SCALE)
                        S.mm(po_v, [(TL(V[:, kb, kv * 128:(kv + 1) * 128], [("V", kb)]), pt_v)],
                             start=(ki == 0), stop=(ki == len(kbs) - 1))
                        S.mm(pden_v, [(ONES1, pt_v)], start=(ki == 0), stop=(ki == len(kbs) - 1))
                        if live:
                            t = live[rr % len(live)]
                            try:
                                next(t)
                                rr += 1
                            except StopIteration:
                                live.remove(t)
                    drain(live)

                drain([q_chain(0)])
                for j in range(KC):
                    tasks = []
                    if j + 1 < KC:
                        tasks.append(q_chain(j + 1))
                    if j >= 1:
                        tasks.append(merge_chain(j - 1))
                    if j == 0:
                        tasks.append(poolf_chain(0))
                    elif j % 2 == 1 and j + 1 < KC:
                        tasks.append(poolf_chain((j + 1) // 2))
                    attention(j, tasks)
                drain([merge_chain(KC - 1)])
                if b == 0 and l == 0 and ci in (0, 1):
                    dbg("arena%d" % ci, TL(ARENA[:, :, :, :], [("AR", bb, jj) for bb in range(4) for jj in range(8)]), [128, 4, 8, EXT], BF16)
                S.tag = f"l{l}c{ci}_09wout"
                mb = [ar(3, kc, 0, n) for kc in range(KC)]
                for j in range(KC):
                    wo = load_w("wo", l, j)
                    py = proj(wo, mb, n)
                    t = t32()
                    tv = TL(t.ap[:, 0:n], t.keys)
                    S.ts("dve", tv, py, pcol("b_out", l, j), modvec(l, 2, j, src), ALU.add, ALU.mult)
                    xj = Xt(j, ci, c0, c0 + n)
                    S.stt("dve", xj, xj, ALPHA, tv, ALU.mult, ALU.add)
                S.tag = f"l{l}c{ci}_10ln1"
                ln_apply(l, "ln1_g", "ln1_b", chunk)
            if b == 0 and l == 0:
                dbg("xmix", TL(X[:, :, :], allx), [128, KC, T_ALL], F32)
            for chunk in chunks:
                c0, n, is_ctx, first, last, ci = chunk
                if is_ctx and lastl:
                    continue
                src = nb if is_ctx else b
                S.tag = f"l{l}c{ci}_11fhc"
                lh, rh = make_hc(l, 3, chunk, src, 1, True)
                hm = hc_main(n)
                S.tag = f"l{l}c{ci}_12fup"
                for j in range(NJF):
                    wa = load_w("wup", l, j)
                    wu = load_w("wup", l, NJF + j)
                    pa = proj(wa, hm, n)
                    pu = proj(wu, hm, n)
                    a = t32()
                    S.copy("act", TL(a.ap[:, 1:1 + n], a.keys), pa)
                    hs = []
                    if lh:
                        hs.append((HL - 1, HL, 0, 0))
                    if rh:
                        hs.append((HL + n, HL + n + 1, n + 1, 1))
                    if hs:
                        ph = pg()
                        for (a0, a1, d0, si) in hs:
                            S.mm(TL(ph.ap[:, si:si + 1], ph.keys),
                                 [(TL(wa.ap[:, kc, :], wa.keys), TL(HC[:, kc, a0:a1], [("HC", kc)]))
                                  for kc in range(KC)])
                        for (a0, a1, d0, si) in hs:
                            S.copy("act", TL(a.ap[:, d0:d0 + 1], a.keys), TL(ph.ap[:, si:si + 1], ph.keys))
                    if not lh:
                        S.memset("pool", TL(a.ap[:, 0:1], a.keys), 0.0)
                    if not rh:
                        S.memset("pool", TL(a.ap[:, n + 1:n + 2], a.keys), 0.0)
                    t = t32()
                    tv = TL(t.ap[:, 0:n], t.keys)
                    S.ts("dve", tv, TL(a.ap[:, 1:1 + n], a.keys), pcol("ffn_dw_w", l, 1 * NJF + j),
                         pcol("ffn_dw_b", l, j), ALU.mult, ALU.add)
                    S.stt("dve", tv, TL(a.ap[:, 0:n], a.keys), pcol("ffn_dw_w", l, 0 * NJF + j), tv, ALU.mult, ALU.add)
                    S.stt("dve", tv, TL(a.ap[:, 2:2 + n], a.keys), pcol("ffn_dw_w", l, 2 * NJF + j), tv, ALU.mult, ALU.add)
                    S.act(tv, tv, AF.Silu)
                    S.tt("dve", ar(j // 8, j % 8, 0, n), tv, pu, ALU.mult)
                S.tag = f"l{l}c{ci}_13fdown"
                for hf in range(2):
                    for j in range(NJF):
                        i, wt = wslot(wd_b[l, hf, j], ("wd", l, hf, j), 512)
                        for i4 in range(4):
                            acc = pd(i4)
                            S.mm(TL(acc.ap[:, 0:n], acc.keys),
                                 [(TL(WS[:, i, i4 * 128:(i4 + 1) * 128], wt.keys), ar(j // 8, j % 8, 0, n))],
                                 start=(j == 0), stop=(j == NJF - 1))
                    for i4 in range(4):
                        jo = hf * 4 + i4
                        acc = pd(i4)
                        t = t32()
                        tv = TL(t.ap[:, 0:n], t.keys)
                        S.ts("dve", tv, TL(acc.ap[:, 0:n], acc.keys), modvec(l, 5, jo, src), None, ALU.mult)
                        xj = Xt(jo, ci, c0, c0 + n)
                        S.stt("dve", xj, xj, ALPHA, tv, ALU.mult, ALU.add)
                S.tag = f"l{l}c{ci}_14ln2"
                ln_apply(l, "ln2_g", "ln2_b", chunk)
                if lastl and not is_ctx:
                    d = S.dma("pool", outT[b, :, c0:c0 + n].rearrange("(k p) t -> p k t", p=128),
                              TL(X[:, :, c0:c0 + n], [("X", j, ci) for j in range(KC)]))
                    out_dmas.append(d)

    S.emit(out_dmas)
    st.close()
    return nc


_CACHE = {}


def _get_program(nb, depth, debug=False):
    key = (nb, depth, debug)
    if key not in _CACHE:
        _CACHE[key] = build_program(nb, depth, debug)
    return _CACHE[key]


def make_in_maps(inputs, ncores, nb, depth):
    f = lambda a: np.ascontiguousarray(np.asarray(a, dtype=np.float32))
    x = f(inputs["x"])
    ctx = f(inputs["ctx"])
    c = f(inputs["c"])
    c_ctx = f(inputs["c_ctx"])
    pp = pack_params(inputs, depth)
    cb, cs, et = const_tables()
    bvb = np.ascontiguousarray(np.broadcast_to(
        f(inputs["b_in"])[None, :depth, K_END:V_END], (128, depth, 256)))
    shared = {
        "pp": pp, "bvb": bvb, "cb": cb, "cs": cs, "et": et,
        "w_ada": f(inputs["w_ada"])[:depth], "w_in": f(inputs["w_in"])[:depth],
        "conv_pw_w": f(inputs["conv_pw_w"])[:depth], "pool_w": f(inputs["pool_w"])[:depth],
        "w_out": f(inputs["w_out"])[:depth], "w_up": f(inputs["w_up"])[:depth],
        "w_down": f(inputs["w_down"])[:depth],
    }
    maps = []
    for i in range(ncores):
        sl = slice(i * nb, (i + 1) * nb)
        cc = np.concatenate([c[sl], c_ctx[None, :]], axis=0)
        csT = np.ascontiguousarray(cc.reshape(nb + 1, KC, 128).transpose(2, 1, 0))
        m = dict(shared)
        m["xT"] = np.ascontiguousarray(x[sl].transpose(0, 2, 1))
        m["ctxT"] = np.ascontiguousarray(ctx[sl].transpose(0, 2, 1))
        m["csT"] = csT
        maps.append(m)
    return maps


def run(inputs, ncores, nb, depth, debug=False):
    nc = _get_program(nb, depth, debug)
    maps = make_in_maps(inputs, ncores, nb, depth)
    res = run_bass_kernel_spmd(nc, maps, core_ids=list(range(ncores)))
    if debug:
        global DBG
        DBG = {k: np.asarray(v) for k, v in res.results[0].items() if k.startswith("dbg_")}
    outs = [np.asarray(r["outT"]).transpose(0, 2, 1) for r in res.results]
    return np.ascontiguousarray(np.concatenate(outs, axis=0).astype(np.float32))


def kernel(**inputs):
    return run(inputs, 8, BATCH // 8, DEPTH)
```

```python
import contextlib
import numpy as np
import concourse.bass as bass
import concourse.mybir as mybir
from concourse.bass_utils import run_bass_kernel_spmd

F32 = mybir.dt.float32
BF16 = mybir.dt.bfloat16
AF = mybir.ActivationFunctionType
ALU = mybir.AluOpType

D = 1024
KC = 8
T_LAT = 2048
T_CTX = 256
T_ALL = T_LAT + T_CTX
CH = 512
DEPTH = 4
BATCH = 32
D_FF = 2816
NJF = D_FF // 128
D_IN = 7680
Q_END, K_END, V_END, CONV_END, POOL_END = 1024, 1280, 1536, 3584, 4608
POOL_WINDOWS = (2, 4, 8, 16)
ALPHA = float((2 * DEPTH) ** 0.25)
LN_EPS = 1e-5
RMS_EPS = 1e-6
ATTN_SCALE = float(128 ** -0.5)
HL = 15
EXT = 544


def param_layout(depth):
    lay = {}
    col = 0
    spec = [("b_ada", 48), ("b_in", 60), ("q_gain", 1), ("k_gain", 1), ("conv_dw_w", 31 * 8),
            ("conv_dw_b", 8), ("conv_ln_g", 8), ("conv_ln_b", 8), ("conv_pw_b", 8), ("pool_scale", 8),
            ("b_out", 8), ("ln1_g", 8), ("ln1_b", 8), ("ln2_g", 8), ("ln2_b", 8),
            ("ffn_dw_w", 3 * NJF), ("ffn_dw_b", NJF)]
    for l in range(depth):
        for name, n in spec:
            lay[(name, l)] = col
            col += n
    return lay, col


def pack_params(inputs, depth):
    lay, ncol = param_layout(depth)
    pp = np.zeros((128, ncol), np.float32)

    def put(name, l, arr2d):
        a = np.asarray(arr2d, np.float32)
        m = a.shape[0]
        nch = a.shape[1] // 128
        blk = a.reshape(m, nch, 128).transpose(2, 0, 1).reshape(128, m * nch)
        c0 = lay[(name, l)]
        pp[:, c0:c0 + m * nch] = blk

    for l in range(depth):
        for name in ("b_ada", "b_in", "q_gain", "k_gain", "conv_dw_b", "conv_ln_g", "conv_ln_b",
                     "conv_pw_b", "pool_scale", "b_out", "ln1_g", "ln1_b", "ln2_g", "ln2_b", "ffn_dw_b"):
            put(name, l, np.asarray(inputs[name][l])[None, :])
        put("conv_dw_w", l, inputs["conv_dw_w"][l])
        put("ffn_dw_w", l, inputs["ffn_dw_w"][l])
    return pp


def const_tables():
    cb = np.zeros((128, 5, 128), np.float32)
    cb[:, 0, :] = np.eye(128, dtype=np.float32)
    cb[:, 1, :] = 1.0 / 1024.0
    cb[:, 2, :] = 1.0 / 128.0
    cb[:, 3, :] = 1.0
    rot = np.zeros((128, 128), np.float32)
    for a in range(2):
        for i in range(32):
            rot[a * 64 + 32 + i, a * 64 + i] = -1.0
            rot[a * 64 + i, a * 64 + 32 + i] = 1.0
    cb[:, 4, :] = rot
    t = np.arange(T_LAT)
    row = (t // 64).astype(np.float32)
    colp = (t % 64).astype(np.float32)
    inv_freq = (np.float32(10000.0) ** (-np.arange(32, dtype=np.float32) / np.float32(32))).astype(np.float32)
    cs = np.zeros((2, 128, T_LAT), np.float32)
    for p in range(128):
        pos = row if p < 64 else colp
        ang = (pos * inv_freq[p % 32]).astype(np.float32)
        cs[0, p] = np.cos(ang)
        cs[1, p] = np.sin(ang)
    et = np.ones((128, 4, 2, 8), np.float32)
    n = 4096
    for wi, w in enumerate(POOL_WINDOWS):
        for i in range(8):
            tt = i
            lo = max(tt - w // 2, 0)
            hi = min(tt - w // 2 + w, n)
            et[:, wi, 0, i] = np.float32(w) / np.float32(hi - lo)
            tt = n - 8 + i
            lo = max(tt - w // 2, 0)
            hi = min(tt - w // 2 + w, n)
            et[:, wi, 1, i] = np.float32(w) / np.float32(hi - lo)
    return cb, cs, et


class TL:
    __slots__ = ("ap", "keys")

    def __init__(self, ap, keys):
        self.ap = ap
        self.keys = tuple(keys)

    def v(self, ap):
        return TL(ap, self.keys)


def _ap(x):
    return x.ap if isinstance(x, TL) else x


class _Op:
    __slots__ = ("eng", "fn", "deps", "dma", "sig", "signo", "dsem", "dval", "tag")

    def __init__(self, eng, fn, deps, dma):
        self.eng = eng
        self.fn = fn
        self.deps = deps
        self.dma = dma
        self.sig = False
        self.signo = 0
        self.dsem = 0
        self.dval = 0


EPOCH = 30000
FAST_RECIP = False
import os
KOPT = int(os.environ.get('KOPT', '7'))
ND = 16


class Sched:
    def __init__(self, nc):
        self.nc = nc
        self.ops = []
        self.lastw = {}
        self.readers = {}
        self.dmas = []
        self.tag = None
        self.use_tags = False

    def op(self, eng, fn, ins=(), outs=(), dma=False):
        deps = set()
        for x in ins:
            if isinstance(x, TL):
                for k in x.keys:
                    w = self.lastw.get(k)
                    if w is not None:
                        deps.add(w)
        for x in outs:
            if isinstance(x, TL):
                for k in x.keys:
                    w = self.lastw.get(k)
                    if w is not None:
                        deps.add(w)
                    for r in self.readers.get(k, ()):
                        deps.add(r)
        idx = len(self.ops)
        if dma:
            k = len(self.dmas)
            if k >= ND:
                deps.add(self.dmas[k - ND])
            self.dmas.append(idx)
        o = _Op(eng, fn, sorted(deps), dma)
        o.tag = self.tag
        if dma:
            k = len(self.dmas) - 1
            o.dsem = k % ND
            o.dval = 16 * (k // ND + 1)
        for d in o.deps:
            self.ops[d].sig = True
        self.ops.append(o)
        for x in ins:
            if isinstance(x, TL):
                for k in x.keys:
                    self.readers.setdefault(k, []).append(idx)
        for x in outs:
            if isinstance(x, TL):
                for k in x.keys:
                    self.lastw[k] = idx
                    self.readers[k] = []
        return idx

    def act(self, out, in_, func, bias=0.0, scale=1.0, eng="act"):
        o, i, b, s = _ap(out), _ap(in_), _ap(bias), _ap(scale)
        self.op("act", lambda e: e.activation(out=o, in_=i, func=func, bias=b, scale=s),
                (in_, bias, scale), (out,))

    def tt(self, eng, out, in0, in1, op):
        o, a, b = _ap(out), _ap(in0), _ap(in1)
        self.op(eng, lambda e: e.tensor_tensor(out=o, in0=a, in1=b, op=op), (in0, in1), (out,))

    def ts(self, eng, out, in0, s1, s2, op0, op1=None):
        o, a, x1, x2 = _ap(out), _ap(in0), _ap(s1), _ap(s2)
        if op1 is None:
            self.op(eng, lambda e: e.tensor_scalar(out=o, in0=a, scalar1=x1, scalar2=None, op0=op0),
                    (in0, s1), (out,))
        else:
            self.op(eng, lambda e: e.tensor_scalar(out=o, in0=a, scalar1=x1, scalar2=x2, op0=op0, op1=op1),
                    (in0, s1, s2), (out,))

    def stt(self, eng, out, in0, sc, in1, op0, op1):
        o, a, s, b = _ap(out), _ap(in0), _ap(sc), _ap(in1)
        self.op(eng, lambda e: e.scalar_tensor_tensor(out=o, in0=a, scalar=s, in1=b, op0=op0, op1=op1),
                (in0, sc, in1), (out,))

    def copy(self, eng, out, in_):
        o, i = _ap(out), _ap(in_)
        if eng == "act":
            self.op(eng, lambda e: e.copy(out=o, in_=i), (in_,), (out,))
        else:
            self.op(eng, lambda e: e.tensor_copy(out=o, in_=i), (in_,), (out,))

    def memset(self, eng, out, val):
        o = _ap(out)
        self.op(eng, lambda e: e.memset(o, val), (), (out,))

    def recip(self, out, in_):
        o, i = _ap(out), _ap(in_)
        if FAST_RECIP:
            self.op("dve", lambda e: e.reciprocal_approx_fast(out=o, in_=i), (in_,), (out,))
        else:
            self.op("dve", lambda e: e.reciprocal(out=o, in_=i), (in_,), (out,))

    def mm(self, out, pairs, start=True, stop=True):
        o = _ap(out)
        ps = [(_ap(a), _ap(b)) for a, b in pairs]
        n = len(ps)

        def fn(e):
            r = None
            for i, (a, b) in enumerate(ps):
                r = e.matmul(o, a, b, start=(start and i == 0), stop=(stop and i == n - 1))
            return r
        ins = [a for a, _ in pairs] + [b for _, b in pairs]
        if not start:
            ins.append(out)
        self.op("pe", fn, ins, (out,))

    def dma(self, eng, out, in_):
        o, i = _ap(out), _ap(in_)
        return self.op(eng, lambda e: e.dma_start(out=o, in_=i), (in_,), (out,), dma=True)

    def emit(self, final_deps):
        nc = self.nc
        engs = ["pe", "act", "dve", "pool", "sp"]
        fin = _Op("sp", None, sorted(final_deps), False)
        fin.tag = None
        for d in fin.deps:
            self.ops[d].sig = True
        self.ops.append(fin)
        cnt = {e: 0 for e in engs}
        for o in self.ops:
            if o.dma or not o.sig:
                continue
            cnt[o.eng] += 1
            o.signo = cnt[o.eng]
        with contextlib.ExitStack() as st:
            esems = {}
            for e in engs:
                nep = max(1, (cnt[e] + EPOCH - 1) // EPOCH)
                esems[e] = [st.enter_context(nc.semaphore(f"s_{e}{i}")) for i in range(nep)]
            dsems = [st.enter_context(nc.semaphore(f"s_dma{i}")) for i in range(ND)]
            block = st.enter_context(nc.Block())
            per = {e: [o for o in self.ops if o.eng == e] for e in engs}
            ops = self.ops

            def run(e, eng):
                seen_e = {}
                seen_d = {}
                for o in per[e]:
                    for d in o.deps:
                        p = ops[d]
                        if p.dma:
                            if seen_d.get(p.dsem, 0) >= p.dval:
                                continue
                            seen_d[p.dsem] = p.dval
                            eng.wait_ge(dsems[p.dsem], p.dval)
                        else:
                            if p.eng == e and e == "pe":
                                continue
                            if seen_e.get(p.eng, 0) >= p.signo:
                                continue
                            seen_e[p.eng] = p.signo
                            ep = (p.signo - 1) // EPOCH
                            eng.wait_ge(esems[p.eng][ep], p.signo - ep * EPOCH)
                    if o.fn is None:
                        continue
                    if self.use_tags and o.tag:
                        with nc.named_scope(o.tag):
                            r = o.fn(eng)
                    else:
                        r = o.fn(eng)
                    if o.dma:
                        r.then_inc(dsems[o.dsem], 16)
                    elif o.sig:
                        ep = (o.signo - 1) // EPOCH
                        r.then_inc(esems[e][ep], 1)

            block.tensor(lambda eng: run("pe", eng))
            block.scalar(lambda eng: run("act", eng))
            block.vector(lambda eng: run("dve", eng))
            block.gpsimd(lambda eng: run("pool", eng))
            block.sync(lambda eng: run("sp", eng))


def build_program(nb, depth, debug=False):
    nc = bass.Bass("TRN2", target_bir_lowering=False)
    lay, npcol = param_layout(depth)
    NS = nb + 1

    def din(name, shape, dt=F32):
        return nc.dram_tensor(name, list(shape), dt, kind="ExternalInput").ap()

    xT = din("xT", [nb, D, T_LAT])
    ctxT = din("ctxT", [nb, D, T_CTX])
    csT = din("csT", [128, KC, NS])
    pp_d = din("pp", [128, npcol])
    bvb_d = din("bvb", [128, depth, 256])
    cb_d = din("cb", [128, 5, 128])
    cs_d = din("cs", [2, 128, T_LAT])
    et_d = din("et", [128, 4, 2, 8])
    w_ada = din("w_ada", [depth, D, 6 * D])
    w_in = din("w_in", [depth, D, D_IN])
    conv_pw_w = din("conv_pw_w", [depth, D, D])
    pool_w = din("pool_w", [depth, 4, 256, 256])
    w_out = din("w_out", [depth, D, D])
    w_up = din("w_up", [depth, D, 2 * D_FF])
    w_down = din("w_down", [depth, D_FF, D])
    outT = nc.dram_tensor("outT", [nb, D, T_LAT], F32, kind="ExternalOutput").ap()

    win_b = nc.dram_tensor("win_b", [depth, 60, 128, KC, 128], BF16).ap()
    pw_b = nc.dram_tensor("pw_b", [depth, 8, 128, KC, 128], BF16).ap()
    wo_b = nc.dram_tensor("wo_b", [depth, 8, 128, KC, 128], BF16).ap()
    wup_b = nc.dram_tensor("wup_b", [depth, 44, 128, KC, 128], BF16).ap()
    plw_b = nc.dram_tensor("plw_b", [depth, 4, 128, 2, 256], BF16).ap()
    wd_b = nc.dram_tensor("wd_b", [depth, 2, NJF, 128, 512], BF16).ap()

    S = Sched(nc)
    S.use_tags = (debug == "prof")
    st = contextlib.ExitStack()

    def sb(name, shape, dt):
        return st.enter_context(nc.sbuf_tensor(name, list(shape), dt))

    def pst(name):
        return st.enter_context(nc.psum_tensor(name, [128, 512], F32))

    X = sb("X", [128, KC, T_ALL], F32)
    KT = sb("KT", [128, 2, T_ALL], BF16)
    V = sb("V", [128, T_ALL // 128, 256], BF16)
    HC = sb("HC", [128, KC, EXT], BF16)
    HS = sb("HS", [128, KC, 16], BF16)
    PP = sb("PP", [128, npcol], F32)
    ADA = sb("ADA", [128, depth, 6, KC, NS], F32)
    CST = sb("CST", [128, KC, NS], F32)
    CB = sb("CB", [128, 5, 128], BF16)
    ET = sb("ET", [128, 4, 2, 8], F32)
    COS = sb("COS", [128, CH], F32)
    SIN = sb("SIN", [128, CH], F32)
    NWS = 9
    WS = sb("WS", [128, NWS, KC * 128], BF16)
    ARENA = sb("ARENA", [128, 4, 8, EXT], BF16)
    NPT = 5
    PT = sb("PT", [128, NPT, CH], BF16)
    NT32 = 7
    T32 = sb("T32", [128, NT32, EXT], F32)
    MURS = sb("MURS", [128, 2, CH], F32)
    GB = sb("GB", [128, depth, 24], F32)
    NT16 = 4
    T16 = sb("T16", [128, NT16, CH], BF16)
    NDG = 16
    DG = sb("DG", [128, NDG, 128], BF16)
    PL = sb("PL", [128, 2, 2, CH], BF16)
    NPG = 4 if (KOPT & 2) else 2
    PG = [pst(f"pg{i}") for i in range(NPG)]
    PD = [pst(f"pd{i}") for i in range(4)]

    ctr = {"pg": 0, "ws": 0, "pt": 0, "t32": 0, "t16": 0, "dg": 0}

    def nxt(name, n):
        i = ctr[name] % n
        ctr[name] += 1
        return i

    def pg():
        i = nxt("pg", NPG)
        return TL(PG[i][:, :], [("PG", i)])

    def pd(i):
        return TL(PD[i][:, :], [("PD", i)])

    def pstat(i):
        return TL(PD[2 + i][:, :], [("PD", 2 + i)])

    def t32():
        i = nxt("t32", NT32)
        return TL(T32[:, i, :], [("T32", i)])

    def t16():
        i = nxt("t16", NT16)
        return TL(T16[:, i, :], [("T16", i)])

    def ptile():
        i = nxt("pt", NPT)
        return TL(PT[:, i, :], [("PT", i)])

    def dgt():
        i = nxt("dg", NDG)
        return TL(DG[:, i, :], [("DG", i)])

    def wslot(src_ap, src_key, width=KC * 128):
        i = nxt("ws", NWS)
        t = TL(WS[:, i, 0:width], [("WS", i)])
        S.dma("sp", t, TL(src_ap, [src_key]))
        return i, t

    def pcol(name, l, c):
        c0 = lay[(name, l)] + c
        return TL(PP[:, c0:c0 + 1], [("PP",)])

    IDENT = TL(CB[:, 0, :], [("CB",)])
    ONESD = TL(CB[:, 1, :], [("CB",)])
    ONESH = TL(CB[:, 2, :], [("CB",)])
    ONES1 = TL(CB[:, 3, :], [("CB",)])
    ROT = TL(CB[:, 4, :], [("CB",)])

    def Xt(j, c, a, b):
        return TL(X[:, j, a:b], [("X", j, c)])

    F32T = []
    for i4 in range(4):
        apv = ARENA[:, 1, 2 * i4:2 * i4 + 2, :].rearrange("p a b -> p (a b)").bitcast(F32)
        F32T.append(TL(apv, [("AR", 1, 2 * i4), ("AR", 1, 2 * i4 + 1)]))
    F32M = [TL(MURS[:, 0, :], [("MURS", 0)]), TL(MURS[:, 1, :], [("MURS", 1)])]

    def ar(blk, j, a=0, b=CH):
        return TL(ARENA[:, blk, j, a:b], [("AR", blk, j)])

    S.dma("sp", TL(PP[:, :], [("PP",)]), pp_d)
    S.dma("sp", TL(CST[:, :, :], [("CST",)]), csT)
    S.dma("sp", TL(ET[:, :, :, :], [("ET",)]), et_d)
    S.dma("pool", TL(CB[:, :, :], [("CB",)]), cb_d)
    cst = TL(CST[:, :, :], [("CST",)])
    S.act(cst, cst, AF.Silu)

    for l in range(depth):
        c0 = lay[("b_in", l)] + POOL_END // 128
        S.ts("dve", TL(GB[:, l, :], [("GB",)]), TL(PP[:, c0:c0 + 24], [("PP",)]), 0.5, None, ALU.mult)

    def gbcol(l, i):
        return TL(GB[:, l, i:i + 1], [("GB",)])

    for l in range(depth):
        for g in range(48):
            i = nxt("ws", NWS)
            wt = TL(WS[:, i, :].bitcast(F32), [("WS", i)])
            half = []
            ps = pg()
            for hh in range(2):
                if hh == 1:
                    i = nxt("ws", NWS)
                    wt = TL(WS[:, i, :].bitcast(F32), [("WS", i)])
                src = w_ada[l, hh * 512:(hh + 1) * 512, g * 128:(g + 1) * 128].rearrange("(k p) c -> p k c", p=128)
                S.dma("sp", TL(wt.ap.rearrange("p (k c) -> p k c", k=4), wt.keys), src)
                half.append(wt)
            pairs = []
            for hh in range(2):
                wv = half[hh].ap.rearrange("p (k c) -> p k c", k=4)
                for k4 in range(4):
                    pairs.append((TL(wv[:, k4, :], half[hh].keys), TL(CST[:, hh * 4 + k4, :], [("CST",)])))
            S.mm(TL(ps.ap[:, 0:NS], ps.keys), pairs)
            v, kc = g // 8, g % 8
            addone = 1.0 if v in (1, 4) else 0.0
            S.ts("dve", TL(ADA[:, l, v, kc, :], [("ADA",)]), TL(ps.ap[:, 0:NS], ps.keys),
                 pcol("b_ada", l, g), addone, ALU.add, ALU.add)

    for l in range(depth):
        for g in range(60):
            S.dma("pool", TL(win_b[l, g], [("win", l, g)]),
                  w_in[l, :, g * 128:(g + 1) * 128].rearrange("(k p) c -> p k c", p=128))
        for g in range(8):
            S.dma("pool", TL(pw_b[l, g], [("pw", l, g)]),
                  conv_pw_w[l, :, g * 128:(g + 1) * 128].rearrange("(k p) c -> p k c", p=128))
            S.dma("pool", TL(wo_b[l, g], [("wo", l, g)]),
                  w_out[l, :, g * 128:(g + 1) * 128].rearrange("(k p) c -> p k c", p=128))
        for g in range(4):
            S.dma("pool", TL(plw_b[l, g], [("plw", l, g)]),
                  pool_w[l, g].rearrange("(i p) o -> p i o", p=128))
        for g in range(44):
            S.dma("pool", TL(wup_b[l, g], [("wup", l, g)]),
                  w_up[l, :, g * 128:(g + 1) * 128].rearrange("(k p) c -> p k c", p=128))
        for hf in range(2):
            for j in range(NJF):
                S.dma("pool", TL(wd_b[l, hf, j], [("wd", l, hf, j)]),
                      w_down[l, j * 128:(j + 1) * 128, hf * 512:(hf + 1) * 512])

    chunks = [(i * CH, CH, False, i == 0, i == T_LAT // CH - 1, i) for i in range(T_LAT // CH)]
    chunks.append((T_LAT, T_CTX, True, True, True, T_LAT // CH))

    def load_w(kind, l, g):
        src = {"win": win_b, "pw": pw_b, "wo": wo_b, "wup": wup_b}[kind]
        i, t = wslot(src[l, g].rearrange("p k c -> p (k c)"), (kind, l, g))
        return TL(WS[:, i, :].rearrange("p (k c) -> p k c", k=KC), t.keys)

    def hc_main(n):
        return [TL(HC[:, kc, HL:HL + n], [("HC", kc)]) for kc in range(KC)]

    def proj(wt, rhs_list, n):
        ps = pg()
        o = TL(ps.ap[:, 0:n], ps.keys)
        S.mm(o, [(TL(wt.ap[:, kc, :], wt.keys), rhs_list[kc]) for kc in range(KC)])
        return o

    def layer_stats(n, eps):
        mu_v = TL(MURS[:, 0, 0:n], [("MURS", 0)])
        rs_v = TL(MURS[:, 1, 0:n], [("MURS", 1)])
        p0 = pstat(0)
        p1 = pstat(1)
        S.copy("act", mu_v, TL(p0.ap[:, 0:n], p0.keys))
        S.tt("pool", rs_v, mu_v, mu_v, ALU.mult)
        S.tt("dve", rs_v, TL(p1.ap[:, 0:n], p1.keys), rs_v, ALU.subtract)
        S.act(rs_v, rs_v, AF.Sqrt, bias=eps)
        S.recip(rs_v, rs_v)
        return mu_v, rs_v

    def qk_norm_rope(ps, n, bias, gain, out, rope):
        qf = t32()
        sq = t16()
        qf_v = TL(qf.ap[:, 0:n], qf.keys)
        sq_v = TL(sq.ap[:, 0:n], sq.keys)
        S.act(qf_v, ps, AF.Identity, bias=bias)
        S.act(sq_v, ps, AF.Square, bias=bias)
        ms = pg()
        ms_v = TL(ms.ap[:, 0:n], ms.keys)
        S.mm(ms_v, [(ONESH, sq_v)])
        rs = t32()
        rs_v = TL(rs.ap[:, 0:n], rs.keys)
        S.act(rs_v, ms_v, AF.Sqrt, bias=RMS_EPS)
        S.recip(rs_v, rs_v)
        S.stt("dve", qf_v, qf_v, gain, rs_v, ALU.mult, ALU.mult)
        if not rope:
            S.copy("pool", out, qf_v)
            return
        qb = t16()
        qb_v = TL(qb.ap[:, 0:n], qb.keys)
        S.copy("pool", qb_v, qf_v)
        rt = pg()
        rt_v = TL(rt.ap[:, 0:n], rt.keys)
        S.mm(rt_v, [(ROT, qb_v)])
        S.tt("dve", rs_v, rt_v, TL(SIN[:, 0:n], [("SIN",)]), ALU.mult)
        S.tt("pool", qf_v, qf_v, TL(COS[:, 0:n], [("COS",)]), ALU.mult)
        S.tt("pool", out, qf_v, rs_v, ALU.add)

    def load_rope(c0, n):
        S.dma("sp", TL(COS[:, 0:n], [("COS",)]), cs_d[0, :, c0:c0 + n])
        S.dma("sp", TL(SIN[:, 0:n], [("SIN",)]), cs_d[1, :, c0:c0 + n])

    def modvec(l, v, kc, src):
        return TL(ADA[:, l, v, kc, src:src + 1], [("ADA",)])

    def make_hc(l, vsh, chunk, src, halo, use_halo):
        c0, n, is_ctx, first, last, ci = chunk
        lh = halo if (use_halo and not first) else 0
        rh = halo if (use_halo and not last) else 0
        if lh:
            S.copy("pool", TL(HC[:, :, HL - lh:HL], [("HC", kc) for kc in range(KC)]),
                   TL(HS[:, :, 0:lh], [("HS",)]))
        for kc in range(KC):
            keys = [("X", kc, ci)] + ([("X", kc, ci + 1)] if rh else [])
            S.ts("pool", TL(HC[:, kc, HL:HL + n + rh], [("HC", kc)]),
                 TL(X[:, kc, c0:c0 + n + rh], keys),
                 modvec(l, vsh + 1, kc, src), modvec(l, vsh, kc, src), ALU.mult, ALU.add)
        if rh:
            S.copy("pool", TL(HS[:, :, 0:halo], [("HS",)]),
                   TL(HC[:, :, HL + n - halo:HL + n], [("HC", kc) for kc in range(KC)]))
        return lh, rh

    def ln_apply(l, gname, bname, chunk):
        c0, n, is_ctx, first, last, ci = chunk
        for j in range(KC):
            xj = Xt(j, ci, c0, c0 + n)
            rb = t16()
            rq = t16()
            rb_v = TL(rb.ap[:, 0:n], rb.keys)
            rq_v = TL(rq.ap[:, 0:n], rq.keys)
            S.copy("act", rb_v, xj)
            S.act(rq_v, xj, AF.Square)
            p0, p1 = pstat(0), pstat(1)
            S.mm(TL(p0.ap[:, 0:n], p0.keys), [(ONESD, rb_v)], start=(j == 0), stop=(j == KC - 1))
            S.mm(TL(p1.ap[:, 0:n], p1.keys), [(ONESD, rq_v)], start=(j == 0), stop=(j == KC - 1))
        mu, rs = layer_stats(n, LN_EPS)
        for j in range(KC):
            xj = Xt(j, ci, c0, c0 + n)
            t = t32()
            tv = TL(t.ap[:, 0:n], t.keys)
            S.tt("dve", tv, xj, mu, ALU.subtract)
            S.tt("pool", tv, tv, rs, ALU.mult)
            S.act(xj, tv, AF.Identity, bias=pcol(bname, l, j), scale=pcol(gname, l, j))

    out_dmas = []

    def dbg(name, tl, shape, dt):
        if not debug:
            return
        dd = nc.dram_tensor("dbg_" + name, list(shape), dt, kind="ExternalOutput").ap()
        out_dmas.append(S.dma("sp", dd, tl))

    dbg("ada", TL(ADA[:, :, :, :, :], [("ADA",)]), [128, depth, 6, KC, NS], F32)
    for b in range(nb):
        allx = [("X", j, c) for j in range(KC) for c in range(len(chunks))]
        S.dma("sp", TL(X[:, :, 0:T_LAT], [("X", j, c) for j in range(KC) for c in range(4)]),
              xT[b].rearrange("(k p) t -> p k t", p=128))
        S.dma("sp", TL(X[:, :, T_LAT:T_ALL], [("X", j, 4) for j in range(KC)]),
              ctxT[b].rearrange("(k p) t -> p k t", p=128))
        for l in range(depth):
            lastl = (l == depth - 1)
            for chunk in chunks:
                c0, n, is_ctx, first, last, ci = chunk
                src = nb if is_ctx else b
                S.tag = f"l{l}c{ci}_00kv"
                make_hc(l, 0, chunk, src, 0, False)
                hm = hc_main(n)
                if not is_ctx:
                    load_rope(c0, n)
                for hk in range(2):
                    wt = load_w("win", l, Q_END // 128 + hk)
                    ps = proj(wt, hm, n)
                    qk_norm_rope(ps, n, pcol("b_in", l, Q_END // 128 + hk), pcol("k_gain", l, 0),
                                 TL(KT[:, hk, c0:c0 + n], [("KT", hk, ci)]), not is_ctx)
                wv = [load_w("win", l, K_END // 128 + i) for i in range(2)]
                for tb in range(n // 128):
                    ps = pg()
                    for i in range(2):
                        S.mm(TL(ps.ap[:, i * 128:(i + 1) * 128], ps.keys),
                             [(TL(HC[:, kc, HL + tb * 128:HL + (tb + 1) * 128], [("HC", kc)]),
                               TL(wv[i].ap[:, kc, :], wv[i].keys)) for kc in range(KC)])
                    kb = c0 // 128 + tb
                    S.copy("dve" if tb % 2 else "act", TL(V[:, kb, :], [("V", kb)]), TL(ps.ap[:, 0:256], ps.keys))
            if b == 0 and l == 0:
                dbg("kt", TL(KT[:, :, :], [("KT", hk, c) for hk in range(2) for c in range(5)]), [128, 2, T_ALL], BF16)
                dbg("v", TL(V[:, :, :], [("V", kb) for kb in range(18)]), [128, 18, 256], BF16)
            for chunk in chunks:
                c0, n, is_ctx, first, last, ci = chunk
                if is_ctx and lastl:
                    continue
                src = nb if is_ctx else b
                S.tag = f"l{l}c{ci}_01hc"
                lh, rh = make_hc(l, 0, chunk, src, HL, True)
                hm = hc_main(n)
                if not is_ctx:
                    load_rope(c0, n)
                S.tag = f"l{l}c{ci}_02glu"
                for j in range(KC):
                    wa = load_w("win", l, V_END // 128 + j)
                    wg = load_w("win", l, V_END // 128 + 8 + j)
                    pa = proj(wa, hm, n)
                    pgt = proj(wg, hm, n)
                    ba = pcol("b_in", l, V_END // 128 + j)
                    bg = pcol("b_in", l, V_END // 128 + 8 + j)
                    sg = t32()
                    sg_v = TL(sg.ap[:, 0:n], sg.keys)
                    S.act(sg_v, pgt, AF.Sigmoid, bias=bg)
                    S.stt("dve", ar(0, j, HL, HL + n), pa, ba, sg_v, ALU.add, ALU.mult)
                    sides = []
                    if lh:
                        sides.append((0, HL, 0))
                    if rh:
                        sides.append((HL + n, HL + n + HL, 1))
                    if sides:
                        ph = pg()
                        for (a0, a1, si) in sides:
                            for (wt, off) in ((wa, 0), (wg, 32)):
                                S.mm(TL(ph.ap[:, off + si * HL: off + (si + 1) * HL], ph.keys),
                                     [(TL(wt.ap[:, kc, :], wt.keys), TL(HC[:, kc, a0:a1], [("HC", kc)]))
                                      for kc in range(KC)])
                        sh = t32()
                        for (a0, a1, si) in sides:
                            S.act(TL(sh.ap[:, si * HL:(si + 1) * HL], sh.keys),
                                  TL(ph.ap[:, 32 + si * HL:32 + (si + 1) * HL], ph.keys), AF.Sigmoid, bias=bg)
                            S.stt("dve", ar(0, j, a0, a1), TL(ph.ap[:, si * HL:(si + 1) * HL], ph.keys), ba,
                                  TL(sh.ap[:, si * HL:(si + 1) * HL], sh.keys), ALU.add, ALU.mult)
                    if not lh:
                        S.memset("pool", ar(0, j, 0, HL), 0.0)
                    if not rh:
                        S.memset("pool", ar(0, j, HL + n, HL + n + HL), 0.0)
                S.tag = f"l{l}c{ci}_03dw"
                for j in range(KC):
                    ps = pg()
                    ps_v = TL(ps.ap[:, 0:n], ps.keys)
                    taps = list(range(31))
                    for s0 in range(0, 31, 8):
                        grp = taps[s0:s0 + 8]
                        pairs = []
                        for k in grp:
                            dg = dgt()
                            if KOPT & 1:
                                S.ts("pool" if k % 3 else "dve", dg, IDENT, pcol("conv_dw_w", l, k * 8 + j), 1.0, ALU.mult, ALU.mult)
                            else:
                                S.ts("pool", dg, IDENT, pcol("conv_dw_w", l, k * 8 + j), None, ALU.mult)
                            pairs.append((dg, ar(0, j, k, k + n)))
                        S.mm(ps_v, pairs, start=(s0 == 0), stop=(s0 + 8 >= 31))
                    bd = pcol("conv_dw_b", l, j)
                    S.act(ar(1, j, 0, n), ps_v, AF.Identity, bias=bd)
                    sq = t16()
                    sq_v = TL(sq.ap[:, 0:n], sq.keys)
                    S.act(sq_v, ps_v, AF.Square, bias=bd)
                    p0, p1 = pstat(0), pstat(1)
                    S.mm(TL(p0.ap[:, 0:n], p0.keys), [(ONESD, ar(1, j, 0, n))], start=(j == 0), stop=(j == KC - 1))
                    S.mm(TL(p1.ap[:, 0:n], p1.keys), [(ONESD, sq_v)], start=(j == 0), stop=(j == KC - 1))
                S.tag = f"l{l}c{ci}_04cln"
                mu, rs = layer_stats(n, LN_EPS)
                for j in range(KC):
                    t = t32()
                    tv = TL(t.ap[:, 0:n], t.keys)
                    S.tt("dve", tv, ar(1, j, 0, n), mu, ALU.subtract)
                    S.tt("pool", tv, tv, rs, ALU.mult)
                    S.act(ar(2, j, 0, n), tv, AF.Silu, bias=pcol("conv_ln_b", l, j), scale=pcol("conv_ln_g", l, j))
                convh = [ar(2, kc, 0, n) for kc in range(KC)]
                kbs = list(range(T_LAT // 128, T_ALL // 128)) if is_ctx else list(range(T_ALL // 128))
                rope = not is_ctx
                tagp = f"l{l}c{ci}_"

                def q_chain(j):
                    p = j % 2
                    S.tag = tagp + "06q"
                    qf_v = TL(F32T[2 * p].ap[:, 0:n], F32T[2 * p].keys)
                    rs_v = TL(F32T[2 * p + 1].ap[:, 0:n], F32T[2 * p + 1].keys)
                    sq_v = TL(T16[:, p, 0:n], [("T16", p)])
                    out = ar(0, j, 0, n)
                    bias = pcol("b_in", l, j)
                    wq = load_w("win", l, j)
                    pq = proj(wq, hm, n)
                    S.act(qf_v, pq, AF.Identity, bias=bias)
                    S.act(sq_v, pq, AF.Square, bias=bias)
                    yield
                    S.tag = tagp + "06q"
                    ms = pg()
                    ms_v = TL(ms.ap[:, 0:n], ms.keys)
                    S.mm(ms_v, [(ONESH, sq_v)])
                    S.act(rs_v, ms_v, AF.Sqrt, bias=RMS_EPS)
                    S.recip(rs_v, rs_v)
                    S.stt("dve", qf_v, qf_v, pcol("q_gain", l, 0), rs_v, ALU.mult, ALU.mult)
                    if not rope:
                        S.copy("pool", out, qf_v)
                        return
                    S.copy("pool", sq_v, qf_v)
                    yield
                    S.tag = tagp + "06q"
                    rt = pg()
                    rt_v = TL(rt.ap[:, 0:n], rt.keys)
                    S.mm(rt_v, [(ROT, sq_v)])
                    S.tt("dve", rs_v, rt_v, TL(SIN[:, 0:n], [("SIN",)]), ALU.mult)
                    S.tt("pool", qf_v, qf_v, TL(COS[:, 0:n], [("COS",)]), ALU.mult)
                    S.tt("pool", out, qf_v, rs_v, ALU.add)

                def poolf_chain(g):
                    w = POOL_WINDOWS[g]
                    plh = 8 if not first else 0
                    prh = 8 if not last else 0
                    gp = g % 2
                    for jj in range(2):
                        S.tag = tagp + "05poolf"
                        jc = 2 * g + jj
                        wp = load_w("win", l, CONV_END // 128 + jc)
                        bp = pcol("b_in", l, CONV_END // 128 + jc)
                        pu = proj(wp, hm, n)
                        u = TL(T32[:, 4, :], [("T32", 4)])
                        S.act(TL(u.ap[:, 8:8 + n], u.keys), pu, AF.Identity, bias=bp)
                        hs = []
                        if plh:
                            hs.append((HL - 8, HL, 0, 0))
                        if prh:
                            hs.append((HL + n, HL + n + 8, 8 + n, 1))
                        if hs:
                            yield
                            S.tag = tagp + "05poolf"
                            ph = pg()
                            for (a0, a1, d0, si) in hs:
                                S.mm(TL(ph.ap[:, si * 8:(si + 1) * 8], ph.keys),
                                     [(TL(wp.ap[:, kc, :], wp.keys), TL(HC[:, kc, a0:a1], [("HC", kc)]))
                                      for kc in range(KC)])
                            for (a0, a1, d0, si) in hs:
                                S.act(TL(u.ap[:, d0:d0 + 8], u.keys), TL(ph.ap[:, si * 8:(si + 1) * 8], ph.keys),
                                      AF.Identity, bias=bp)
                        if not plh:
                            S.memset("pool", TL(u.ap[:, 0:8], u.keys), 0.0)
                        if not prh:
                            S.memset("pool", TL(u.ap[:, 8 + n:16 + n], u.keys), 0.0)
                        ln_ = 16 + n
                        cur = u
                        step = 1
                        ab = [TL(T32[:, 5, :], [("T32", 5)]), TL(T32[:, 6, :], [("T32", 6)])]
                        si_ = 0
                        while step < w:
                            nx = ab[si_ % 2]
                            si_ += 1
                            S.tt("pool" if step % 2 else "dve", TL(nx.ap[:, 0:ln_ - step], nx.keys),
                                 TL(cur.ap[:, 0:ln_ - step], cur.keys), TL(cur.ap[:, step:ln_], cur.keys), ALU.add)
                            ln_ -= step
                            cur = nx
                            step *= 2
                        o0 = 8 - w // 2
                        mean = ab[si_ % 2]
                        S.ts("dve", TL(mean.ap[:, 0:n], mean.keys), TL(cur.ap[:, o0:o0 + n], cur.keys),
                             1.0 / w, None, ALU.mult)
                        if first:
                            S.tt("dve", TL(mean.ap[:, 0:8], mean.keys), TL(mean.ap[:, 0:8], mean.keys),
                                 TL(ET[:, g, 0, :], [("ET",)]), ALU.mult)
                        if last:
                            S.tt("dve", TL(mean.ap[:, n - 8:n], mean.keys), TL(mean.ap[:, n - 8:n], mean.keys),
                                 TL(ET[:, g, 1, :], [("ET",)]), ALU.mult)
                        S.tt("pool", TL(PL[:, gp, jj, 0:n], [("PL", gp, jj)]), TL(mean.ap[:, 0:n], mean.keys),
                             TL(u.ap[:, 8:8 + n], u.keys), ALU.subtract)
                        yield

                def merge_chain(j):
                    p = j % 2
                    g = j // 2
                    gp = g % 2
                    S.tag = tagp + "08merge"
                    po = pd(2 * p)
                    pden = pd(2 * p + 1)
                    po_v = TL(po.ap[:, 0:n], po.keys)
                    pden_v = TL(pden.ap[:, 0:n], pden.keys)
                    rd_v = TL(T32[:, 2 * p, 0:n], [("T32", 2 * p)])
                    sg_v = TL(T32[:, 2 * p + 1, 0:n], [("T32", 2 * p + 1)])
                    S.recip(rd_v, pden_v)
                    S.tt("dve", rd_v, po_v, rd_v, ALU.mult)
                    yield
                    S.tag = tagp + "08merge"
                    wg0 = load_w("win", l, POOL_END // 128 + j)
                    pz = proj(wg0, hm, n)
                    S.act(sg_v, pz, AF.Tanh, bias=gbcol(l, j), scale=0.5)
                    S.ts("dve", sg_v, sg_v, 0.5, 0.5, ALU.mult, ALU.add)
                    S.stt("dve", rd_v, rd_v, pcol("b_in", l, K_END // 128 + j // 4), sg_v, ALU.add, ALU.mult)
                    yield
                    S.tag = tagp + "08merge"
                    wg1 = load_w("win", l, POOL_END // 128 + 8 + j)
                    pz = proj(wg1, hm, n)
                    sg1 = sg_v
                    S.act(sg1, pz, AF.Tanh, bias=gbcol(l, 8 + j), scale=0.5)
                    S.ts("pool", sg1, sg1, 0.5, 0.5, ALU.mult, ALU.add)
                    yield
                    S.tag = tagp + "08merge"
                    wpw = load_w("pw", l, j)
                    pc = proj(wpw, convh, n)
                    S.stt("dve", sg1, pc, pcol("conv_pw_b", l, j), sg1, ALU.add, ALU.mult)
                    S.tt("pool", rd_v, rd_v, sg1, ALU.add)
                    yield
                    S.tag = tagp + "08merge"
                    wg2 = load_w("win", l, POOL_END // 128 + 16 + j)
                    pz = proj(wg2, hm, n)
                    S.act(sg_v, pz, AF.Tanh, bias=gbcol(l, 16 + j), scale=0.5)
                    S.ts("pool", sg_v, sg_v, 0.5, 0.5, ALU.mult, ALU.add)
                    yield
                    S.tag = tagp + "08merge"
                    i, pwt = wslot(plw_b[l, g].rearrange("p i o -> p (i o)"), ("plw", l, g), 512)
                    plw_t = TL(WS[:, i, 0:512].rearrange("p (i o) -> p i o", i=2), pwt.keys)
                    ppo = pg()
                    ppo_v = TL(ppo.ap[:, 0:n], ppo.keys)
                    oc = j % 2
                    S.mm(ppo_v, [(TL(plw_t.ap[:, ic, oc * 128:(oc + 1) * 128], plw_t.keys),
                                  TL(PL[:, gp, ic, 0:n], [("PL", gp, ic)])) for ic in range(2)])
                    S.stt("dve", sg_v, ppo_v, pcol("pool_scale", l, j), sg_v, ALU.mult, ALU.mult)
                    S.tt("dve", ar(3, j, 0, n), rd_v, sg_v, ALU.add)

                def drain(tasks):
                    for t in tasks:
                        for _ in t:
                            pass

                def attention(j, tasks):
                    kv = j // 4
                    p = j % 2
                    qt_v = ar(0, j, 0, n)
                    po = pd(2 * p)
                    pden = pd(2 * p + 1)
                    po_v = TL(po.ap[:, 0:n], po.keys)
                    pden_v = TL(pden.ap[:, 0:n], pden.keys)
                    sts = {}
                    live = list(tasks)

                    def issue_st(ki_):
                        kb_ = kbs[ki_]
                        stp = pg()
                        sv = TL(stp.ap[:, 0:n], stp.keys)
                        S.mm(sv, [(TL(KT[:, kv, kb_ * 128:(kb_ + 1) * 128], [("KT", kv, min(kb_ // 4, 4))]), qt_v)])
                        sts[ki_] = sv
                    S.tag = tagp + "07attn"
                    issue_st(0)
                    rr = 0
                    for ki, kb in enumerate(kbs):
                        S.tag = tagp + "07attn"
                        if ki + 1 < len(kbs):
                            issue_st(ki + 1)
                        st_v = sts.pop(ki)
                        pt = ptile()
                        pt_v = TL(pt.ap[:, 0:n], pt.keys)
                        S.act(pt_v, st_v, AF.Exp, scale=ATTN_SCALE)
                        S.mm(po_v, [(TL(V[:, kb, kv * 128:(kv + 1) * 128], [("V", kb)]), pt_v)],
                             start=(ki == 0), stop=(ki == len(kbs) - 1))
                        S.mm(pden_v, [(ONES1, pt_v)], start=(ki == 0), stop=(ki == len(kbs) - 1))
                        if live:
                            t = live[rr % len(live)]
                            try:
                                next(t)
                                rr += 1
                            except StopIteration:
                                live.remove(t)
                    drain(live)

                drain([q_chain(0)])
                for j in range(KC):
                    tasks = []
                    if j + 1 < KC:
                        tasks.append(q_chain(j + 1))
                    if j >= 1:
                        tasks.append(merge_chain(j - 1))
                    if j == 0:
                        tasks.append(poolf_chain(0))
                    elif j % 2 == 1 and j + 1 < KC:
                        tasks.append(poolf_chain((j + 1) // 2))
                    attention(j, tasks)
                drain([merge_chain(KC - 1)])
                if b == 0 and l == 0 and ci in (0, 1):
                    dbg("arena%d" % ci, TL(ARENA[:, :, :, :], [("AR", bb, jj) for bb in range(4) for jj in range(8)]), [128, 4, 8, EXT], BF16)
                S.tag = f"l{l}c{ci}_09wout"
                mb = [ar(3, kc, 0, n) for kc in range(KC)]
                for j in range(KC):
                    wo = load_w("wo", l, j)
                    py = proj(wo, mb, n)
                    t = t32()
                    tv = TL(t.ap[:, 0:n], t.keys)
                    S.ts("dve", tv, py, pcol("b_out", l, j), modvec(l, 2, j, src), ALU.add, ALU.mult)
                    xj = Xt(j, ci, c0, c0 + n)
                    S.stt("dve", xj, xj, ALPHA, tv, ALU.mult, ALU.add)
                S.tag = f"l{l}c{ci}_10ln1"
                ln_apply(l, "ln1_g", "ln1_b", chunk)
            if b == 0 and l == 0:
                dbg("xmix", TL(X[:, :, :], allx), [128, KC, T_ALL], F32)
            for chunk in chunks:
                c0, n, is_ctx, first, last, ci = chunk
                if is_ctx and lastl:
                    continue
                src = nb if is_ctx else b
                S.tag = f"l{l}c{ci}_11fhc"
                lh, rh = make_hc(l, 3, chunk, src, 1, True)
                hm = hc_main(n)
                S.tag = f"l{l}c{ci}_12fup"
                for j in range(NJF):
                    wa = load_w("wup", l, j)
                    wu = load_w("wup", l, NJF + j)
                    pa = proj(wa, hm, n)
                    pu = proj(wu, hm, n)
                    a = t32()
                    S.copy("act", TL(a.ap[:, 1:1 + n], a.keys), pa)
                    hs = []
                    if lh:
                        hs.append((HL - 1, HL, 0, 0))
                    if rh:
                        hs.append((HL + n, HL + n + 1, n + 1, 1))
                    if hs:
                        ph = pg()
                        for (a0, a1, d0, si) in hs:
                            S.mm(TL(ph.ap[:, si:si + 1], ph.keys),
                                 [(TL(wa.ap[:, kc, :], wa.keys), TL(HC[:, kc, a0:a1], [("HC", kc)]))
                                  for kc in range(KC)])
                        for (a0, a1, d0, si) in hs:
                            S.copy("act", TL(a.ap[:, d0:d0 + 1], a.keys), TL(ph.ap[:, si:si + 1], ph.keys))
                    if not lh:
                        S.memset("pool", TL(a.ap[:, 0:1], a.keys), 0.0)
                    if not rh:
                        S.memset("pool", TL(a.ap[:, n + 1:n + 2], a.keys), 0.0)
                    t = t32()
                    tv = TL(t.ap[:, 0:n], t.keys)
                    S.ts("dve", tv, TL(a.ap[:, 1:1 + n], a.keys), pcol("ffn_dw_w", l, 1 * NJF + j),
                         pcol("ffn_dw_b", l, j), ALU.mult, ALU.add)
                    S.stt("dve", tv, TL(a.ap[:, 0:n], a.keys), pcol("ffn_dw_w", l, 0 * NJF + j), tv, ALU.mult, ALU.add)
                    S.stt("dve", tv, TL(a.ap[:, 2:2 + n], a.keys), pcol("ffn_dw_w", l, 2 * NJF + j), tv, ALU.mult, ALU.add)
                    S.act(tv, tv, AF.Silu)
                    S.tt("dve", ar(j // 8, j % 8, 0, n), tv, pu, ALU.mult)
                S.tag = f"l{l}c{ci}_13fdown"
                for hf in range(2):
                    for j in range(NJF):
                        i, wt = wslot(wd_b[l, hf, j], ("wd", l, hf, j), 512)
                        for i4 in range(4):
                            acc = pd(i4)
                            S.mm(TL(acc.ap[:, 0:n], acc.keys),
                                 [(TL(WS[:, i, i4 * 128:(i4 + 1) * 128], wt.keys), ar(j // 8, j % 8, 0, n))],
                                 start=(j == 0), stop=(j == NJF - 1))
                    for i4 in range(4):
                        jo = hf * 4 + i4
                        acc = pd(i4)
                        t = t32()
                        tv = TL(t.ap[:, 0:n], t.keys)
                        S.ts("dve", tv, TL(acc.ap[:, 0:n], acc.keys), modvec(l, 5, jo, src), None, ALU.mult)
                        xj = Xt(jo, ci, c0, c0 + n)
                        S.stt("dve", xj, xj, ALPHA, tv, ALU.mult, ALU.add)
                S.tag = f"l{l}c{ci}_14ln2"
                ln_apply(l, "ln2_g", "ln2_b", chunk)
                if lastl and not is_ctx:
                    d = S.dma("pool", outT[b, :, c0:c0 + n].rearrange("(k p) t -> p k t", p=128),
                              TL(X[:, :, c0:c0 + n], [("X", j, ci) for j in range(KC)]))
                    out_dmas.append(d)

    S.emit(out_dmas)
    st.close()
    return nc


_CACHE = {}


def _get_program(nb, depth, debug=False):
    key = (nb, depth, debug)
    if key not in _CACHE:
        _CACHE[key] = build_program(nb, depth, debug)
    return _CACHE[key]


def make_in_maps(inputs, ncores, nb, depth):
    f = lambda a: np.ascontiguousarray(np.asarray(a, dtype=np.float32))
    x = f(inputs["x"])
    ctx = f(inputs["ctx"])
    c = f(inputs["c"])
    c_ctx = f(inputs["c_ctx"])
    pp = pack_params(inputs, depth)
    cb, cs, et = const_tables()
    bvb = np.ascontiguousarray(np.broadcast_to(
        f(inputs["b_in"])[None, :depth, K_END:V_END], (128, depth, 256)))
    shared = {
        "pp": pp, "bvb": bvb, "cb": cb, "cs": cs, "et": et,
        "w_ada": f(inputs["w_ada"])[:depth], "w_in": f(inputs["w_in"])[:depth],
        "conv_pw_w": f(inputs["conv_pw_w"])[:depth], "pool_w": f(inputs["pool_w"])[:depth],
        "w_out": f(inputs["w_out"])[:depth], "w_up": f(inputs["w_up"])[:depth],
        "w_down": f(inputs["w_down"])[:depth],
    }
    maps = []
    for i in range(ncores):
        sl = slice(i * nb, (i + 1) * nb)
        cc = np.concatenate([c[sl], c_ctx[None, :]], axis=0)
        csT = np.ascontiguousarray(cc.reshape(nb + 1, KC, 128).transpose(2, 1, 0))
        m = dict(shared)
        m["xT"] = np.ascontiguousarray(x[sl].transpose(0, 2, 1))
        m["ctxT"] = np.ascontiguousarray(ctx[sl].transpose(0, 2, 1))
        m["csT"] = csT
        maps.append(m)
    return maps


def run(inputs, ncores, nb, depth, debug=False):
    nc = _get_program(nb, depth, debug)
    maps = make_in_maps(inputs, ncores, nb, depth)
    res = run_bass_kernel_spmd(nc, maps, core_ids=list(range(ncores)))
    if debug:
        global DBG
        DBG = {k: np.asarray(v) for k, v in res.results[0].items() if k.startswith("dbg_")}
    outs = [np.asarray(r["outT"]).transpose(0, 2, 1) for r in res.results]
    return np.ascontiguousarray(np.concatenate(outs, axis=0).astype(np.float32))


def kernel(**inputs):
    return run(inputs, 8, BATCH // 8, DEPTH)
```

```python
import contextlib
import numpy as np
import concourse.bass as bass
import concourse.mybir as mybir
from concourse.bass_utils import run_bass_kernel_spmd

F32 = mybir.dt.float32
BF16 = mybir.dt.bfloat16
AF = mybir.ActivationFunctionType
ALU = mybir.AluOpType

D = 1024
KC = 8
T_LAT = 2048
T_CTX = 256
T_ALL = T_LAT + T_CTX
CH = 512
DEPTH = 4
BATCH = 32
D_FF = 2816
NJF = D_FF // 128
D_IN = 7680
Q_END, K_END, V_END, CONV_END, POOL_END = 1024, 1280, 1536, 3584, 4608
POOL_WINDOWS = (2, 4, 8, 16)
ALPHA = float((2 * DEPTH) ** 0.25)
LN_EPS = 1e-5
RMS_EPS = 1e-6
ATTN_SCALE = float(128 ** -0.5)
HL = 15
EXT = 544


def param_layout(depth):
    lay = {}
    col = 0
    spec = [("b_ada", 48), ("b_in", 60), ("q_gain", 1), ("k_gain", 1), ("conv_dw_w", 31 * 8),
            ("conv_dw_b", 8), ("conv_ln_g", 8), ("conv_ln_b", 8), ("conv_pw_b", 8), ("pool_scale", 8),
            ("b_out", 8), ("ln1_g", 8), ("ln1_b", 8), ("ln2_g", 8), ("ln2_b", 8),
            ("ffn_dw_w", 3 * NJF), ("ffn_dw_b", NJF)]
    for l in range(depth):
        for name, n in spec:
            lay[(name, l)] = col
            col += n
    return lay, col


def pack_params(inputs, depth):
    lay, ncol = param_layout(depth)
    pp = np.zeros((128, ncol), np.float32)

    def put(name, l, arr2d):
        a = np.asarray(arr2d, np.float32)
        m = a.shape[0]
        nch = a.shape[1] // 128
        blk = a.reshape(m, nch, 128).transpose(2, 0, 1).reshape(128, m * nch)
        c0 = lay[(name, l)]
        pp[:, c0:c0 + m * nch] = blk

    for l in range(depth):
        for name in ("b_ada", "b_in", "q_gain", "k_gain", "conv_dw_b", "conv_ln_g", "conv_ln_b",
                     "conv_pw_b", "pool_scale", "b_out", "ln1_g", "ln1_b", "ln2_g", "ln2_b", "ffn_dw_b"):
            put(name, l, np.asarray(inputs[name][l])[None, :])
        put("conv_dw_w", l, inputs["conv_dw_w"][l])
        put("ffn_dw_w", l, inputs["ffn_dw_w"][l])
    return pp


def const_tables():
    cb = np.zeros((128, 5, 128), np.float32)
    cb[:, 0, :] = np.eye(128, dtype=np.float32)
    cb[:, 1, :] = 1.0 / 1024.0
    cb[:, 2, :] = 1.0 / 128.0
    cb[:, 3, :] = 1.0
    rot = np.zeros((128, 128), np.float32)
    for a in range(2):
        for i in range(32):
            rot[a * 64 + 32 + i, a * 64 + i] = -1.0
            rot[a * 64 + i, a * 64 + 32 + i] = 1.0
    cb[:, 4, :] = rot
    t = np.arange(T_LAT)
    row = (t // 64).astype(np.float32)
    colp = (t % 64).astype(np.float32)
    inv_freq = (np.float32(10000.0) ** (-np.arange(32, dtype=np.float32) / np.float32(32))).astype(np.float32)
    cs = np.zeros((2, 128, T_LAT), np.float32)
    for p in range(128):
        pos = row if p < 64 else colp
        ang = (pos * inv_freq[p % 32]).astype(np.float32)
        cs[0, p] = np.cos(ang)
        cs[1, p] = np.sin(ang)
    et = np.ones((128, 4, 2, 8), np.float32)
    n = 4096
    for wi, w in enumerate(POOL_WINDOWS):
        for i in range(8):
            tt = i
            lo = max(tt - w // 2, 0)
            hi = min(tt - w // 2 + w, n)
            et[:, wi, 0, i] = np.float32(w) / np.float32(hi - lo)
            tt = n - 8 + i
            lo = max(tt - w // 2, 0)
            hi = min(tt - w // 2 + w, n)
            et[:, wi, 1, i] = np.float32(w) / np.float32(hi - lo)
    return cb, cs, et


class TL:
    __slots__ = ("ap", "keys")

    def __init__(self, ap, keys):
        self.ap = ap
        self.keys = tuple(keys)

    def v(self, ap):
        return TL(ap, self.keys)


def _ap(x):
    return x.ap if isinstance(x, TL) else x


class _Op:
    __slots__ = ("eng", "fn", "deps", "dma", "sig", "signo", "dsem", "dval", "tag")

    def __init__(self, eng, fn, deps, dma):
        self.eng = eng
        self.fn = fn
        self.deps = deps
        self.dma = dma
        self.sig = False
        self.signo = 0
        self.dsem = 0
        self.dval = 0


EPOCH = 30000
FAST_RECIP = False
import os
KOPT = int(os.environ.get('KOPT', '7'))
ND = 16


class Sched:
    def __init__(self, nc):
        self.nc = nc
        self.ops = []
        self.lastw = {}
        self.readers = {}
        self.dmas = []
        self.tag = None
        self.use_tags = False

    def op(self, eng, fn, ins=(), outs=(), dma=False):
        deps = set()
        for x in ins:
            if isinstance(x, TL):
                for k in x.keys:
                    w = self.lastw.get(k)
                    if w is not None:
                        deps.add(w)
        for x in outs:
            if isinstance(x, TL):
                for k in x.keys:
                    w = self.lastw.get(k)
                    if w is not None:
                        deps.add(w)
                    for r in self.readers.get(k, ()):
                        deps.add(r)
        idx = len(self.ops)
        if dma:
            k = len(self.dmas)
            if k >= ND:
                deps.add(self.dmas[k - ND])
            self.dmas.append(idx)
        o = _Op(eng, fn, sorted(deps), dma)
        o.tag = self.tag
        if dma:
            k = len(self.dmas) - 1
            o.dsem = k % ND
            o.dval = 16 * (k // ND + 1)
        for d in o.deps:
            self.ops[d].sig = True
        self.ops.append(o)
        for x in ins:
            if isinstance(x, TL):
                for k in x.keys:
                    self.readers.setdefault(k, []).append(idx)
        for x in outs:
            if isinstance(x, TL):
                for k in x.keys:
                    self.lastw[k] = idx
                    self.readers[k] = []
        return idx

    def act(self, out, in_, func, bias=0.0, scale=1.0, eng="act"):
        o, i, b, s = _ap(out), _ap(in_), _ap(bias), _ap(scale)
        self.op("act", lambda e: e.activation(out=o, in_=i, func=func, bias=b, scale=s),
                (in_, bias, scale), (out,))

    def tt(self, eng, out, in0, in1, op):
        o, a, b = _ap(out), _ap(in0), _ap(in1)
        self.op(eng, lambda e: e.tensor_tensor(out=o, in0=a, in1=b, op=op), (in0, in1), (out,))

    def ts(self, eng, out, in0, s1, s2, op0, op1=None):
        o, a, x1, x2 = _ap(out), _ap(in0), _ap(s1), _ap(s2)
        if op1 is None:
            self.op(eng, lambda e: e.tensor_scalar(out=o, in0=a, scalar1=x1, scalar2=None, op0=op0),
                    (in0, s1), (out,))
        else:
            self.op(eng, lambda e: e.tensor_scalar(out=o, in0=a, scalar1=x1, scalar2=x2, op0=op0, op1=op1),
                    (in0, s1, s2), (out,))

    def stt(self, eng, out, in0, sc, in1, op0, op1):
        o, a, s, b = _ap(out), _ap(in0), _ap(sc), _ap(in1)
        self.op(eng, lambda e: e.scalar_tensor_tensor(out=o, in0=a, scalar=s, in1=b, op0=op0, op1=op1),
                (in0, sc, in1), (out,))

    def copy(self, eng, out, in_):
        o, i = _ap(out), _ap(in_)
        if eng == "act":
            self.op(eng, lambda e: e.copy(out=o, in_=i), (in_,), (out,))
        else:
            self.op(eng, lambda e: e.tensor_copy(out=o, in_=i), (in_,), (out,))

    def memset(self, eng, out, val):
        o = _ap(out)
        self.op(eng, lambda e: e.memset(o, val), (), (out,))

    def recip(self, out, in_):
        o, i = _ap(out), _ap(in_)
        if FAST_RECIP:
            self.op("dve", lambda e: e.reciprocal_approx_fast(out=o, in_=i), (in_,), (out,))
        else:
            self.op("dve", lambda e: e.reciprocal(out=o, in_=i), (in_,), (out,))

    def mm(self, out, pairs, start=True, stop=True):
        o = _ap(out)
        ps = [(_ap(a), _ap(b)) for a, b in pairs]
        n = len(ps)

        def fn(e):
            r = None
            for i, (a, b) in enumerate(ps):
                r = e.matmul(o, a, b, start=(start and i == 0), stop=(stop and i == n - 1))
            return r
        ins = [a for a, _ in pairs] + [b for _, b in pairs]
        if not start:
            ins.append(out)
        self.op("pe", fn, ins, (out,))

    def dma(self, eng, out, in_):
        o, i = _ap(out), _ap(in_)
        return self.op(eng, lambda e: e.dma_start(out=o, in_=i), (in_,), (out,), dma=True)

    def emit(self, final_deps):
        nc = self.nc
        engs = ["pe", "act", "dve", "pool", "sp"]
        fin = _Op("sp", None, sorted(final_deps), False)
        fin.tag = None
        for d in fin.deps:
            self.ops[d].sig = True
        self.ops.append(fin)
        cnt = {e: 0 for e in engs}
        for o in self.ops:
            if o.dma or not o.sig:
                continue
            cnt[o.eng] += 1
            o.signo = cnt[o.eng]
        with contextlib.ExitStack() as st:
            esems = {}
            for e in engs:
                nep = max(1, (cnt[e] + EPOCH - 1) // EPOCH)
                esems[e] = [st.enter_context(nc.semaphore(f"s_{e}{i}")) for i in range(nep)]
            dsems = [st.enter_context(nc.semaphore(f"s_dma{i}")) for i in range(ND)]
            block = st.enter_context(nc.Block())
            per = {e: [o for o in self.ops if o.eng == e] for e in engs}
            ops = self.ops

            def run(e, eng):
                seen_e = {}
                seen_d = {}
                for o in per[e]:
                    for d in o.deps:
                        p = ops[d]
                        if p.dma:
                            if seen_d.get(p.dsem, 0) >= p.dval:
                                continue
                            seen_d[p.dsem] = p.dval
                            eng.wait_ge(dsems[p.dsem], p.dval)
                        else:
                            if p.eng == e and e == "pe":
                                continue
                            if seen_e.get(p.eng, 0) >= p.signo:
                                continue
                            seen_e[p.eng] = p.signo
                            ep = (p.signo - 1) // EPOCH
                            eng.wait_ge(esems[p.eng][ep], p.signo - ep * EPOCH)
                    if o.fn is None:
                        continue
                    if self.use_tags and o.tag:
                        with nc.named_scope(o.tag):
                            r = o.fn(eng)
                    else:
                        r = o.fn(eng)
                    if o.dma:
                        r.then_inc(dsems[o.dsem], 16)
                    elif o.sig:
                        ep = (o.signo - 1) // EPOCH
                        r.then_inc(esems[e][ep], 1)

            block.tensor(lambda eng: run("pe", eng))
            block.scalar(lambda eng: run("act", eng))
            block.vector(lambda eng: run("dve", eng))
            block.gpsimd(lambda eng: run("pool", eng))
            block.sync(lambda eng: run("sp", eng))


def build_program(nb, depth, debug=False):
    nc = bass.Bass("TRN2", target_bir_lowering=False)
    lay, npcol = param_layout(depth)
    NS = nb + 1

    def din(name, shape, dt=F32):
        return nc.dram_tensor(name, list(shape), dt, kind="ExternalInput").ap()

    xT = din("xT", [nb, D, T_LAT])
    ctxT = din("ctxT", [nb, D, T_CTX])
    csT = din("csT", [128, KC, NS])
    pp_d = din("pp", [128, npcol])
    bvb_d = din("bvb", [128, depth, 256])
    cb_d = din("cb", [128, 5, 128])
    cs_d = din("cs", [2, 128, T_LAT])
    et_d = din("et", [128, 4, 2, 8])
    w_ada = din("w_ada", [depth, D, 6 * D])
    w_in = din("w_in", [depth, D, D_IN])
    conv_pw_w = din("conv_pw_w", [depth, D, D])
    pool_w = din("pool_w", [depth, 4, 256, 256])
    w_out = din("w_out", [depth, D, D])
    w_up = din("w_up", [depth, D, 2 * D_FF])
    w_down = din("w_down", [depth, D_FF, D])
    outT = nc.dram_tensor("outT", [nb, D, T_LAT], F32, kind="ExternalOutput").ap()

    win_b = nc.dram_tensor("win_b", [depth, 60, 128, KC, 128], BF16).ap()
    pw_b = nc.dram_tensor("pw_b", [depth, 8, 128, KC, 128], BF16).ap()
    wo_b = nc.dram_tensor("wo_b", [depth, 8, 128, KC, 128], BF16).ap()
    wup_b = nc.dram_tensor("wup_b", [depth, 44, 128, KC, 128], BF16).ap()
    plw_b = nc.dram_tensor("plw_b", [depth, 4, 128, 2, 256], BF16).ap()
    wd_b = nc.dram_tensor("wd_b", [depth, 2, NJF, 128, 512], BF16).ap()

    S = Sched(nc)
    S.use_tags = (debug == "prof")
    st = contextlib.ExitStack()

    def sb(name, shape, dt):
        return st.enter_context(nc.sbuf_tensor(name, list(shape), dt))

    def pst(name):
        return st.enter_context(nc.psum_tensor(name, [128, 512], F32))

    X = sb("X", [128, KC, T_ALL], F32)
    KT = sb("KT", [128, 2, T_ALL], BF16)
    V = sb("V", [128, T_ALL // 128, 256], BF16)
    HC = sb("HC", [128, KC, EXT], BF16)
    HS = sb("HS", [128, KC, 16], BF16)
    PP = sb("PP", [128, npcol], F32)
    ADA = sb("ADA", [128, depth, 6, KC, NS], F32)
    CST = sb("CST", [128, KC, NS], F32)
    CB = sb("CB", [128, 5, 128], BF16)
    ET = sb("ET", [128, 4, 2, 8], F32)
    COS = sb("COS", [128, CH], F32)
    SIN = sb("SIN", [128, CH], F32)
    NWS = 9
    WS = sb("WS", [128, NWS, KC * 128], BF16)
    ARENA = sb("ARENA", [128, 4, 8, EXT], BF16)
    NPT = 5
    PT = sb("PT", [128, NPT, CH], BF16)
    NT32 = 7
    T32 = sb("T32", [128, NT32, EXT], F32)
    MURS = sb("MURS", [128, 2, CH], F32)
    GB = sb("GB", [128, depth, 24], F32)
    NT16 = 4
    T16 = sb("T16", [128, NT16, CH], BF16)
    NDG = 16
    DG = sb("DG", [128, NDG, 128], BF16)
    PL = sb("PL", [128, 2, 2, CH], BF16)
    NPG = 4 if (KOPT & 2) else 2
    PG = [pst(f"pg{i}") for i in range(NPG)]
    PD = [pst(f"pd{i}") for i in range(4)]

    ctr = {"pg": 0, "ws": 0, "pt": 0, "t32": 0, "t16": 0, "dg": 0}

    def nxt(name, n):
        i = ctr[name] % n
        ctr[name] += 1
        return i

    def pg():
        i = nxt("pg", NPG)
        return TL(PG[i][:, :], [("PG", i)])

    def pd(i):
        return TL(PD[i][:, :], [("PD", i)])

    def pstat(i):
        return TL(PD[2 + i][:, :], [("PD", 2 + i)])

    def t32():
        i = nxt("t32", NT32)
        return TL(T32[:, i, :], [("T32", i)])

    def t16():
        i = nxt("t16", NT16)
        return TL(T16[:, i, :], [("T16", i)])

    def ptile():
        i = nxt("pt", NPT)
        return TL(PT[:, i, :], [("PT", i)])

    def dgt():
        i = nxt("dg", NDG)
        return TL(DG[:, i, :], [("DG", i)])

    def wslot(src_ap, src_key, width=KC * 128):
        i = nxt("ws", NWS)
        t = TL(WS[:, i, 0:width], [("WS", i)])
        S.dma("sp", t, TL(src_ap, [src_key]))
        return i, t

    def pcol(name, l, c):
        c0 = lay[(name, l)] + c
        return TL(PP[:, c0:c0 + 1], [("PP",)])

    IDENT = TL(CB[:, 0, :], [("CB",)])
    ONESD = TL(CB[:, 1, :], [("CB",)])
    ONESH = TL(CB[:, 2, :], [("CB",)])
    ONES1 = TL(CB[:, 3, :], [("CB",)])
    ROT = TL(CB[:, 4, :], [("CB",)])

    def Xt(j, c, a, b):
        return TL(X[:, j, a:b], [("X", j, c)])

    F32T = []
    for i4 in range(4):
        apv = ARENA[:, 1, 2 * i4:2 * i4 + 2, :].rearrange("p a b -> p (a b)").bitcast(F32)
        F32T.append(TL(apv, [("AR", 1, 2 * i4), ("AR", 1, 2 * i4 + 1)]))
    F32M = [TL(MURS[:, 0, :], [("MURS", 0)]), TL(MURS[:, 1, :], [("MURS", 1)])]

    def ar(blk, j, a=0, b=CH):
        return TL(ARENA[:, blk, j, a:b], [("AR", blk, j)])

    S.dma("sp", TL(PP[:, :], [("PP",)]), pp_d)
    S.dma("sp", TL(CST[:, :, :], [("CST",)]), csT)
    S.dma("sp", TL(ET[:, :, :, :], [("ET",)]), et_d)
    S.dma("pool", TL(CB[:, :, :], [("CB",)]), cb_d)
    cst = TL(CST[:, :, :], [("CST",)])
    S.act(cst, cst, AF.Silu)

    for l in range(depth):
        c0 = lay[("b_in", l)] + POOL_END // 128
        S.ts("dve", TL(GB[:, l, :], [("GB",)]), TL(PP[:, c0:c0 + 24], [("PP",)]), 0.5, None, ALU.mult)

    def gbcol(l, i):
        return TL(GB[:, l, i:i + 1], [("GB",)])

    for l in range(depth):
        for g in range(48):
            i = nxt("ws", NWS)
            wt = TL(WS[:, i, :].bitcast(F32), [("WS", i)])
            half = []
            ps = pg()
            for hh in range(2):
                if hh == 1:
                    i = nxt("ws", NWS)
                    wt = TL(WS[:, i, :].bitcast(F32), [("WS", i)])
                src = w_ada[l, hh * 512:(hh + 1) * 512, g * 128:(g + 1) * 128].rearrange("(k p) c -> p k c", p=128)
                S.dma("sp", TL(wt.ap.rearrange("p (k c) -> p k c", k=4), wt.keys), src)
                half.append(wt)
            pairs = []
            for hh in range(2):
                wv = half[hh].ap.rearrange("p (k c) -> p k c", k=4)
                for k4 in range(4):
                    pairs.append((TL(wv[:, k4, :], half[hh].keys), TL(CST[:, hh * 4 + k4, :], [("CST",)])))
            S.mm(TL(ps.ap[:, 0:NS], ps.keys), pairs)
            v, kc = g // 8, g % 8
            addone = 1.0 if v in (1, 4) else 0.0
            S.ts("dve", TL(ADA[:, l, v, kc, :], [("ADA",)]), TL(ps.ap[:, 0:NS], ps.keys),
                 pcol("b_ada", l, g), addone, ALU.add, ALU.add)

    for l in range(depth):
        for g in range(60):
            S.dma("pool", TL(win_b[l, g], [("win", l, g)]),
                  w_in[l, :, g * 128:(g + 1) * 128].rearrange("(k p) c -> p k c", p=128))
        for g in range(8):
            S.dma("pool", TL(pw_b[l, g], [("pw", l, g)]),
                  conv_pw_w[l, :, g * 128:(g + 1) * 128].rearrange("(k p) c -> p k c", p=128))
            S.dma("pool", TL(wo_b[l, g], [("wo", l, g)]),
                  w_out[l, :, g * 128:(g + 1) * 128].rearrange("(k p) c -> p k c", p=128))
        for g in range(4):
            S.dma("pool", TL(plw_b[l, g], [("plw", l, g)]),
                  pool_w[l, g].rearrange("(i p) o -> p i o", p=128))
        for g in range(44):
            S.dma("pool", TL(wup_b[l, g], [("wup", l, g)]),
                  w_up[l, :, g * 128:(g + 1) * 128].rearrange("(k p) c -> p k c", p=128))
        for hf in range(2):
            for j in range(NJF):
                S.dma("pool", TL(wd_b[l, hf, j], [("wd", l, hf, j)]),
                      w_down[l, j * 128:(j + 1) * 128, hf * 512:(hf + 1) * 512])

    chunks = [(i * CH, CH, False, i == 0, i == T_LAT // CH - 1, i) for i in range(T_LAT // CH)]
    chunks.append((T_LAT, T_CTX, True, True, True, T_LAT // CH))

    def load_w(kind, l, g):
        src = {"win": win_b, "pw": pw_b, "wo": wo_b, "wup": wup_b}[kind]
        i, t = wslot(src[l, g].rearrange("p k c -> p (k c)"), (kind, l, g))
        return TL(WS[:, i, :].rearrange("p (k c) -> p k c", k=KC), t.keys)

    def hc_main(n):
        return [TL(HC[:, kc, HL:HL + n], [("HC", kc)]) for kc in range(KC)]

    def proj(wt, rhs_list, n):
        ps = pg()
        o = TL(ps.ap[:, 0:n], ps.keys)
        S.mm(o, [(TL(wt.ap[:, kc, :], wt.keys), rhs_list[kc]) for kc in range(KC)])
        return o

    def layer_stats(n, eps):
        mu_v = TL(MURS[:, 0, 0:n], [("MURS", 0)])
        rs_v = TL(MURS[:, 1, 0:n], [("MURS", 1)])
        p0 = pstat(0)
        p1 = pstat(1)
        S.copy("act", mu_v, TL(p0.ap[:, 0:n], p0.keys))
        S.tt("pool", rs_v, mu_v, mu_v, ALU.mult)
        S.tt("dve", rs_v, TL(p1.ap[:, 0:n], p1.keys), rs_v, ALU.subtract)
        S.act(rs_v, rs_v, AF.Sqrt, bias=eps)
        S.recip(rs_v, rs_v)
        return mu_v, rs_v

    def qk_norm_rope(ps, n, bias, gain, out, rope):
        qf = t32()
        sq = t16()
        qf_v = TL(qf.ap[:, 0:n], qf.keys)
        sq_v = TL(sq.ap[:, 0:n], sq.keys)
        S.act(qf_v, ps, AF.Identity, bias=bias)
        S.act(sq_v, ps, AF.Square, bias=bias)
        ms = pg()
        ms_v = TL(ms.ap[:, 0:n], ms.keys)
        S.mm(ms_v, [(ONESH, sq_v)])
        rs = t32()
        rs_v = TL(rs.ap[:, 0:n], rs.keys)
        S.act(rs_v, ms_v, AF.Sqrt, bias=RMS_EPS)
        S.recip(rs_v, rs_v)
        S.stt("dve", qf_v, qf_v, gain, rs_v, ALU.mult, ALU.mult)
        if not rope:
            S.copy("pool", out, qf_v)
            return
        qb = t16()
        qb_v = TL(qb.ap[:, 0:n], qb.keys)
        S.copy("pool", qb_v, qf_v)
        rt = pg()
        rt_v = TL(rt.ap[:, 0:n], rt.keys)
        S.mm(rt_v, [(ROT, qb_v)])
        S.tt("dve", rs_v, rt_v, TL(SIN[:, 0:n], [("SIN",)]), ALU.mult)
        S.tt("pool", qf_v, qf_v, TL(COS[:, 0:n], [("COS",)]), ALU.mult)
        S.tt("pool", out, qf_v, rs_v, ALU.add)

    def load_rope(c0, n):
        S.dma("sp", TL(COS[:, 0:n], [("COS",)]), cs_d[0, :, c0:c0 + n])
        S.dma("sp", TL(SIN[:, 0:n], [("SIN",)]), cs_d[1, :, c0:c0 + n])

    def modvec(l, v, kc, src):
        return TL(ADA[:, l, v, kc, src:src + 1], [("ADA",)])

    def make_hc(l, vsh, chunk, src, halo, use_halo):
        c0, n, is_ctx, first, last, ci = chunk
        lh = halo if (use_halo and not first) else 0
        rh = halo if (use_halo and not last) else 0
        if lh:
            S.copy("pool", TL(HC[:, :, HL - lh:HL], [("HC", kc) for kc in range(KC)]),
                   TL(HS[:, :, 0:lh], [("HS",)]))
        for kc in range(KC):
            keys = [("X", kc, ci)] + ([("X", kc, ci + 1)] if rh else [])
            S.ts("pool", TL(HC[:, kc, HL:HL + n + rh], [("HC", kc)]),
                 TL(X[:, kc, c0:c0 + n + rh], keys),
                 modvec(l, vsh + 1, kc, src), modvec(l, vsh, kc, src), ALU.mult, ALU.add)
        if rh:
            S.copy("pool", TL(HS[:, :, 0:halo], [("HS",)]),
                   TL(HC[:, :, HL + n - halo:HL + n], [("HC", kc) for kc in range(KC)]))
        return lh, rh

    def ln_apply(l, gname, bname, chunk):
        c0, n, is_ctx, first, last, ci = chunk
        for j in range(KC):
            xj = Xt(j, ci, c0, c0 + n)
            rb = t16()
            rq = t16()
            rb_v = TL(rb.ap[:, 0:n], rb.keys)
            rq_v = TL(rq.ap[:, 0:n], rq.keys)
            S.copy("act", rb_v, xj)
            S.act(rq_v, xj, AF.Square)
            p0, p1 = pstat(0), pstat(1)
            S.mm(TL(p0.ap[:, 0:n], p0.keys), [(ONESD, rb_v)], start=(j == 0), stop=(j == KC - 1))
            S.mm(TL(p1.ap[:, 0:n], p1.keys), [(ONESD, rq_v)], start=(j == 0), stop=(j == KC - 1))
        mu, rs = layer_stats(n, LN_EPS)
        for j in range(KC):
            xj = Xt(j, ci, c0, c0 + n)
            t = t32()
            tv = TL(t.ap[:, 0:n], t.keys)
            S.tt("dve", tv, xj, mu, ALU.subtract)
            S.tt("pool", tv, tv, rs, ALU.mult)
            S.act(xj, tv, AF.Identity, bias=pcol(bname, l, j), scale=pcol(gname, l, j))

    out_dmas = []

    def dbg(name, tl, shape, dt):
        if not debug:
            return
        dd = nc.dram_tensor("dbg_" + name, list(shape), dt, kind="ExternalOutput").ap()
        out_dmas.append(S.dma("sp", dd, tl))

    dbg("ada", TL(ADA[:, :, :, :, :], [("ADA",)]), [128, depth, 6, KC, NS], F32)
    for b in range(nb):
        allx = [("X", j, c) for j in range(KC) for c in range(len(chunks))]
        S.dma("sp", TL(X[:, :, 0:T_LAT], [("X", j, c) for j in range(KC) for c in range(4)]),
              xT[b].rearrange("(k p) t -> p k t", p=128))
        S.dma("sp", TL(X[:, :, T_LAT:T_ALL], [("X", j, 4) for j in range(KC)]),
              ctxT[b].rearrange("(k p) t -> p k t", p=128))
        for l in range(depth):
            lastl = (l == depth - 1)
            for chunk in chunks:
                c0, n, is_ctx, first, last, ci = chunk
                src = nb if is_ctx else b
                S.tag = f"l{l}c{ci}_00kv"
                make_hc(l, 0, chunk, src, 0, False)
                hm = hc_main(n)
                if not is_ctx:
                    load_rope(c0, n)
                for hk in range(2):
                    wt = load_w("win", l, Q_END // 128 + hk)
                    ps = proj(wt, hm, n)
                    qk_norm_rope(ps, n, pcol("b_in", l, Q_END // 128 + hk), pcol("k_gain", l, 0),
                                 TL(KT[:, hk, c0:c0 + n], [("KT", hk, ci)]), not is_ctx)
                wv = [load_w("win", l, K_END // 128 + i) for i in range(2)]
                for tb in range(n // 128):
                    ps = pg()
                    for i in range(2):
                        S.mm(TL(ps.ap[:, i * 128:(i + 1) * 128], ps.keys),
                             [(TL(HC[:, kc, HL + tb * 128:HL + (tb + 1) * 128], [("HC", kc)]),
                               TL(wv[i].ap[:, kc, :], wv[i].keys)) for kc in range(KC)])
                    kb = c0 // 128 + tb
                    S.copy("dve" if tb % 2 else "act", TL(V[:, kb, :], [("V", kb)]), TL(ps.ap[:, 0:256], ps.keys))
            if b == 0 and l == 0:
                dbg("kt", TL(KT[:, :, :], [("KT", hk, c) for hk in range(2) for c in range(5)]), [128, 2, T_ALL], BF16)
                dbg("v", TL(V[:, :, :], [("V", kb) for kb in range(18)]), [128, 18, 256], BF16)
            for chunk in chunks:
                c0, n, is_ctx, first, last, ci = chunk
                if is_ctx and lastl:
                    continue
                src = nb if is_ctx else b
                S.tag = f"l{l}c{ci}_01hc"
                lh, rh = make_hc(l, 0, chunk, src, HL, True)
                hm = hc_main(n)
                if not is_ctx:
                    load_rope(c0, n)
                S.tag = f"l{l}c{ci}_02glu"
                for j in range(KC):
                    wa = load_w("win", l, V_END // 128 + j)
                    wg = load_w("win", l, V_END // 128 + 8 + j)
                    pa = proj(wa, hm, n)
                    pgt = proj(wg, hm, n)
                    ba = pcol("b_in", l, V_END // 128 + j)
                    bg = pcol("b_in", l, V_END // 128 + 8 + j)
                    sg = t32()
                    sg_v = TL(sg.ap[:, 0:n], sg.keys)
                    S.act(sg_v, pgt, AF.Sigmoid, bias=bg)
                    S.stt("dve", ar(0, j, HL, HL + n), pa, ba, sg_v, ALU.add, ALU.mult)
                    sides = []
                    if lh:
                        sides.append((0, HL, 0))
                    if rh:
                        sides.append((HL + n, HL + n + HL, 1))
                    if sides:
                        ph = pg()
                        for (a0, a1, si) in sides:
                            for (wt, off) in ((wa, 0), (wg, 32)):
                                S.mm(TL(ph.ap[:, off + si * HL: off + (si + 1) * HL], ph.keys),
                                     [(TL(wt.ap[:, kc, :], wt.keys), TL(HC[:, kc, a0:a1], [("HC", kc)]))
                                      for kc in range(KC)])
                        sh = t32()
                        for (a0, a1, si) in sides:
                            S.act(TL(sh.ap[:, si * HL:(si + 1) * HL], sh.keys),
                                  TL(ph.ap[:, 32 + si * HL:32 + (si + 1) * HL], ph.keys), AF.Sigmoid, bias=bg)
                            S.stt("dve", ar(0, j, a0, a1), TL(ph.ap[:, si * HL:(si + 1) * HL], ph.keys), ba,
                                  TL(sh.ap[:, si * HL:(si + 1) * HL], sh.keys), ALU.add, ALU.mult)
                    if not lh:
                        S.memset("pool", ar(0, j, 0, HL), 0.0)
                    if not rh:
                        S.memset("pool", ar(0, j, HL + n, HL + n + HL), 0.0)
                S.tag = f"l{l}c{ci}_03dw"
                for j in range(KC):
                    ps = pg()
                    ps_v = TL(ps.ap[:, 0:n], ps.keys)
                    taps = list(range(31))
                    for s0 in range(0, 31, 8):
                        grp = taps[s0:s0 + 8]
                        pairs = []
                        for k in grp:
                            dg = dgt()
                            if KOPT & 1:
                                S.ts("pool" if k % 2 else "dve", dg, IDENT, pcol("conv_dw_w", l, k * 8 + j), 1.0, ALU.mult, ALU.mult)
                            else:
                                S.ts("pool", dg, IDENT, pcol("conv_dw_w", l, k * 8 + j), None, ALU.mult)
                            pairs.append((dg, ar(0, j, k, k + n)))
                        S.mm(ps_v, pairs, start=(s0 == 0), stop=(s0 + 8 >= 31))
                    bd = pcol("conv_dw_b", l, j)
                    S.act(ar(1, j, 0, n), ps_v, AF.Identity, bias=bd)
                    sq = t16()
                    sq_v = TL(sq.ap[:, 0:n], sq.keys)
                    S.act(sq_v, ps_v, AF.Square, bias=bd)
                    p0, p1 = pstat(0), pstat(1)
                    S.mm(TL(p0.ap[:, 0:n], p0.keys), [(ONESD, ar(1, j, 0, n))], start=(j == 0), stop=(j == KC - 1))
                    S.mm(TL(p1.ap[:, 0:n], p1.keys), [(ONESD, sq_v)], start=(j == 0), stop=(j == KC - 1))
                S.tag = f"l{l}c{ci}_04cln"
                mu, rs = layer_stats(n, LN_EPS)
                for j in range(KC):
                    t = t32()
                    tv = TL(t.ap[:, 0:n], t.keys)
                    S.tt("dve", tv, ar(1, j, 0, n), mu, ALU.subtract)
                    S.tt("pool", tv, tv, rs, ALU.mult)
                    S.act(ar(2, j, 0, n), tv, AF.Silu, bias=pcol("conv_ln_b", l, j), scale=pcol("conv_ln_g", l, j))
                convh = [ar(2, kc, 0, n) for kc in range(KC)]
                kbs = list(range(T_LAT // 128, T_ALL // 128)) if is_ctx else list(range(T_ALL // 128))
                rope = not is_ctx
                tagp = f"l{l}c{ci}_"

                def q_chain(j):
                    p = j % 2
                    S.tag = tagp + "06q"
                    qf_v = TL(F32T[2 * p].ap[:, 0:n], F32T[2 * p].keys)
                    rs_v = TL(F32T[2 * p + 1].ap[:, 0:n], F32T[2 * p + 1].keys)
                    sq_v = TL(T16[:, p, 0:n], [("T16", p)])
                    out = ar(0, j, 0, n)
                    bias = pcol("b_in", l, j)
                    wq = load_w("win", l, j)
                    pq = proj(wq, hm, n)
                    S.act(qf_v, pq, AF.Identity, bias=bias)
                    S.act(sq_v, pq, AF.Square, bias=bias)
                    yield
                    S.tag = tagp + "06q"
                    ms = pg()
                    ms_v = TL(ms.ap[:, 0:n], ms.keys)
                    S.mm(ms_v, [(ONESH, sq_v)])
                    S.act(rs_v, ms_v, AF.Sqrt, bias=RMS_EPS)
                    S.recip(rs_v, rs_v)
                    S.stt("dve", qf_v, qf_v, pcol("q_gain", l, 0), rs_v, ALU.mult, ALU.mult)
                    if not rope:
                        S.copy("pool", out, qf_v)
                        return
                    S.copy("pool", sq_v, qf_v)
                    yield
                    S.tag = tagp + "06q"
                    rt = pg()
                    rt_v = TL(rt.ap[:, 0:n], rt.keys)
                    S.mm(rt_v, [(ROT, sq_v)])
                    S.tt("dve", rs_v, rt_v, TL(SIN[:, 0:n], [("SIN",)]), ALU.mult)
                    S.tt("pool", qf_v, qf_v, TL(COS[:, 0:n], [("COS",)]), ALU.mult)
                    S.tt("pool", out, qf_v, rs_v, ALU.add)

                def poolf_chain(g):
                    w = POOL_WINDOWS[g]
                    plh = 8 if not first else 0
                    prh = 8 if not last else 0
                    gp = g % 2
                    for jj in range(2):
                        S.tag = tagp + "05poolf"
                        jc = 2 * g + jj
                        wp = load_w("win", l, CONV_END // 128 + jc)
                        bp = pcol("b_in", l, CONV_END // 128 + jc)
                        pu = proj(wp, hm, n)
                        u = TL(T32[:, 4, :], [("T32", 4)])
                        S.act(TL(u.ap[:, 8:8 + n], u.keys), pu, AF.Identity, bias=bp)
                        hs = []
                        if plh:
                            hs.append((HL - 8, HL, 0, 0))
                        if prh:
                            hs.append((HL + n, HL + n + 8, 8 + n, 1))
                        if hs:
                            yield
                            S.tag = tagp + "05poolf"
                            ph = pg()
                            for (a0, a1, d0, si) in hs:
                                S.mm(TL(ph.ap[:, si * 8:(si + 1) * 8], ph.keys),
                                     [(TL(wp.ap[:, kc, :], wp.keys), TL(HC[:, kc, a0:a1], [("HC", kc)]))
                                      for kc in range(KC)])
                            for (a0, a1, d0, si) in hs:
                                S.act(TL(u.ap[:, d0:d0 + 8], u.keys), TL(ph.ap[:, si * 8:(si + 1) * 8], ph.keys),
                                      AF.Identity, bias=bp)
                        if not plh:
                            S.memset("pool", TL(u.ap[:, 0:8], u.keys), 0.0)
                        if not prh:
                            S.memset("pool", TL(u.ap[:, 8 + n:16 + n], u.keys), 0.0)
                        ln_ = 16 + n
                        cur = u
                        step = 1
                        ab = [TL(T32[:, 5, :], [("T32", 5)]), TL(T32[:, 6, :], [("T32", 6)])]
                        si_ = 0
                        while step < w:
                            nx = ab[si_ % 2]
                            si_ += 1
                            S.tt("pool" if step % 2 else "dve", TL(nx.ap[:, 0:ln_ - step], nx.keys),
                                 TL(cur.ap[:, 0:ln_ - step], cur.keys), TL(cur.ap[:, step:ln_], cur.keys), ALU.add)
                            ln_ -= step
                            cur = nx
                            step *= 2
                        o0 = 8 - w // 2
                        mean = ab[si_ % 2]
                        S.ts("dve", TL(mean.ap[:, 0:n], mean.keys), TL(cur.ap[:, o0:o0 + n], cur.keys),
                             1.0 / w, None, ALU.mult)
                        if first:
                            S.tt("dve", TL(mean.ap[:, 0:8], mean.keys), TL(mean.ap[:, 0:8], mean.keys),
                                 TL(ET[:, g, 0, :], [("ET",)]), ALU.mult)
                        if last:
                            S.tt("dve", TL(mean.ap[:, n - 8:n], mean.keys), TL(mean.ap[:, n - 8:n], mean.keys),
                                 TL(ET[:, g, 1, :], [("ET",)]), ALU.mult)
                        S.tt("pool", TL(PL[:, gp, jj, 0:n], [("PL", gp, jj)]), TL(mean.ap[:, 0:n], mean.keys),
                             TL(u.ap[:, 8:8 + n], u.keys), ALU.subtract)
                        yield

                def merge_chain(j):
                    p = j % 2
                    g = j // 2
                    gp = g % 2
                    S.tag = tagp + "08merge"
                    po = pd(2 * p)
                    pden = pd(2 * p + 1)
                    po_v = TL(po.ap[:, 0:n], po.keys)
                    pden_v = TL(pden.ap[:, 0:n], pden.keys)
                    rd_v = TL(T32[:, 2 * p, 0:n], [("T32", 2 * p)])
                    sg_v = TL(T32[:, 2 * p + 1, 0:n], [("T32", 2 * p + 1)])
                    S.recip(rd_v, pden_v)
                    S.tt("dve", rd_v, po_v, rd_v, ALU.mult)
                    yield
                    S.tag = tagp + "08merge"
                    wg0 = load_w("win", l, POOL_END // 128 + j)
                    pz = proj(wg0, hm, n)
                    S.act(sg_v, pz, AF.Tanh, bias=gbcol(l, j), scale=0.5)
                    S.ts("dve", sg_v, sg_v, 0.5, 0.5, ALU.mult, ALU.add)
                    S.stt("dve", rd_v, rd_v, pcol("b_in", l, K_END // 128 + j // 4), sg_v, ALU.add, ALU.mult)
                    yield
                    S.tag = tagp + "08merge"
                    wg1 = load_w("win", l, POOL_END // 128 + 8 + j)
                    pz = proj(wg1, hm, n)
                    sg1 = sg_v
                    S.act(sg1, pz, AF.Tanh, bias=gbcol(l, 8 + j), scale=0.5)
                    S.ts("pool", sg1, sg1, 0.5, 0.5, ALU.mult, ALU.add)
                    yield
                    S.tag = tagp + "08merge"
                    wpw = load_w("pw", l, j)
                    pc = proj(wpw, convh, n)
                    S.stt("dve", sg1, pc, pcol("conv_pw_b", l, j), sg1, ALU.add, ALU.mult)
                    S.tt("pool", rd_v, rd_v, sg1, ALU.add)
                    yield
                    S.tag = tagp + "08merge"
                    wg2 = load_w("win", l, POOL_END // 128 + 16 + j)
                    pz = proj(wg2, hm, n)
                    S.act(sg_v, pz, AF.Tanh, bias=gbcol(l, 16 + j), scale=0.5)
                    S.ts("pool", sg_v, sg_v, 0.5, 0.5, ALU.mult, ALU.add)
                    yield
                    S.tag = tagp + "08merge"
                    i, pwt = wslot(plw_b[l, g].rearrange("p i o -> p (i o)"), ("plw", l, g), 512)
                    plw_t = TL(WS[:, i, 0:512].rearrange("p (i o) -> p i o", i=2), pwt.keys)
                    ppo = pg()
                    ppo_v = TL(ppo.ap[:, 0:n], ppo.keys)
                    oc = j % 2
                    S.mm(ppo_v, [(TL(plw_t.ap[:, ic, oc * 128:(oc + 1) * 128], plw_t.keys),
                                  TL(PL[:, gp, ic, 0:n], [("PL", gp, ic)])) for ic in range(2)])
                    S.stt("dve", sg_v, ppo_v, pcol("pool_scale", l, j), sg_v, ALU.mult, ALU.mult)
                    S.tt("dve", ar(3, j, 0, n), rd_v, sg_v, ALU.add)

                def drain(tasks):
                    for t in tasks:
                        for _ in t:
                            pass

                def attention(j, tasks):
                    kv = j // 4
                    p = j % 2
                    qt_v = ar(0, j, 0, n)
                    po = pd(2 * p)
                    pden = pd(2 * p + 1)
                    po_v = TL(po.ap[:, 0:n], po.keys)
                    pden_v = TL(pden.ap[:, 0:n], pden.keys)
                    sts = {}
                    live = list(tasks)

                    def issue_st(ki_):
                        kb_ = kbs[ki_]
                        stp = pg()
                        sv = TL(stp.ap[:, 0:n], stp.keys)
                        S.mm(sv, [(TL(KT[:, kv, kb_ * 128:(kb_ + 1) * 128], [("KT", kv, min(kb_ // 4, 4))]), qt_v)])
                        sts[ki_] = sv
                    S.tag = tagp + "07attn"
                    issue_st(0)
                    rr = 0
                    for ki, kb in enumerate(kbs):
                        S.tag = tagp + "07attn"
                        if ki + 1 < len(kbs):
                            issue_st(ki + 1)
                        st_v = sts.pop(ki)
                        pt = ptile()
                        pt_v = TL(pt.ap[:, 0:n], pt.keys)
                        S.act(pt_v, st_v, AF.Exp, scale=ATTN_SCALE)
                        S.mm(po_v, [(TL(V[:, kb, kv * 128:(kv + 1) * 128], [("V", kb)]), pt_v)],
                             start=(ki == 0), stop=(ki == len(kbs) - 1))
                        S.mm(pden_v, [(ONES1, pt_v)], start=(ki == 0), stop=(ki == len(kbs) - 1))
                        if live:
                            t = live[rr % len(live)]
                            try:
                                next(t)
                                rr += 1
                            except StopIteration:
                                live.remove(t)
                    drain(live)

                drain([q_chain(0)])
                for j in range(KC):
                    tasks = []
                    if j + 1 < KC:
                        tasks.append(q_chain(j + 1))
                    if j >= 1:
                        tasks.append(merge_chain(j - 1))
                    if j == 0:
                        tasks.append(poolf_chain(0))
                    elif j % 2 == 1 and j + 1 < KC:
                        tasks.append(poolf_chain((j + 1) // 2))
                    attention(j, tasks)
                drain([merge_chain(KC - 1)])
                if b == 0 and l == 0 and ci in (0, 1):
                    dbg("arena%d" % ci, TL(ARENA[:, :, :, :], [("AR", bb, jj) for bb in range(4) for jj in range(8)]), [128, 4, 8, EXT], BF16)
                S.tag = f"l{l}c{ci}_09wout"
                mb = [ar(3, kc, 0, n) for kc in range(KC)]
                for j in range(KC):
                    wo = load_w("wo", l, j)
                    py = proj(wo, mb, n)
                    t = t32()
                    tv = TL(t.ap[:, 0:n], t.keys)
                    S.ts("dve", tv, py, pcol("b_out", l, j), modvec(l, 2, j, src), ALU.add, ALU.mult)
                    xj = Xt(j, ci, c0, c0 + n)
                    S.stt("dve", xj, xj, ALPHA, tv, ALU.mult, ALU.add)
                S.tag = f"l{l}c{ci}_10ln1"
                ln_apply(l, "ln1_g", "ln1_b", chunk)
            if b == 0 and l == 0:
                dbg("xmix", TL(X[:, :, :], allx), [128, KC, T_ALL], F32)
            for chunk in chunks:
                c0, n, is_ctx, first, last, ci = chunk
                if is_ctx and lastl:
                    continue
                src = nb if is_ctx else b
                S.tag = f"l{l}c{ci}_11fhc"
                lh, rh = make_hc(l, 3, chunk, src, 1, True)
                hm = hc_main(n)
                S.tag = f"l{l}c{ci}_12fup"
                def ffn_a(j):
                    wa = load_w("wup", l, j)
                    pa = proj(wa, hm, n)
                    a = t32()
                    S.copy("act", TL(a.ap[:, 1:1 + n], a.keys), pa)
                    hs = []
                    if lh:
                        hs.append((HL - 1, HL, 0, 0))
                    if rh:
                        hs.append((HL + n, HL + n + 1, n + 1, 1))
                    if hs:
                        ph = pg()
                        for (a0, a1, d0, si) in hs:
                            S.mm(TL(ph.ap[:, si:si + 1], ph.keys),
                                 [(TL(wa.ap[:, kc, :], wa.keys), TL(HC[:, kc, a0:a1], [("HC", kc)]))
                                  for kc in range(KC)])
                        for (a0, a1, d0, si) in hs:
                            S.copy("act", TL(a.ap[:, d0:d0 + 1], a.keys), TL(ph.ap[:, si:si + 1], ph.keys))
                    if not lh:
                        S.memset("pool", TL(a.ap[:, 0:1], a.keys), 0.0)
                    if not rh:
                        S.memset("pool", TL(a.ap[:, n + 1:n + 2], a.keys), 0.0)
                    return a

                def ffn_b(j, a):
                    t = t32()
                    tv = TL(t.ap[:, 0:n], t.keys)
                    S.ts("dve", tv, TL(a.ap[:, 1:1 + n], a.keys), pcol("ffn_dw_w", l, 1 * NJF + j),
                         pcol("ffn_dw_b", l, j), ALU.mult, ALU.add)
                    S.stt("dve", tv, TL(a.ap[:, 0:n], a.keys), pcol("ffn_dw_w", l, 0 * NJF + j), tv, ALU.mult, ALU.add)
                    S.stt("dve", tv, TL(a.ap[:, 2:2 + n], a.keys), pcol("ffn_dw_w", l, 2 * NJF + j), tv, ALU.mult, ALU.add)
                    S.act(tv, tv, AF.Silu)
                    wu = load_w("wup", l, NJF + j)
                    pu = proj(wu, hm, n)
                    S.tt("dve", ar(j // 8, j % 8, 0, n), tv, pu, ALU.mult)

                a_next = ffn_a(0)
                for j in range(NJF):
                    a_cur = a_next
                    if j + 1 < NJF:
                        a_next = ffn_a(j + 1)
                    ffn_b(j, a_cur)
                S.tag = f"l{l}c{ci}_13fdown"
                for hf in range(2):
                    for j in range(NJF):
                        i, wt = wslot(wd_b[l, hf, j], ("wd", l, hf, j), 512)
                        for i4 in range(4):
                            acc = pd(i4)
                            S.mm(TL(acc.ap[:, 0:n], acc.keys),
                                 [(TL(WS[:, i, i4 * 128:(i4 + 1) * 128], wt.keys), ar(j // 8, j % 8, 0, n))],
                                 start=(j == 0), stop=(j == NJF - 1))
                    for i4 in range(4):
                        jo = hf * 4 + i4
                        acc = pd(i4)
                        t = t32()
                        tv = TL(t.ap[:, 0:n], t.keys)
                        S.ts("dve", tv, TL(acc.ap[:, 0:n], acc.keys), modvec(l, 5, jo, src), None, ALU.mult)
                        xj = Xt(jo, ci, c0, c0 + n)
                        S.stt("dve", xj, xj, ALPHA, tv, ALU.mult, ALU.add)
                S.tag = f"l{l}c{ci}_14ln2"
                ln_apply(l, "ln2_g", "ln2_b", chunk)
                if lastl and not is_ctx:
                    d = S.dma("pool", outT[b, :, c0:c0 + n].rearrange("(k p) t -> p k t", p=128),
                              TL(X[:, :, c0:c0 + n], [("X", j, ci) for j in range(KC)]))
                    out_dmas.append(d)

    S.emit(out_dmas)
    st.close()
    return nc


_CACHE = {}


def _get_program(nb, depth, debug=False):
    key = (nb, depth, debug)
    if key not in _CACHE:
        _CACHE[key] = build_program(nb, depth, debug)
    return _CACHE[key]


def make_in_maps(inputs, ncores, nb, depth):
    f = lambda a: np.ascontiguousarray(np.asarray(a, dtype=np.float32))
    x = f(inputs["x"])
    ctx = f(inputs["ctx"])
    c = f(inputs["c"])
    c_ctx = f(inputs["c_ctx"])
    pp = pack_params(inputs, depth)
    cb, cs, et = const_tables()
    bvb = np.ascontiguousarray(np.broadcast_to(
        f(inputs["b_in"])[None, :depth, K_END:V_END], (128, depth, 256)))
    shared = {
        "pp": pp, "bvb": bvb, "cb": cb, "cs": cs, "et": et,
        "w_ada": f(inputs["w_ada"])[:depth], "w_in": f(inputs["w_in"])[:depth],
        "conv_pw_w": f(inputs["conv_pw_w"])[:depth], "pool_w": f(inputs["pool_w"])[:depth],
        "w_out": f(inputs["w_out"])[:depth], "w_up": f(inputs["w_up"])[:depth],
        "w_down": f(inputs["w_down"])[:depth],
    }
    maps = []
    for i in range(ncores):
        sl = slice(i * nb, (i + 1) * nb)
        cc = np.concatenate([c[sl], c_ctx[None, :]], axis=0)
        csT = np.ascontiguousarray(cc.reshape(nb + 1, KC, 128).transpose(2, 1, 0))
        m = dict(shared)
        m["xT"] = np.ascontiguousarray(x[sl].transpose(0, 2, 1))
        m["ctxT"] = np.ascontiguousarray(ctx[sl].transpose(0, 2, 1))
        m["csT"] = csT
        maps.append(m)
    return maps


def run(inputs, ncores, nb, depth, debug=False):
    nc = _get_program(nb, depth, debug)
    maps = make_in_maps(inputs, ncores, nb, depth)
    res = run_bass_kernel_spmd(nc, maps, core_ids=list(range(ncores)))
    if debug:
        global DBG
        DBG = {k: np.asarray(v) for k, v in res.results[0].items() if k.startswith("dbg_")}
    outs = [np.asarray(r["outT"]).transpose(0, 2, 1) for r in res.results]
    return np.ascontiguousarray(np.concatenate(outs, axis=0).astype(np.float32))


def kernel(**inputs):
    return run(inputs, 8, BATCH // 8, DEPTH)
```

```python
import contextlib
import numpy as np
import concourse.bass as bass
import concourse.mybir as mybir
from concourse.bass_utils import run_bass_kernel_spmd

F32 = mybir.dt.float32
BF16 = mybir.dt.bfloat16
AF = mybir.ActivationFunctionType
ALU = mybir.AluOpType

D = 1024
KC = 8
T_LAT = 2048
T_CTX = 256
T_ALL = T_LAT + T_CTX
CH = 512
DEPTH = 4
BATCH = 32
D_FF = 2816
NJF = D_FF // 128
D_IN = 7680
Q_END, K_END, V_END, CONV_END, POOL_END = 1024, 1280, 1536, 3584, 4608
POOL_WINDOWS = (2, 4, 8, 16)
ALPHA = float((2 * DEPTH) ** 0.25)
LN_EPS = 1e-5
RMS_EPS = 1e-6
ATTN_SCALE = float(128 ** -0.5)
HL = 15
EXT = 544


def param_layout(depth):
    lay = {}
    col = 0
    spec = [("b_ada", 48), ("b_in", 60), ("q_gain", 1), ("k_gain", 1), ("conv_dw_w", 31 * 8),
            ("conv_dw_b", 8), ("conv_ln_g", 8), ("conv_ln_b", 8), ("conv_pw_b", 8), ("pool_scale", 8),
            ("b_out", 8), ("ln1_g", 8), ("ln1_b", 8), ("ln2_g", 8), ("ln2_b", 8),
            ("ffn_dw_w", 3 * NJF), ("ffn_dw_b", NJF)]
    for l in range(depth):
        for name, n in spec:
            lay[(name, l)] = col
            col += n
    return lay, col


def pack_params(inputs, depth):
    lay, ncol = param_layout(depth)
    pp = np.zeros((128, ncol), np.float32)

    def put(name, l, arr2d):
        a = np.asarray(arr2d, np.float32)
        m = a.shape[0]
        nch = a.shape[1] // 128
        blk = a.reshape(m, nch, 128).transpose(2, 0, 1).reshape(128, m * nch)
        c0 = lay[(name, l)]
        pp[:, c0:c0 + m * nch] = blk

    for l in range(depth):
        for name in ("b_ada", "b_in", "q_gain", "k_gain", "conv_dw_b", "conv_ln_g", "conv_ln_b",
                     "conv_pw_b", "pool_scale", "b_out", "ln1_g", "ln1_b", "ln2_g", "ln2_b", "ffn_dw_b"):
            put(name, l, np.asarray(inputs[name][l])[None, :])
        put("conv_dw_w", l, inputs["conv_dw_w"][l])
        put("ffn_dw_w", l, inputs["ffn_dw_w"][l])
    return pp


def const_tables():
    cb = np.zeros((128, 5, 128), np.float32)
    cb[:, 0, :] = np.eye(128, dtype=np.float32)
    cb[:, 1, :] = 1.0 / 1024.0
    cb[:, 2, :] = 1.0 / 128.0
    cb[:, 3, :] = 1.0
    rot = np.zeros((128, 128), np.float32)
    for a in range(2):
        for i in range(32):
            rot[a * 64 + 32 + i, a * 64 + i] = -1.0
            rot[a * 64 + i, a * 64 + 32 + i] = 1.0
    cb[:, 4, :] = rot
    t = np.arange(T_LAT)
    row = (t // 64).astype(np.float32)
    colp = (t % 64).astype(np.float32)
    inv_freq = (np.float32(10000.0) ** (-np.arange(32, dtype=np.float32) / np.float32(32))).astype(np.float32)
    cs = np.zeros((2, 128, T_LAT), np.float32)
    for p in range(128):
        pos = row if p < 64 else colp
        ang = (pos * inv_freq[p % 32]).astype(np.float32)
        cs[0, p] = np.cos(ang)
        cs[1, p] = np.sin(ang)
    et = np.ones((128, 4, 2, 8), np.float32)
    n = 4096
    for wi, w in enumerate(POOL_WINDOWS):
        for i in range(8):
            tt = i
            lo = max(tt - w // 2, 0)
            hi = min(tt - w // 2 + w, n)
            et[:, wi, 0, i] = np.float32(w) / np.float32(hi - lo)
            tt = n - 8 + i
            lo = max(tt - w // 2, 0)
            hi = min(tt - w // 2 + w, n)
            et[:, wi, 1, i] = np.float32(w) / np.float32(hi - lo)
    return cb, cs, et


class TL:
    __slots__ = ("ap", "keys")

    def __init__(self, ap, keys):
        self.ap = ap
        self.keys = tuple(keys)

    def v(self, ap):
        return TL(ap, self.keys)


def _ap(x):
    return x.ap if isinstance(x, TL) else x


class _Op:
    __slots__ = ("eng", "fn", "deps", "dma", "sig", "signo", "dsem", "dval", "tag")

    def __init__(self, eng, fn, deps, dma):
        self.eng = eng
        self.fn = fn
        self.deps = deps
        self.dma = dma
        self.sig = False
        self.signo = 0
        self.dsem = 0
        self.dval = 0


EPOCH = 30000
FAST_RECIP = False
KOPT = 7
ND = 16


class Sched:
    def __init__(self, nc):
        self.nc = nc
        self.ops = []
        self.lastw = {}
        self.readers = {}
        self.dmas = []
        self.tag = None
        self.use_tags = False

    def op(self, eng, fn, ins=(), outs=(), dma=False):
        deps = set()
        for x in ins:
            if isinstance(x, TL):
                for k in x.keys:
                    w = self.lastw.get(k)
                    if w is not None:
                        deps.add(w)
        for x in outs:
            if isinstance(x, TL):
                for k in x.keys:
                    w = self.lastw.get(k)
                    if w is not None:
                        deps.add(w)
                    for r in self.readers.get(k, ()):
                        deps.add(r)
        idx = len(self.ops)
        if dma:
            k = len(self.dmas)
            if k >= ND:
                deps.add(self.dmas[k - ND])
            self.dmas.append(idx)
        o = _Op(eng, fn, sorted(deps), dma)
        o.tag = self.tag
        if dma:
            k = len(self.dmas) - 1
            o.dsem = k % ND
            o.dval = 16 * (k // ND + 1)
        for d in o.deps:
            self.ops[d].sig = True
        self.ops.append(o)
        for x in ins:
            if isinstance(x, TL):
                for k in x.keys:
                    self.readers.setdefault(k, []).append(idx)
        for x in outs:
            if isinstance(x, TL):
                for k in x.keys:
                    self.lastw[k] = idx
                    self.readers[k] = []
        return idx

    def act(self, out, in_, func, bias=0.0, scale=1.0, eng="act"):
        o, i, b, s = _ap(out), _ap(in_), _ap(bias), _ap(scale)
        self.op("act", lambda e: e.activation(out=o, in_=i, func=func, bias=b, scale=s),
                (in_, bias, scale), (out,))

    def tt(self, eng, out, in0, in1, op):
        o, a, b = _ap(out), _ap(in0), _ap(in1)
        self.op(eng, lambda e: e.tensor_tensor(out=o, in0=a, in1=b, op=op), (in0, in1), (out,))

    def ts(self, eng, out, in0, s1, s2, op0, op1=None):
        o, a, x1, x2 = _ap(out), _ap(in0), _ap(s1), _ap(s2)
        if op1 is None:
            self.op(eng, lambda e: e.tensor_scalar(out=o, in0=a, scalar1=x1, scalar2=None, op0=op0),
                    (in0, s1), (out,))
        else:
            self.op(eng, lambda e: e.tensor_scalar(out=o, in0=a, scalar1=x1, scalar2=x2, op0=op0, op1=op1),
                    (in0, s1, s2), (out,))

    def stt(self, eng, out, in0, sc, in1, op0, op1):
        o, a, s, b = _ap(out), _ap(in0), _ap(sc), _ap(in1)
        self.op(eng, lambda e: e.scalar_tensor_tensor(out=o, in0=a, scalar=s, in1=b, op0=op0, op1=op1),
                (in0, sc, in1), (out,))

    def copy(self, eng, out, in_):
        o, i = _ap(out), _ap(in_)
        if eng == "act":
            self.op(eng, lambda e: e.copy(out=o, in_=i), (in_,), (out,))
        else:
            self.op(eng, lambda e: e.tensor_copy(out=o, in_=i), (in_,), (out,))

    def memset(self, eng, out, val):
        o = _ap(out)
        self.op(eng, lambda e: e.memset(o, val), (), (out,))

    def recip(self, out, in_):
        o, i = _ap(out), _ap(in_)
        if FAST_RECIP:
            self.op("dve", lambda e: e.reciprocal_approx_fast(out=o, in_=i), (in_,), (out,))
        else:
            self.op("dve", lambda e: e.reciprocal(out=o, in_=i), (in_,), (out,))

    def mm(self, out, pairs, start=True, stop=True):
        o = _ap(out)
        ps = [(_ap(a), _ap(b)) for a, b in pairs]
        n = len(ps)

        def fn(e):
            r = None
            for i, (a, b) in enumerate(ps):
                r = e.matmul(o, a, b, start=(start and i == 0), stop=(stop and i == n - 1))
            return r
        ins = [a for a, _ in pairs] + [b for _, b in pairs]
        if not start:
            ins.append(out)
        self.op("pe", fn, ins, (out,))

    def dma(self, eng, out, in_):
        o, i = _ap(out), _ap(in_)
        return self.op(eng, lambda e: e.dma_start(out=o, in_=i), (in_,), (out,), dma=True)

    def emit(self, final_deps):
        nc = self.nc
        engs = ["pe", "act", "dve", "pool", "sp"]
        fin = _Op("sp", None, sorted(final_deps), False)
        fin.tag = None
        for d in fin.deps:
            self.ops[d].sig = True
        self.ops.append(fin)
        cnt = {e: 0 for e in engs}
        for o in self.ops:
            if o.dma or not o.sig:
                continue
            cnt[o.eng] += 1
            o.signo = cnt[o.eng]
        with contextlib.ExitStack() as st:
            esems = {}
            for e in engs:
                nep = max(1, (cnt[e] + EPOCH - 1) // EPOCH)
                esems[e] = [st.enter_context(nc.semaphore(f"s_{e}{i}")) for i in range(nep)]
            dsems = [st.enter_context(nc.semaphore(f"s_dma{i}")) for i in range(ND)]
            block = st.enter_context(nc.Block())
            per = {e: [o for o in self.ops if o.eng == e] for e in engs}
            ops = self.ops

            def run(e, eng):
                seen_e = {}
                seen_d = {}
                for o in per[e]:
                    for d in o.deps:
                        p = ops[d]
                        if p.dma:
                            if seen_d.get(p.dsem, 0) >= p.dval:
                                continue
                            seen_d[p.dsem] = p.dval
                            eng.wait_ge(dsems[p.dsem], p.dval)
                        else:
                            if p.eng == e and e == "pe":
                                continue
                            if seen_e.get(p.eng, 0) >= p.signo:
                                continue
                            seen_e[p.eng] = p.signo
                            ep = (p.signo - 1) // EPOCH
                            eng.wait_ge(esems[p.eng][ep], p.signo - ep * EPOCH)
                    if o.fn is None:
                        continue
                    if self.use_tags and o.tag:
                        with nc.named_scope(o.tag):
                            r = o.fn(eng)
                    else:
                        r = o.fn(eng)
                    if o.dma:
                        r.then_inc(dsems[o.dsem], 16)
                    elif o.sig:
                        ep = (o.signo - 1) // EPOCH
                        r.then_inc(esems[e][ep], 1)

            block.tensor(lambda eng: run("pe", eng))
            block.scalar(lambda eng: run("act", eng))
            block.vector(lambda eng: run("dve", eng))
            block.gpsimd(lambda eng: run("pool", eng))
            block.sync(lambda eng: run("sp", eng))


def build_program(nb, depth, debug=False):
    nc = bass.Bass("TRN2", target_bir_lowering=False)
    lay, npcol = param_layout(depth)
    NS = nb + 1

    def din(name, shape, dt=F32):
        return nc.dram_tensor(name, list(shape), dt, kind="ExternalInput").ap()

    xT = din("xT", [nb, D, T_LAT])
    ctxT = din("ctxT", [nb, D, T_CTX])
    csT = din("csT", [128, KC, NS])
    pp_d = din("pp", [128, npcol])
    bvb_d = din("bvb", [128, depth, 256])
    cb_d = din("cb", [128, 5, 128])
    cs_d = din("cs", [2, 128, T_LAT])
    et_d = din("et", [128, 4, 2, 8])
    w_ada = din("w_ada", [depth, D, 6 * D])
    w_in = din("w_in", [depth, D, D_IN])
    conv_pw_w = din("conv_pw_w", [depth, D, D])
    pool_w = din("pool_w", [depth, 4, 256, 256])
    w_out = din("w_out", [depth, D, D])
    w_up = din("w_up", [depth, D, 2 * D_FF])
    w_down = din("w_down", [depth, D_FF, D])
    outT = nc.dram_tensor("outT", [nb, D, T_LAT], F32, kind="ExternalOutput").ap()

    win_b = nc.dram_tensor("win_b", [depth, 60, 128, KC, 128], BF16).ap()
    pw_b = nc.dram_tensor("pw_b", [depth, 8, 128, KC, 128], BF16).ap()
    wo_b = nc.dram_tensor("wo_b", [depth, 8, 128, KC, 128], BF16).ap()
    wup_b = nc.dram_tensor("wup_b", [depth, 44, 128, KC, 128], BF16).ap()
    plw_b = nc.dram_tensor("plw_b", [depth, 4, 128, 2, 256], BF16).ap()
    wd_b = nc.dram_tensor("wd_b", [depth, 2, NJF, 128, 512], BF16).ap()

    S = Sched(nc)
    S.use_tags = (debug == "prof")
    st = contextlib.ExitStack()

    def sb(name, shape, dt):
        return st.enter_context(nc.sbuf_tensor(name, list(shape), dt))

    def pst(name):
        return st.enter_context(nc.psum_tensor(name, [128, 512], F32))

    X = sb("X", [128, KC, T_ALL], F32)
    KT = sb("KT", [128, 2, T_ALL], BF16)
    V = sb("V", [128, T_ALL // 128, 256], BF16)
    HC = sb("HC", [128, KC, EXT], BF16)
    HS = sb("HS", [128, KC, 16], BF16)
    PP = sb("PP", [128, npcol], F32)
    ADA = sb("ADA", [128, depth, 6, KC, NS], F32)
    CST = sb("CST", [128, KC, NS], F32)
    CB = sb("CB", [128, 5, 128], BF16)
    ET = sb("ET", [128, 4, 2, 8], F32)
    COS = sb("COS", [128, CH], F32)
    SIN = sb("SIN", [128, CH], F32)
    NWS = 9
    WS = sb("WS", [128, NWS, KC * 128], BF16)
    ARENA = sb("ARENA", [128, 4, 8, EXT], BF16)
    NPT = 5
    PT = sb("PT", [128, NPT, CH], BF16)
    NT32 = 7
    T32 = sb("T32", [128, NT32, EXT], F32)
    MURS = sb("MURS", [128, 2, CH], F32)
    GB = sb("GB", [128, depth, 24], F32)
    NT16 = 4
    T16 = sb("T16", [128, NT16, CH], BF16)
    NDG = 16
    DG = sb("DG", [128, NDG, 128], BF16)
    PL = sb("PL", [128, 2, 2, CH], BF16)
    NPG = 4 if (KOPT & 2) else 2
    PG = [pst(f"pg{i}") for i in range(NPG)]
    PD = [pst(f"pd{i}") for i in range(4)]

    ctr = {"pg": 0, "ws": 0, "pt": 0, "t32": 0, "t16": 0, "dg": 0}

    def nxt(name, n):
        i = ctr[name] % n
        ctr[name] += 1
        return i

    def pg():
        i = nxt("pg", NPG)
        return TL(PG[i][:, :], [("PG", i)])

    def pd(i):
        return TL(PD[i][:, :], [("PD", i)])

    def pstat(i):
        return TL(PD[2 + i][:, :], [("PD", 2 + i)])

    def t32():
        i = nxt("t32", NT32)
        return TL(T32[:, i, :], [("T32", i)])

    def t16():
        i = nxt("t16", NT16)
        return TL(T16[:, i, :], [("T16", i)])

    def ptile():
        i = nxt("pt", NPT)
        return TL(PT[:, i, :], [("PT", i)])

    def dgt():
        i = nxt("dg", NDG)
        return TL(DG[:, i, :], [("DG", i)])

    def wslot(src_ap, src_key, width=KC * 128):
        i = nxt("ws", NWS)
        t = TL(WS[:, i, 0:width], [("WS", i)])
        S.dma("sp", t, TL(src_ap, [src_key]))
        return i, t

    def pcol(name, l, c):
        c0 = lay[(name, l)] + c
        return TL(PP[:, c0:c0 + 1], [("PP",)])

    IDENT = TL(CB[:, 0, :], [("CB",)])
    ONESD = TL(CB[:, 1, :], [("CB",)])
    ONESH = TL(CB[:, 2, :], [("CB",)])
    ONES1 = TL(CB[:, 3, :], [("CB",)])
    ROT = TL(CB[:, 4, :], [("CB",)])

    def Xt(j, c, a, b):
        return TL(X[:, j, a:b], [("X", j, c)])

    F32T = []
    for i4 in range(4):
        apv = ARENA[:, 1, 2 * i4:2 * i4 + 2, :].rearrange("p a b -> p (a b)").bitcast(F32)
        F32T.append(TL(apv, [("AR", 1, 2 * i4), ("AR", 1, 2 * i4 + 1)]))
    F32M = [TL(MURS[:, 0, :], [("MURS", 0)]), TL(MURS[:, 1, :], [("MURS", 1)])]

    def ar(blk, j, a=0, b=CH):
        return TL(ARENA[:, blk, j, a:b], [("AR", blk, j)])

    S.dma("sp", TL(PP[:, :], [("PP",)]), pp_d)
    S.dma("sp", TL(CST[:, :, :], [("CST",)]), csT)
    S.dma("sp", TL(ET[:, :, :, :], [("ET",)]), et_d)
    S.dma("pool", TL(CB[:, :, :], [("CB",)]), cb_d)
    cst = TL(CST[:, :, :], [("CST",)])
    S.act(cst, cst, AF.Silu)

    for l in range(depth):
        c0 = lay[("b_in", l)] + POOL_END // 128
        S.ts("dve", TL(GB[:, l, :], [("GB",)]), TL(PP[:, c0:c0 + 24], [("PP",)]), 0.5, None, ALU.mult)

    def gbcol(l, i):
        return TL(GB[:, l, i:i + 1], [("GB",)])

    for l in range(depth):
        for g in range(48):
            i = nxt("ws", NWS)
            wt = TL(WS[:, i, :].bitcast(F32), [("WS", i)])
            half = []
            ps = pg()
            for hh in range(2):
                if hh == 1:
                    i = nxt("ws", NWS)
                    wt = TL(WS[:, i, :].bitcast(F32), [("WS", i)])
                src = w_ada[l, hh * 512:(hh + 1) * 512, g * 128:(g + 1) * 128].rearrange("(k p) c -> p k c", p=128)
                S.dma("sp", TL(wt.ap.rearrange("p (k c) -> p k c", k=4), wt.keys), src)
                half.append(wt)
            pairs = []
            for hh in range(2):
                wv = half[hh].ap.rearrange("p (k c) -> p k c", k=4)
                for k4 in range(4):
                    pairs.append((TL(wv[:, k4, :], half[hh].keys), TL(CST[:, hh * 4 + k4, :], [("CST",)])))
            S.mm(TL(ps.ap[:, 0:NS], ps.keys), pairs)
            v, kc = g // 8, g % 8
            addone = 1.0 if v in (1, 4) else 0.0
            S.ts("dve", TL(ADA[:, l, v, kc, :], [("ADA",)]), TL(ps.ap[:, 0:NS], ps.keys),
                 pcol("b_ada", l, g), addone, ALU.add, ALU.add)

    for l in range(depth):
        for g in range(60):
            S.dma("pool", TL(win_b[l, g], [("win", l, g)]),
                  w_in[l, :, g * 128:(g + 1) * 128].rearrange("(k p) c -> p k c", p=128))
        for g in range(8):
            S.dma("pool", TL(pw_b[l, g], [("pw", l, g)]),
                  conv_pw_w[l, :, g * 128:(g + 1) * 128].rearrange("(k p) c -> p k c", p=128))
            S.dma("pool", TL(wo_b[l, g], [("wo", l, g)]),
                  w_out[l, :, g * 128:(g + 1) * 128].rearrange("(k p) c -> p k c", p=128))
        for g in range(4):
            S.dma("pool", TL(plw_b[l, g], [("plw", l, g)]),
                  pool_w[l, g].rearrange("(i p) o -> p i o", p=128))
        for g in range(44):
            S.dma("pool", TL(wup_b[l, g], [("wup", l, g)]),
                  w_up[l, :, g * 128:(g + 1) * 128].rearrange("(k p) c -> p k c", p=128))
        for hf in range(2):
            for j in range(NJF):
                S.dma("pool", TL(wd_b[l, hf, j], [("wd", l, hf, j)]),
                      w_down[l, j * 128:(j + 1) * 128, hf * 512:(hf + 1) * 512])

    chunks = [(i * CH, CH, False, i == 0, i == T_LAT // CH - 1, i) for i in range(T_LAT // CH)]
    chunks.append((T_LAT, T_CTX, True, True, True, T_LAT // CH))

    def load_w(kind, l, g):
        src = {"win": win_b, "pw": pw_b, "wo": wo_b, "wup": wup_b}[kind]
        i, t = wslot(src[l, g].rearrange("p k c -> p (k c)"), (kind, l, g))
        return TL(WS[:, i, :].rearrange("p (k c) -> p k c", k=KC), t.keys)

    def hc_main(n):
        return [TL(HC[:, kc, HL:HL + n], [("HC", kc)]) for kc in range(KC)]

    def proj(wt, rhs_list, n):
        ps = pg()
        o = TL(ps.ap[:, 0:n], ps.keys)
        S.mm(o, [(TL(wt.ap[:, kc, :], wt.keys), rhs_list[kc]) for kc in range(KC)])
        return o

    def layer_stats(n, eps):
        mu_v = TL(MURS[:, 0, 0:n], [("MURS", 0)])
        rs_v = TL(MURS[:, 1, 0:n], [("MURS", 1)])
        p0 = pstat(0)
        p1 = pstat(1)
        S.copy("act", mu_v, TL(p0.ap[:, 0:n], p0.keys))
        S.tt("pool", rs_v, mu_v, mu_v, ALU.mult)
        S.tt("dve", rs_v, TL(p1.ap[:, 0:n], p1.keys), rs_v, ALU.subtract)
        S.act(rs_v, rs_v, AF.Sqrt, bias=eps)
        S.recip(rs_v, rs_v)
        return mu_v, rs_v

    def qk_norm_rope(ps, n, bias, gain, out, rope):
        qf = t32()
        sq = t16()
        qf_v = TL(qf.ap[:, 0:n], qf.keys)
        sq_v = TL(sq.ap[:, 0:n], sq.keys)
        S.act(qf_v, ps, AF.Identity, bias=bias)
        S.act(sq_v, ps, AF.Square, bias=bias)
        ms = pg()
        ms_v = TL(ms.ap[:, 0:n], ms.keys)
        S.mm(ms_v, [(ONESH, sq_v)])
        rs = t32()
        rs_v = TL(rs.ap[:, 0:n], rs.keys)
        S.act(rs_v, ms_v, AF.Sqrt, bias=RMS_EPS)
        S.recip(rs_v, rs_v)
        S.stt("dve", qf_v, qf_v, gain, rs_v, ALU.mult, ALU.mult)
        if not rope:
            S.copy("pool", out, qf_v)
            return
        qb = t16()
        qb_v = TL(qb.ap[:, 0:n], qb.keys)
        S.copy("pool", qb_v, qf_v)
        rt = pg()
        rt_v = TL(rt.ap[:, 0:n], rt.keys)
        S.mm(rt_v, [(ROT, qb_v)])
        S.tt("dve", rs_v, rt_v, TL(SIN[:, 0:n], [("SIN",)]), ALU.mult)
        S.tt("pool", qf_v, qf_v, TL(COS[:, 0:n], [("COS",)]), ALU.mult)
        S.tt("pool", out, qf_v, rs_v, ALU.add)

    def load_rope(c0, n):
        S.dma("sp", TL(COS[:, 0:n], [("COS",)]), cs_d[0, :, c0:c0 + n])
        S.dma("sp", TL(SIN[:, 0:n], [("SIN",)]), cs_d[1, :, c0:c0 + n])

    def modvec(l, v, kc, src):
        return TL(ADA[:, l, v, kc, src:src + 1], [("ADA",)])

    def make_hc(l, vsh, chunk, src, halo, use_halo):
        c0, n, is_ctx, first, last, ci = chunk
        lh = halo if (use_halo and not first) else 0
        rh = halo if (use_halo and not last) else 0
        if lh:
            S.copy("pool", TL(HC[:, :, HL - lh:HL], [("HC", kc) for kc in range(KC)]),
                   TL(HS[:, :, 0:lh], [("HS",)]))
        for kc in range(KC):
            keys = [("X", kc, ci)] + ([("X", kc, ci + 1)] if rh else [])
            S.ts("pool", TL(HC[:, kc, HL:HL + n + rh], [("HC", kc)]),
                 TL(X[:, kc, c0:c0 + n + rh], keys),
                 modvec(l, vsh + 1, kc, src), modvec(l, vsh, kc, src), ALU.mult, ALU.add)
        if rh:
            S.copy("pool", TL(HS[:, :, 0:halo], [("HS",)]),
                   TL(HC[:, :, HL + n - halo:HL + n], [("HC", kc) for kc in range(KC)]))
        return lh, rh

    def ln_apply(l, gname, bname, chunk):
        c0, n, is_ctx, first, last, ci = chunk
        for j in range(KC):
            xj = Xt(j, ci, c0, c0 + n)
            rb = t16()
            rq = t16()
            rb_v = TL(rb.ap[:, 0:n], rb.keys)
            rq_v = TL(rq.ap[:, 0:n], rq.keys)
            S.copy("act", rb_v, xj)
            S.act(rq_v, xj, AF.Square)
            p0, p1 = pstat(0), pstat(1)
            S.mm(TL(p0.ap[:, 0:n], p0.keys), [(ONESD, rb_v)], start=(j == 0), stop=(j == KC - 1))
            S.mm(TL(p1.ap[:, 0:n], p1.keys), [(ONESD, rq_v)], start=(j == 0), stop=(j == KC - 1))
        mu, rs = layer_stats(n, LN_EPS)
        for j in range(KC):
            xj = Xt(j, ci, c0, c0 + n)
            t = t32()
            tv = TL(t.ap[:, 0:n], t.keys)
            S.tt("dve", tv, xj, mu, ALU.subtract)
            S.tt("dve", tv, tv, rs, ALU.mult)
            S.act(xj, tv, AF.Identity, bias=pcol(bname, l, j), scale=pcol(gname, l, j))

    out_dmas = []

    def dbg(name, tl, shape, dt):
        if not debug:
            return
        dd = nc.dram_tensor("dbg_" + name, list(shape), dt, kind="ExternalOutput").ap()
        out_dmas.append(S.dma("sp", dd, tl))

    dbg("ada", TL(ADA[:, :, :, :, :], [("ADA",)]), [128, depth, 6, KC, NS], F32)
    for b in range(nb):
        allx = [("X", j, c) for j in range(KC) for c in range(len(chunks))]
        S.dma("sp", TL(X[:, :, 0:T_LAT], [("X", j, c) for j in range(KC) for c in range(4)]),
              xT[b].rearrange("(k p) t -> p k t", p=128))
        S.dma("sp", TL(X[:, :, T_LAT:T_ALL], [("X", j, 4) for j in range(KC)]),
              ctxT[b].rearrange("(k p) t -> p k t", p=128))
        for l in range(depth):
            lastl = (l == depth - 1)
            for chunk in chunks:
                c0, n, is_ctx, first, last, ci = chunk
                src = nb if is_ctx else b
                S.tag = f"l{l}c{ci}_00kv"
                make_hc(l, 0, chunk, src, 0, False)
                hm = hc_main(n)
                if not is_ctx:
                    load_rope(c0, n)
                for hk in range(2):
                    wt = load_w("win", l, Q_END // 128 + hk)
                    ps = proj(wt, hm, n)
                    qk_norm_rope(ps, n, pcol("b_in", l, Q_END // 128 + hk), pcol("k_gain", l, 0),
                                 TL(KT[:, hk, c0:c0 + n], [("KT", hk, ci)]), not is_ctx)
                wv = [load_w("win", l, K_END // 128 + i) for i in range(2)]
                for tb in range(n // 128):
                    ps = pg()
                    for i in range(2):
                        S.mm(TL(ps.ap[:, i * 128:(i + 1) * 128], ps.keys),
                             [(TL(HC[:, kc, HL + tb * 128:HL + (tb + 1) * 128], [("HC", kc)]),
                               TL(wv[i].ap[:, kc, :], wv[i].keys)) for kc in range(KC)])
                    kb = c0 // 128 + tb
                    S.copy("dve" if tb % 2 else "act", TL(V[:, kb, :], [("V", kb)]), TL(ps.ap[:, 0:256], ps.keys))
            if b == 0 and l == 0:
                dbg("kt", TL(KT[:, :, :], [("KT", hk, c) for hk in range(2) for c in range(5)]), [128, 2, T_ALL], BF16)
                dbg("v", TL(V[:, :, :], [("V", kb) for kb in range(18)]), [128, 18, 256], BF16)
            for chunk in chunks:
                c0, n, is_ctx, first, last, ci = chunk
                if is_ctx and lastl:
                    continue
                src = nb if is_ctx else b
                S.tag = f"l{l}c{ci}_01hc"
                lh, rh = make_hc(l, 0, chunk, src, HL, True)
                hm = hc_main(n)
                if not is_ctx:
                    load_rope(c0, n)
                S.tag = f"l{l}c{ci}_02glu"
                for j in range(KC):
                    wa = load_w("win", l, V_END // 128 + j)
                    wg = load_w("win", l, V_END // 128 + 8 + j)
                    pa = proj(wa, hm, n)
                    pgt = proj(wg, hm, n)
                    ba = pcol("b_in", l, V_END // 128 + j)
                    bg = pcol("b_in", l, V_END // 128 + 8 + j)
                    sg = t32()
                    sg_v = TL(sg.ap[:, 0:n], sg.keys)
                    S.act(sg_v, pgt, AF.Sigmoid, bias=bg)
                    S.stt("dve", ar(0, j, HL, HL + n), pa, ba, sg_v, ALU.add, ALU.mult)
                    sides = []
                    if lh:
                        sides.append((0, HL, 0))
                    if rh:
                        sides.append((HL + n, HL + n + HL, 1))
                    if sides:
                        ph = pg()
                        for (a0, a1, si) in sides:
                            for (wt, off) in ((wa, 0), (wg, 32)):
                                S.mm(TL(ph.ap[:, off + si * HL: off + (si + 1) * HL], ph.keys),
                                     [(TL(wt.ap[:, kc, :], wt.keys), TL(HC[:, kc, a0:a1], [("HC", kc)]))
                                      for kc in range(KC)])
                        sh = t32()
                        for (a0, a1, si) in sides:
                            S.act(TL(sh.ap[:, si * HL:(si + 1) * HL], sh.keys),
                                  TL(ph.ap[:, 32 + si * HL:32 + (si + 1) * HL], ph.keys), AF.Sigmoid, bias=bg)
                            S.stt("dve", ar(0, j, a0, a1), TL(ph.ap[:, si * HL:(si + 1) * HL], ph.keys), ba,
                                  TL(sh.ap[:, si * HL:(si + 1) * HL], sh.keys), ALU.add, ALU.mult)
                    if not lh:
                        S.memset("pool", ar(0, j, 0, HL), 0.0)
                    if not rh:
                        S.memset("pool", ar(0, j, HL + n, HL + n + HL), 0.0)
                S.tag = f"l{l}c{ci}_03dw"
                for j in range(KC):
                    ps = pg()
                    ps_v = TL(ps.ap[:, 0:n], ps.keys)
                    taps = list(range(31))
                    for s0 in range(0, 31, 8):
                        grp = taps[s0:s0 + 8]
                        pairs = []
                        for k in grp:
                            dg = dgt()
                            if KOPT & 1:
                                S.ts("pool" if k % 2 else "dve", dg, IDENT, pcol("conv_dw_w", l, k * 8 + j), 1.0, ALU.mult, ALU.mult)
                            else:
                                S.ts("pool", dg, IDENT, pcol("conv_dw_w", l, k * 8 + j), None, ALU.mult)
                            pairs.append((dg, ar(0, j, k, k + n)))
                        S.mm(ps_v, pairs, start=(s0 == 0), stop=(s0 + 8 >= 31))
                    bd = pcol("conv_dw_b", l, j)
                    S.act(ar(1, j, 0, n), ps_v, AF.Identity, bias=bd)
                    sq = t16()
                    sq_v = TL(sq.ap[:, 0:n], sq.keys)
                    S.act(sq_v, ps_v, AF.Square, bias=bd)
                    p0, p1 = pstat(0), pstat(1)
                    S.mm(TL(p0.ap[:, 0:n], p0.keys), [(ONESD, ar(1, j, 0, n))], start=(j == 0), stop=(j == KC - 1))
                    S.mm(TL(p1.ap[:, 0:n], p1.keys), [(ONESD, sq_v)], start=(j == 0), stop=(j == KC - 1))
                S.tag = f"l{l}c{ci}_04cln"
                mu, rs = layer_stats(n, LN_EPS)
                for j in range(KC):
                    t = t32()
                    tv = TL(t.ap[:, 0:n], t.keys)
                    S.tt("dve", tv, ar(1, j, 0, n), mu, ALU.subtract)
                    S.tt("dve", tv, tv, rs, ALU.mult)
                    S.act(ar(2, j, 0, n), tv, AF.Silu, bias=pcol("conv_ln_b", l, j), scale=pcol("conv_ln_g", l, j))
                convh = [ar(2, kc, 0, n) for kc in range(KC)]
                kbs = list(range(T_LAT // 128, T_ALL // 128)) if is_ctx else list(range(T_ALL // 128))
                rope = not is_ctx
                tagp = f"l{l}c{ci}_"

                def q_chain(j):
                    p = j % 2
                    S.tag = tagp + "06q"
                    qf_v = TL(F32T[2 * p].ap[:, 0:n], F32T[2 * p].keys)
                    rs_v = TL(F32T[2 * p + 1].ap[:, 0:n], F32T[2 * p + 1].keys)
                    sq_v = TL(T16[:, p, 0:n], [("T16", p)])
                    out = ar(0, j, 0, n)
                    bias = pcol("b_in", l, j)
                    wq = load_w("win", l, j)
                    pq = proj(wq, hm, n)
                    S.act(qf_v, pq, AF.Identity, bias=bias)
                    S.act(sq_v, pq, AF.Square, bias=bias)
                    yield
                    S.tag = tagp + "06q"
                    ms = pg()
                    ms_v = TL(ms.ap[:, 0:n], ms.keys)
                    S.mm(ms_v, [(ONESH, sq_v)])
                    S.act(rs_v, ms_v, AF.Sqrt, bias=RMS_EPS)
                    S.recip(rs_v, rs_v)
                    S.stt("dve", qf_v, qf_v, pcol("q_gain", l, 0), rs_v, ALU.mult, ALU.mult)
                    if not rope:
                        S.copy("pool", out, qf_v)
                        return
                    S.copy("pool", sq_v, qf_v)
                    yield
                    S.tag = tagp + "06q"
                    rt = pg()
                    rt_v = TL(rt.ap[:, 0:n], rt.keys)
                    S.mm(rt_v, [(ROT, sq_v)])
                    S.tt("dve", rs_v, rt_v, TL(SIN[:, 0:n], [("SIN",)]), ALU.mult)
                    S.tt("pool", qf_v, qf_v, TL(COS[:, 0:n], [("COS",)]), ALU.mult)
                    S.tt("pool", out, qf_v, rs_v, ALU.add)

                def poolf_chain(g):
                    w = POOL_WINDOWS[g]
                    plh = 8 if not first else 0
                    prh = 8 if not last else 0
                    gp = g % 2
                    for jj in range(2):
                        S.tag = tagp + "05poolf"
                        jc = 2 * g + jj
                        wp = load_w("win", l, CONV_END // 128 + jc)
                        bp = pcol("b_in", l, CONV_END // 128 + jc)
                        pu = proj(wp, hm, n)
                        u = TL(T32[:, 4, :], [("T32", 4)])
                        S.act(TL(u.ap[:, 8:8 + n], u.keys), pu, AF.Identity, bias=bp)
                        hs = []
                        if plh:
                            hs.append((HL - 8, HL, 0, 0))
                        if prh:
                            hs.append((HL + n, HL + n + 8, 8 + n, 1))
                        if hs:
                            yield
                            S.tag = tagp + "05poolf"
                            ph = pg()
                            for (a0, a1, d0, si) in hs:
                                S.mm(TL(ph.ap[:, si * 8:(si + 1) * 8], ph.keys),
                                     [(TL(wp.ap[:, kc, :], wp.keys), TL(HC[:, kc, a0:a1], [("HC", kc)]))
                                      for kc in range(KC)])
                            for (a0, a1, d0, si) in hs:
                                S.act(TL(u.ap[:, d0:d0 + 8], u.keys), TL(ph.ap[:, si * 8:(si + 1) * 8], ph.keys),
                                      AF.Identity, bias=bp)
                        if not plh:
                            S.memset("pool", TL(u.ap[:, 0:8], u.keys), 0.0)
                        if not prh:
                            S.memset("pool", TL(u.ap[:, 8 + n:16 + n], u.keys), 0.0)
                        ln_ = 16 + n
                        cur = u
                        step = 1
                        ab = [TL(T32[:, 5, :], [("T32", 5)]), TL(T32[:, 6, :], [("T32", 6)])]
                        si_ = 0
                        while step < w:
                            nx = ab[si_ % 2]
                            si_ += 1
                            S.tt("pool" if step % 2 else "dve", TL(nx.ap[:, 0:ln_ - step], nx.keys),
                                 TL(cur.ap[:, 0:ln_ - step], cur.keys), TL(cur.ap[:, step:ln_], cur.keys), ALU.add)
                            ln_ -= step
                            cur = nx
                            step *= 2
                        o0 = 8 - w // 2
                        mean = ab[si_ % 2]
                        S.ts("dve", TL(mean.ap[:, 0:n], mean.keys), TL(cur.ap[:, o0:o0 + n], cur.keys),
                             1.0 / w, None, ALU.mult)
                        if first:
                            S.tt("dve", TL(mean.ap[:, 0:8], mean.keys), TL(mean.ap[:, 0:8], mean.keys),
                                 TL(ET[:, g, 0, :], [("ET",)]), ALU.mult)
                        if last:
                            S.tt("dve", TL(mean.ap[:, n - 8:n], mean.keys), TL(mean.ap[:, n - 8:n], mean.keys),
                                 TL(ET[:, g, 1, :], [("ET",)]), ALU.mult)
                        S.tt("pool", TL(PL[:, gp, jj, 0:n], [("PL", gp, jj)]), TL(mean.ap[:, 0:n], mean.keys),
                             TL(u.ap[:, 8:8 + n], u.keys), ALU.subtract)
                        yield

                def merge_chain(j):
                    p = j % 2
                    g = j // 2
                    gp = g % 2
                    S.tag = tagp + "08merge"
                    po = pd(2 * p)
                    pden = pd(2 * p + 1)
                    po_v = TL(po.ap[:, 0:n], po.keys)
                    pden_v = TL(pden.ap[:, 0:n], pden.keys)
                    rd_v = TL(T32[:, 2 * p, 0:n], [("T32", 2 * p)])
                    sg_v = TL(T32[:, 2 * p + 1, 0:n], [("T32", 2 * p + 1)])
                    S.recip(rd_v, pden_v)
                    S.tt("dve", rd_v, po_v, rd_v, ALU.mult)
                    yield
                    S.tag = tagp + "08merge"
                    wg0 = load_w("win", l, POOL_END // 128 + j)
                    pz = proj(wg0, hm, n)
                    S.act(sg_v, pz, AF.Tanh, bias=gbcol(l, j), scale=0.5)
                    S.ts("dve", sg_v, sg_v, 0.5, 0.5, ALU.mult, ALU.add)
                    S.stt("dve", rd_v, rd_v, pcol("b_in", l, K_END // 128 + j // 4), sg_v, ALU.add, ALU.mult)
                    yield
                    S.tag = tagp + "08merge"
                    wg1 = load_w("win", l, POOL_END // 128 + 8 + j)
                    pz = proj(wg1, hm, n)
                    sg1 = sg_v
                    S.act(sg1, pz, AF.Tanh, bias=gbcol(l, 8 + j), scale=0.5)
                    S.ts("pool", sg1, sg1, 0.5, 0.5, ALU.mult, ALU.add)
                    yield
                    S.tag = tagp + "08merge"
                    wpw = load_w("pw", l, j)
                    pc = proj(wpw, convh, n)
                    S.stt("dve", sg1, pc, pcol("conv_pw_b", l, j), sg1, ALU.add, ALU.mult)
                    S.tt("pool", rd_v, rd_v, sg1, ALU.add)
                    yield
                    S.tag = tagp + "08merge"
                    wg2 = load_w("win", l, POOL_END // 128 + 16 + j)
                    pz = proj(wg2, hm, n)
                    S.act(sg_v, pz, AF.Tanh, bias=gbcol(l, 16 + j), scale=0.5)
                    S.ts("pool", sg_v, sg_v, 0.5, 0.5, ALU.mult, ALU.add)
                    yield
                    S.tag = tagp + "08merge"
                    i, pwt = wslot(plw_b[l, g].rearrange("p i o -> p (i o)"), ("plw", l, g), 512)
                    plw_t = TL(WS[:, i, 0:512].rearrange("p (i o) -> p i o", i=2), pwt.keys)
                    ppo = pg()
                    ppo_v = TL(ppo.ap[:, 0:n], ppo.keys)
                    oc = j % 2
                    S.mm(ppo_v, [(TL(plw_t.ap[:, ic, oc * 128:(oc + 1) * 128], plw_t.keys),
                                  TL(PL[:, gp, ic, 0:n], [("PL", gp, ic)])) for ic in range(2)])
                    S.stt("dve", sg_v, ppo_v, pcol("pool_scale", l, j), sg_v, ALU.mult, ALU.mult)
                    S.tt("dve", ar(3, j, 0, n), rd_v, sg_v, ALU.add)

                def drain(tasks):
                    for t in tasks:
                        for _ in t:
                            pass

                def attention(j, tasks):
                    kv = j // 4
                    p = j % 2
                    qt_v = ar(0, j, 0, n)
                    po = pd(2 * p)
                    pden = pd(2 * p + 1)
                    po_v = TL(po.ap[:, 0:n], po.keys)
                    pden_v = TL(pden.ap[:, 0:n], pden.keys)
                    sts = {}
                    live = list(tasks)

                    def issue_st(ki_):
                        kb_ = kbs[ki_]
                        stp = pg()
                        sv = TL(stp.ap[:, 0:n], stp.keys)
                        S.mm(sv, [(TL(KT[:, kv, kb_ * 128:(kb_ + 1) * 128], [("KT", kv, min(kb_ // 4, 4))]), qt_v)])
                        sts[ki_] = sv
                    S.tag = tagp + "07attn"
                    issue_st(0)
                    rr = 0
                    for ki, kb in enumerate(kbs):
                        S.tag = tagp + "07attn"
                        if ki + 1 < len(kbs):
                            issue_st(ki + 1)
                        st_v = sts.pop(ki)
                        pt = ptile()
                        pt_v = TL(pt.ap[:, 0:n], pt.keys)
                        S.act(pt_v, st_v, AF.Exp, scale=ATTN_SCALE)
                        S.mm(po_v, [(TL(V[:, kb, kv * 128:(kv + 1) * 128], [("V", kb)]), pt_v)],
                             start=(ki == 0), stop=(ki == len(kbs) - 1))
                        S.mm(pden_v, [(ONES1, pt_v)], start=(ki == 0), stop=(ki == len(kbs) - 1))
                        if live:
                            t = live[rr % len(live)]
                            try:
                                next(t)
                                rr += 1
                            except StopIteration:
                                live.remove(t)
                    drain(live)

                drain([q_chain(0)])
                for j in range(KC):
                    tasks = []
                    if j + 1 < KC:
                        tasks.append(q_chain(j + 1))
                    if j >= 1:
                        tasks.append(merge_chain(j - 1))
                    if j == 0:
                        tasks.append(poolf_chain(0))
                    elif j % 2 == 1 and j + 1 < KC:
                        tasks.append(poolf_chain((j + 1) // 2))
                    attention(j, tasks)
                drain([merge_chain(KC - 1)])
                if b == 0 and l == 0 and ci in (0, 1):
                    dbg("arena%d" % ci, TL(ARENA[:, :, :, :], [("AR", bb, jj) for bb in range(4) for jj in range(8)]), [128, 4, 8, EXT], BF16)
                S.tag = f"l{l}c{ci}_09wout"
                mb = [ar(3, kc, 0, n) for kc in range(KC)]
                for j in range(KC):
                    wo = load_w("wo", l, j)
                    py = proj(wo, mb, n)
                    t = t32()
                    tv = TL(t.ap[:, 0:n], t.keys)
                    S.ts("dve", tv, py, pcol("b_out", l, j), modvec(l, 2, j, src), ALU.add, ALU.mult)
                    xj = Xt(j, ci, c0, c0 + n)
                    S.stt("dve", xj, xj, ALPHA, tv, ALU.mult, ALU.add)
                S.tag = f"l{l}c{ci}_10ln1"
                ln_apply(l, "ln1_g", "ln1_b", chunk)
            if b == 0 and l == 0:
                dbg("xmix", TL(X[:, :, :], allx), [128, KC, T_ALL], F32)
            for chunk in chunks:
                c0, n, is_ctx, first, last, ci = chunk
                if is_ctx and lastl:
                    continue
                src = nb if is_ctx else b
                S.tag = f"l{l}c{ci}_11fhc"
                lh, rh = make_hc(l, 3, chunk, src, 1, True)
                hm = hc_main(n)
                S.tag = f"l{l}c{ci}_12fup"
                def ffn_a(j):
                    wa = load_w("wup", l, j)
                    pa = proj(wa, hm, n)
                    a = t32()
                    S.copy("act", TL(a.ap[:, 1:1 + n], a.keys), pa)
                    hs = []
                    if lh:
                        hs.append((HL - 1, HL, 0, 0))
                    if rh:
                        hs.append((HL + n, HL + n + 1, n + 1, 1))
                    if hs:
                        ph = pg()
                        for (a0, a1, d0, si) in hs:
                            S.mm(TL(ph.ap[:, si:si + 1], ph.keys),
                                 [(TL(wa.ap[:, kc, :], wa.keys), TL(HC[:, kc, a0:a1], [("HC", kc)]))
                                  for kc in range(KC)])
                        for (a0, a1, d0, si) in hs:
                            S.copy("act", TL(a.ap[:, d0:d0 + 1], a.keys), TL(ph.ap[:, si:si + 1], ph.keys))
                    if not lh:
                        S.memset("pool", TL(a.ap[:, 0:1], a.keys), 0.0)
                    if not rh:
                        S.memset("pool", TL(a.ap[:, n + 1:n + 2], a.keys), 0.0)
                    return a

                def ffn_b(j, a):
                    t = t32()
                    tv = TL(t.ap[:, 0:n], t.keys)
                    S.ts("dve", tv, TL(a.ap[:, 1:1 + n], a.keys), pcol("ffn_dw_w", l, 1 * NJF + j),
                         pcol("ffn_dw_b", l, j), ALU.mult, ALU.add)
                    S.stt("dve", tv, TL(a.ap[:, 0:n], a.keys), pcol("ffn_dw_w", l, 0 * NJF + j), tv, ALU.mult, ALU.add)
                    S.stt("dve", tv, TL(a.ap[:, 2:2 + n], a.keys), pcol("ffn_dw_w", l, 2 * NJF + j), tv, ALU.mult, ALU.add)
                    S.act(tv, tv, AF.Silu)
                    wu = load_w("wup", l, NJF + j)
                    pu = proj(wu, hm, n)
                    S.tt("dve", ar(j // 8, j % 8, 0, n), tv, pu, ALU.mult)

                a_next = ffn_a(0)
                for j in range(NJF):
                    a_cur = a_next
                    if j + 1 < NJF:
                        a_next = ffn_a(j + 1)
                    ffn_b(j, a_cur)
                S.tag = f"l{l}c{ci}_13fdown"
                for hf in range(2):
                    for j in range(NJF):
                        i, wt = wslot(wd_b[l, hf, j], ("wd", l, hf, j), 512)
                        for i4 in range(4):
                            acc = pd(i4)
                            S.mm(TL(acc.ap[:, 0:n], acc.keys),
                                 [(TL(WS[:, i, i4 * 128:(i4 + 1) * 128], wt.keys), ar(j // 8, j % 8, 0, n))],
                                 start=(j == 0), stop=(j == NJF - 1))
                    for i4 in range(4):
                        jo = hf * 4 + i4
                        acc = pd(i4)
                        t = t32()
                        tv = TL(t.ap[:, 0:n], t.keys)
                        S.ts("dve", tv, TL(acc.ap[:, 0:n], acc.keys), modvec(l, 5, jo, src), None, ALU.mult)
                        xj = Xt(jo, ci, c0, c0 + n)
                        S.stt("dve", xj, xj, ALPHA, tv, ALU.mult, ALU.add)
                S.tag = f"l{l}c{ci}_14ln2"
                ln_apply(l, "ln2_g", "ln2_b", chunk)
                if lastl and not is_ctx:
                    d = S.dma("pool", outT[b, :, c0:c0 + n].rearrange("(k p) t -> p k t", p=128),
                              TL(X[:, :, c0:c0 + n], [("X", j, ci) for j in range(KC)]))
                    out_dmas.append(d)

    S.emit(out_dmas)
    st.close()
    return nc


_CACHE = {}


def _get_program(nb, depth, debug=False):
    key = (nb, depth, debug)
    if key not in _CACHE:
        _CACHE[key] = build_program(nb, depth, debug)
    return _CACHE[key]


def make_in_maps(inputs, ncores, nb, depth):
    f = lambda a: np.ascontiguousarray(np.asarray(a, dtype=np.float32))
    x = f(inputs["x"])
    ctx = f(inputs["ctx"])
    c = f(inputs["c"])
    c_ctx = f(inputs["c_ctx"])
    pp = pack_params(inputs, depth)
    cb, cs, et = const_tables()
    bvb = np.ascontiguousarray(np.broadcast_to(
        f(inputs["b_in"])[None, :depth, K_END:V_END], (128, depth, 256)))
    shared = {
        "pp": pp, "bvb": bvb, "cb": cb, "cs": cs, "et": et,
        "w_ada": f(inputs["w_ada"])[:depth], "w_in": f(inputs["w_in"])[:depth],
        "conv_pw_w": f(inputs["conv_pw_w"])[:depth], "pool_w": f(inputs["pool_w"])[:depth],
        "w_out": f(inputs["w_out"])[:depth], "w_up": f(inputs["w_up"])[:depth],
        "w_down": f(inputs["w_down"])[:depth],
    }
    maps = []
    for i in range(ncores):
        sl = slice(i * nb, (i + 1) * nb)
        cc = np.concatenate([c[sl], c_ctx[None, :]], axis=0)
        csT = np.ascontiguousarray(cc.reshape(nb + 1, KC, 128).transpose(2, 1, 0))
        m = dict(shared)
        m["xT"] = np.ascontiguousarray(x[sl].transpose(0, 2, 1))
        m["ctxT"] = np.ascontiguousarray(ctx[sl].transpose(0, 2, 1))
        m["csT"] = csT
        maps.append(m)
    return maps


def run(inputs, ncores, nb, depth, debug=False):
    nc = _get_program(nb, depth, debug)
    maps = make_in_maps(inputs, ncores, nb, depth)
    res = run_bass_kernel_spmd(nc, maps, core_ids=list(range(ncores)))
    if debug:
        global DBG
        DBG = {k: np.asarray(v) for k, v in res.results[0].items() if k.startswith("dbg_")}
    outs = [np.asarray(r["outT"]).transpose(0, 2, 1) for r in res.results]
    return np.ascontiguousarray(np.concatenate(outs, axis=0).astype(np.float32))


def kernel(**inputs):
    return run(inputs, 8, BATCH // 8, DEPTH)
```

```python
import contextlib
import numpy as np
import concourse.bass as bass
import concourse.mybir as mybir
from concourse.bass_utils import run_bass_kernel_spmd

F32 = mybir.dt.float32
BF16 = mybir.dt.bfloat16
AF = mybir.ActivationFunctionType
ALU = mybir.AluOpType

D = 1024
KC = 8
T_LAT = 2048
T_CTX = 256
T_ALL = T_LAT + T_CTX
CH = 512
DEPTH = 4
BATCH = 32
D_FF = 2816
NJF = D_FF // 128
D_IN = 7680
Q_END, K_END, V_END, CONV_END, POOL_END = 1024, 1280, 1536, 3584, 4608
POOL_WINDOWS = (2, 4, 8, 16)
ALPHA = float((2 * DEPTH) ** 0.25)
LN_EPS = 1e-5
RMS_EPS = 1e-6
ATTN_SCALE = float(128 ** -0.5)
HL = 15
EXT = 544


def param_layout(depth):
    lay = {}
    col = 0
    spec = [("b_ada", 48), ("b_in", 60), ("q_gain", 1), ("k_gain", 1), ("conv_dw_w", 31 * 8),
            ("conv_dw_b", 8), ("conv_ln_g", 8), ("conv_ln_b", 8), ("conv_pw_b", 8), ("pool_scale", 8),
            ("b_out", 8), ("ln1_g", 8), ("ln1_b", 8), ("ln2_g", 8), ("ln2_b", 8),
            ("ffn_dw_w", 3 * NJF), ("ffn_dw_b", NJF)]
    for l in range(depth):
        for name, n in spec:
            lay[(name, l)] = col
            col += n
    return lay, col


def pack_params(inputs, depth):
    lay, ncol = param_layout(depth)
    pp = np.zeros((128, ncol), np.float32)

    def put(name, l, arr2d):
        a = np.asarray(arr2d, np.float32)
        m = a.shape[0]
        nch = a.shape[1] // 128
        blk = a.reshape(m, nch, 128).transpose(2, 0, 1).reshape(128, m * nch)
        c0 = lay[(name, l)]
        pp[:, c0:c0 + m * nch] = blk

    for l in range(depth):
        for name in ("b_ada", "b_in", "q_gain", "k_gain", "conv_dw_b", "conv_ln_g", "conv_ln_b",
                     "conv_pw_b", "pool_scale", "b_out", "ln1_g", "ln1_b", "ln2_g", "ln2_b", "ffn_dw_b"):
            put(name, l, np.asarray(inputs[name][l])[None, :])
        put("conv_dw_w", l, inputs["conv_dw_w"][l])
        put("ffn_dw_w", l, inputs["ffn_dw_w"][l])
    return pp


def const_tables():
    cb = np.zeros((128, 5, 128), np.float32)
    cb[:, 0, :] = np.eye(128, dtype=np.float32)
    cb[:, 1, :] = 1.0 / 1024.0
    cb[:, 2, :] = 1.0 / 128.0
    cb[:, 3, :] = 1.0
    rot = np.zeros((128, 128), np.float32)
    for a in range(2):
        for i in range(32):
            rot[a * 64 + 32 + i, a * 64 + i] = -1.0
            rot[a * 64 + i, a * 64 + 32 + i] = 1.0
    cb[:, 4, :] = rot
    t = np.arange(T_LAT)
    row = (t // 64).astype(np.float32)
    colp = (t % 64).astype(np.float32)
    inv_freq = (np.float32(10000.0) ** (-np.arange(32, dtype=np.float32) / np.float32(32))).astype(np.float32)
    cs = np.zeros((2, 128, T_LAT), np.float32)
    for p in range(128):
        pos = row if p < 64 else colp
        ang = (pos * inv_freq[p % 32]).astype(np.float32)
        cs[0, p] = np.cos(ang)
        cs[1, p] = np.sin(ang)
    et = np.ones((128, 4, 2, 8), np.float32)
    n = 4096
    for wi, w in enumerate(POOL_WINDOWS):
        for i in range(8):
            tt = i
            lo = max(tt - w // 2, 0)
            hi = min(tt - w // 2 + w, n)
            et[:, wi, 0, i] = np.float32(w) / np.float32(hi - lo)
            tt = n - 8 + i
            lo = max(tt - w // 2, 0)
            hi = min(tt - w // 2 + w, n)
            et[:, wi, 1, i] = np.float32(w) / np.float32(hi - lo)
    return cb, cs, et


class TL:
    __slots__ = ("ap", "keys")

    def __init__(self, ap, keys):
        self.ap = ap
        self.keys = tuple(keys)

    def v(self, ap):
        return TL(ap, self.keys)


def _ap(x):
    return x.ap if isinstance(x, TL) else x


class _Op:
    __slots__ = ("eng", "fn", "deps", "dma", "sig", "signo", "dsem", "dval", "tag")

    def __init__(self, eng, fn, deps, dma):
        self.eng = eng
        self.fn = fn
        self.deps = deps
        self.dma = dma
        self.sig = False
        self.signo = 0
        self.dsem = 0
        self.dval = 0


EPOCH = 30000
FAST_RECIP = False
KOPT = 7
ND = 16


class Sched:
    def __init__(self, nc):
        self.nc = nc
        self.ops = []
        self.lastw = {}
        self.readers = {}
        self.dmas = []
        self.tag = None
        self.use_tags = False

    def op(self, eng, fn, ins=(), outs=(), dma=False):
        deps = set()
        for x in ins:
            if isinstance(x, TL):
                for k in x.keys:
                    w = self.lastw.get(k)
                    if w is not None:
                        deps.add(w)
        for x in outs:
            if isinstance(x, TL):
                for k in x.keys:
                    w = self.lastw.get(k)
                    if w is not None:
                        deps.add(w)
                    for r in self.readers.get(k, ()):
                        deps.add(r)
        idx = len(self.ops)
        if dma:
            k = len(self.dmas)
            if k >= ND:
                deps.add(self.dmas[k - ND])
            self.dmas.append(idx)
        o = _Op(eng, fn, sorted(deps), dma)
        o.tag = self.tag
        if dma:
            k = len(self.dmas) - 1
            o.dsem = k % ND
            o.dval = 16 * (k // ND + 1)
        for d in o.deps:
            self.ops[d].sig = True
        self.ops.append(o)
        for x in ins:
            if isinstance(x, TL):
                for k in x.keys:
                    self.readers.setdefault(k, []).append(idx)
        for x in outs:
            if isinstance(x, TL):
                for k in x.keys:
                    self.lastw[k] = idx
                    self.readers[k] = []
        return idx

    def act(self, out, in_, func, bias=0.0, scale=1.0, eng="act"):
        o, i, b, s = _ap(out), _ap(in_), _ap(bias), _ap(scale)
        self.op("act", lambda e: e.activation(out=o, in_=i, func=func, bias=b, scale=s),
                (in_, bias, scale), (out,))

    def tt(self, eng, out, in0, in1, op):
        o, a, b = _ap(out), _ap(in0), _ap(in1)
        self.op(eng, lambda e: e.tensor_tensor(out=o, in0=a, in1=b, op=op), (in0, in1), (out,))

    def ts(self, eng, out, in0, s1, s2, op0, op1=None):
        o, a, x1, x2 = _ap(out), _ap(in0), _ap(s1), _ap(s2)
        if op1 is None:
            self.op(eng, lambda e: e.tensor_scalar(out=o, in0=a, scalar1=x1, scalar2=None, op0=op0),
                    (in0, s1), (out,))
        else:
            self.op(eng, lambda e: e.tensor_scalar(out=o, in0=a, scalar1=x1, scalar2=x2, op0=op0, op1=op1),
                    (in0, s1, s2), (out,))

    def stt(self, eng, out, in0, sc, in1, op0, op1):
        o, a, s, b = _ap(out), _ap(in0), _ap(sc), _ap(in1)
        self.op(eng, lambda e: e.scalar_tensor_tensor(out=o, in0=a, scalar=s, in1=b, op0=op0, op1=op1),
                (in0, sc, in1), (out,))

    def copy(self, eng, out, in_):
        o, i = _ap(out), _ap(in_)
        if eng == "act":
            self.op(eng, lambda e: e.copy(out=o, in_=i), (in_,), (out,))
        else:
            self.op(eng, lambda e: e.tensor_copy(out=o, in_=i), (in_,), (out,))

    def memset(self, eng, out, val):
        o = _ap(out)
        self.op(eng, lambda e: e.memset(o, val), (), (out,))

    def recip(self, out, in_):
        o, i = _ap(out), _ap(in_)
        if FAST_RECIP:
            self.op("dve", lambda e: e.reciprocal_approx_fast(out=o, in_=i), (in_,), (out,))
        else:
            self.op("dve", lambda e: e.reciprocal(out=o, in_=i), (in_,), (out,))

    def mm(self, out, pairs, start=True, stop=True):
        o = _ap(out)
        ps = [(_ap(a), _ap(b)) for a, b in pairs]
        n = len(ps)

        def fn(e):
            r = None
            for i, (a, b) in enumerate(ps):
                r = e.matmul(o, a, b, start=(start and i == 0), stop=(stop and i == n - 1))
            return r
        ins = [a for a, _ in pairs] + [b for _, b in pairs]
        if not start:
            ins.append(out)
        self.op("pe", fn, ins, (out,))

    def dma(self, eng, out, in_):
        o, i = _ap(out), _ap(in_)
        return self.op(eng, lambda e: e.dma_start(out=o, in_=i), (in_,), (out,), dma=True)

    def emit(self, final_deps):
        nc = self.nc
        engs = ["pe", "act", "dve", "pool", "sp"]
        fin = _Op("sp", None, sorted(final_deps), False)
        fin.tag = None
        for d in fin.deps:
            self.ops[d].sig = True
        self.ops.append(fin)
        cnt = {e: 0 for e in engs}
        for o in self.ops:
            if o.dma or not o.sig:
                continue
            cnt[o.eng] += 1
            o.signo = cnt[o.eng]
        with contextlib.ExitStack() as st:
            esems = {}
            for e in engs:
                nep = max(1, (cnt[e] + EPOCH - 1) // EPOCH)
                esems[e] = [st.enter_context(nc.semaphore(f"s_{e}{i}")) for i in range(nep)]
            dsems = [st.enter_context(nc.semaphore(f"s_dma{i}")) for i in range(ND)]
            block = st.enter_context(nc.Block())
            per = {e: [o for o in self.ops if o.eng == e] for e in engs}
            ops = self.ops

            def run(e, eng):
                seen_e = {}
                seen_d = {}
                for o in per[e]:
                    for d in o.deps:
                        p = ops[d]
                        if p.dma:
                            if seen_d.get(p.dsem, 0) >= p.dval:
                                continue
                            seen_d[p.dsem] = p.dval
                            eng.wait_ge(dsems[p.dsem], p.dval)
                        else:
                            if p.eng == e and e == "pe":
                                continue
                            if seen_e.get(p.eng, 0) >= p.signo:
                                continue
                            seen_e[p.eng] = p.signo
                            ep = (p.signo - 1) // EPOCH
                            eng.wait_ge(esems[p.eng][ep], p.signo - ep * EPOCH)
                    if o.fn is None:
                        continue
                    if self.use_tags and o.tag:
                        with nc.named_scope(o.tag):
                            r = o.fn(eng)
                    else:
                        r = o.fn(eng)
                    if o.dma:
                        r.then_inc(dsems[o.dsem], 16)
                    elif o.sig:
                        ep = (o.signo - 1) // EPOCH
                        r.then_inc(esems[e][ep], 1)

            block.tensor(lambda eng: run("pe", eng))
            block.scalar(lambda eng: run("act", eng))
            block.vector(lambda eng: run("dve", eng))
            block.gpsimd(lambda eng: run("pool", eng))
            block.sync(lambda eng: run("sp", eng))


def build_program(nb, depth, debug=False):
    nc = bass.Bass("TRN2", target_bir_lowering=False)
    lay, npcol = param_layout(depth)
    NS = nb + 1

    def din(name, shape, dt=F32):
        return nc.dram_tensor(name, list(shape), dt, kind="ExternalInput").ap()

    xT = din("xT", [nb, D, T_LAT])
    ctxT = din("ctxT", [nb, D, T_CTX])
    csT = din("csT", [128, KC, NS])
    pp_d = din("pp", [128, npcol])
    bvb_d = din("bvb", [128, depth, 256])
    cb_d = din("cb", [128, 5, 128])
    cs_d = din("cs", [2, 128, T_LAT])
    et_d = din("et", [128, 4, 2, 8])
    w_ada = din("w_ada", [depth, D, 6 * D])
    w_in = din("w_in", [depth, D, D_IN])
    conv_pw_w = din("conv_pw_w", [depth, D, D])
    pool_w = din("pool_w", [depth, 4, 256, 256])
    w_out = din("w_out", [depth, D, D])
    w_up = din("w_up", [depth, D, 2 * D_FF])
    w_down = din("w_down", [depth, D_FF, D])
    outT = nc.dram_tensor("outT", [nb, D, T_LAT], F32, kind="ExternalOutput").ap()

    win_b = nc.dram_tensor("win_b", [depth, 60, 128, KC, 128], BF16).ap()
    pw_b = nc.dram_tensor("pw_b", [depth, 8, 128, KC, 128], BF16).ap()
    wo_b = nc.dram_tensor("wo_b", [depth, 8, 128, KC, 128], BF16).ap()
    wup_b = nc.dram_tensor("wup_b", [depth, 44, 128, KC, 128], BF16).ap()
    plw_b = nc.dram_tensor("plw_b", [depth, 4, 128, 2, 256], BF16).ap()
    wd_b = nc.dram_tensor("wd_b", [depth, 2, NJF, 128, 512], BF16).ap()

    S = Sched(nc)
    S.use_tags = (debug == "prof")
    st = contextlib.ExitStack()

    def sb(name, shape, dt):
        return st.enter_context(nc.sbuf_tensor(name, list(shape), dt))

    def pst(name):
        return st.enter_context(nc.psum_tensor(name, [128, 512], F32))

    X = sb("X", [128, KC, T_ALL], F32)
    KT = sb("KT", [128, 2, T_ALL], BF16)
    V = sb("V", [128, T_ALL // 128, 256], BF16)
    HC = sb("HC", [128, KC, EXT], BF16)
    HS = sb("HS", [128, KC, 16], BF16)
    PP = sb("PP", [128, npcol], F32)
    ADA = sb("ADA", [128, depth, 6, KC, NS], F32)
    CST = sb("CST", [128, KC, NS], F32)
    CB = sb("CB", [128, 5, 128], BF16)
    ET = sb("ET", [128, 4, 2, 8], F32)
    COS = sb("COS", [128, CH], F32)
    SIN = sb("SIN", [128, CH], F32)
    NWS = 9
    WS = sb("WS", [128, NWS, KC * 128], BF16)
    ARENA = sb("ARENA", [128, 4, 8, EXT], BF16)
    NPT = 5
    PT = sb("PT", [128, NPT, CH], BF16)
    NT32 = 7
    T32 = sb("T32", [128, NT32, EXT], F32)
    MURS = sb("MURS", [128, 2, CH], F32)
    GB = sb("GB", [128, depth, 24], F32)
    NT16 = 4
    T16 = sb("T16", [128, NT16, CH], BF16)
    NDG = 16
    DG = sb("DG", [128, NDG, 128], BF16)
    PL = sb("PL", [128, 2, 2, CH], BF16)
    NPG = 4 if (KOPT & 2) else 2
    PG = [pst(f"pg{i}") for i in range(NPG)]
    PD = [pst(f"pd{i}") for i in range(4)]

    ctr = {"pg": 0, "ws": 0, "pt": 0, "t32": 0, "t16": 0, "dg": 0}

    def nxt(name, n):
        i = ctr[name] % n
        ctr[name] += 1
        return i

    def pg():
        i = nxt("pg", NPG)
        return TL(PG[i][:, :], [("PG", i)])

    def pd(i):
        return TL(PD[i][:, :], [("PD", i)])

    def pstat(i):
        return TL(PD[2 + i][:, :], [("PD", 2 + i)])

    def t32():
        i = nxt("t32", NT32)
        return TL(T32[:, i, :], [("T32", i)])

    def t16():
        i = nxt("t16", NT16)
        return TL(T16[:, i, :], [("T16", i)])

    def ptile():
        i = nxt("pt", NPT)
        return TL(PT[:, i, :], [("PT", i)])

    def dgt():
        i = nxt("dg", NDG)
        return TL(DG[:, i, :], [("DG", i)])

    def wslot(src_ap, src_key, width=KC * 128):
        i = nxt("ws", NWS)
        t = TL(WS[:, i, 0:width], [("WS", i)])
        S.dma("sp", t, TL(src_ap, [src_key]))
        return i, t

    def pcol(name, l, c):
        c0 = lay[(name, l)] + c
        return TL(PP[:, c0:c0 + 1], [("PP",)])

    IDENT = TL(CB[:, 0, :], [("CB",)])
    ONESD = TL(CB[:, 1, :], [("CB",)])
    ONESH = TL(CB[:, 2, :], [("CB",)])
    ONES1 = TL(CB[:, 3, :], [("CB",)])
    ROT = TL(CB[:, 4, :], [("CB",)])

    def Xt(j, c, a, b):
        return TL(X[:, j, a:b], [("X", j, c)])

    F32T = []
    for i4 in range(4):
        apv = ARENA[:, 1, 2 * i4:2 * i4 + 2, :].rearrange("p a b -> p (a b)").bitcast(F32)
        F32T.append(TL(apv, [("AR", 1, 2 * i4), ("AR", 1, 2 * i4 + 1)]))
    F32M = [TL(MURS[:, 0, :], [("MURS", 0)]), TL(MURS[:, 1, :], [("MURS", 1)])]

    def ar(blk, j, a=0, b=CH):
        return TL(ARENA[:, blk, j, a:b], [("AR", blk, j)])

    S.dma("sp", TL(PP[:, :], [("PP",)]), pp_d)
    S.dma("sp", TL(CST[:, :, :], [("CST",)]), csT)
    S.dma("sp", TL(ET[:, :, :, :], [("ET",)]), et_d)
    S.dma("pool", TL(CB[:, :, :], [("CB",)]), cb_d)
    cst = TL(CST[:, :, :], [("CST",)])
    S.act(cst, cst, AF.Silu)

    for l in range(depth):
        c0 = lay[("b_in", l)] + POOL_END // 128
        S.ts("dve", TL(GB[:, l, :], [("GB",)]), TL(PP[:, c0:c0 + 24], [("PP",)]), 0.5, None, ALU.mult)

    def gbcol(l, i):
        return TL(GB[:, l, i:i + 1], [("GB",)])

    for l in range(depth):
        for g in range(48):
            i = nxt("ws", NWS)
            wt = TL(WS[:, i, :].bitcast(F32), [("WS", i)])
            half = []
            ps = pg()
            for hh in range(2):
                if hh == 1:
                    i = nxt("ws", NWS)
                    wt = TL(WS[:, i, :].bitcast(F32), [("WS", i)])
                src = w_ada[l, hh * 512:(hh + 1) * 512, g * 128:(g + 1) * 128].rearrange("(k p) c -> p k c", p=128)
                S.dma("sp", TL(wt.ap.rearrange("p (k c) -> p k c", k=4), wt.keys), src)
                half.append(wt)
            pairs = []
            for hh in range(2):
                wv = half[hh].ap.rearrange("p (k c) -> p k c", k=4)
                for k4 in range(4):
                    pairs.append((TL(wv[:, k4, :], half[hh].keys), TL(CST[:, hh * 4 + k4, :], [("CST",)])))
            S.mm(TL(ps.ap[:, 0:NS], ps.keys), pairs)
            v, kc = g // 8, g % 8
            addone = 1.0 if v in (1, 4) else 0.0
            S.ts("dve", TL(ADA[:, l, v, kc, :], [("ADA",)]), TL(ps.ap[:, 0:NS], ps.keys),
                 pcol("b_ada", l, g), addone, ALU.add, ALU.add)

    for l in range(depth):
        for g in range(60):
            S.dma("pool", TL(win_b[l, g], [("win", l, g)]),
                  w_in[l, :, g * 128:(g + 1) * 128].rearrange("(k p) c -> p k c", p=128))
        for g in range(8):
            S.dma("pool", TL(pw_b[l, g], [("pw", l, g)]),
                  conv_pw_w[l, :, g * 128:(g + 1) * 128].rearrange("(k p) c -> p k c", p=128))
            S.dma("pool", TL(wo_b[l, g], [("wo", l, g)]),
                  w_out[l, :, g * 128:(g + 1) * 128].rearrange("(k p) c -> p k c", p=128))
        for g in range(4):
            S.dma("pool", TL(plw_b[l, g], [("plw", l, g)]),
                  pool_w[l, g].rearrange("(i p) o -> p i o", p=128))
        for g in range(44):
            S.dma("pool", TL(wup_b[l, g], [("wup", l, g)]),
                  w_up[l, :, g * 128:(g + 1) * 128].rearrange("(k p) c -> p k c", p=128))
        for hf in range(2):
            for j in range(NJF):
                S.dma("pool", TL(wd_b[l, hf, j], [("wd", l, hf, j)]),
                      w_down[l, j * 128:(j + 1) * 128, hf * 512:(hf + 1) * 512])

    chunks = [(i * CH, CH, False, i == 0, i == T_LAT // CH - 1, i) for i in range(T_LAT // CH)]
    chunks.append((T_LAT, T_CTX, True, True, True, T_LAT // CH))

    def load_w(kind, l, g):
        src = {"win": win_b, "pw": pw_b, "wo": wo_b, "wup": wup_b}[kind]
        i, t = wslot(src[l, g].rearrange("p k c -> p (k c)"), (kind, l, g))
        return TL(WS[:, i, :].rearrange("p (k c) -> p k c", k=KC), t.keys)

    def hc_main(n):
        return [TL(HC[:, kc, HL:HL + n], [("HC", kc)]) for kc in range(KC)]

    def proj(wt, rhs_list, n):
        ps = pg()
        o = TL(ps.ap[:, 0:n], ps.keys)
        S.mm(o, [(TL(wt.ap[:, kc, :], wt.keys), rhs_list[kc]) for kc in range(KC)])
        return o

    def layer_stats(n, eps):
        mu_v = TL(MURS[:, 0, 0:n], [("MURS", 0)])
        rs_v = TL(MURS[:, 1, 0:n], [("MURS", 1)])
        p0 = pstat(0)
        p1 = pstat(1)
        S.copy("act", mu_v, TL(p0.ap[:, 0:n], p0.keys))
        S.tt("pool", rs_v, mu_v, mu_v, ALU.mult)
        S.tt("dve", rs_v, TL(p1.ap[:, 0:n], p1.keys), rs_v, ALU.subtract)
        S.act(rs_v, rs_v, AF.Sqrt, bias=eps)
        S.recip(rs_v, rs_v)
        return mu_v, rs_v

    def qk_norm_rope(ps, n, bias, gain, out, rope):
        qf = t32()
        sq = t16()
        qf_v = TL(qf.ap[:, 0:n], qf.keys)
        sq_v = TL(sq.ap[:, 0:n], sq.keys)
        S.act(qf_v, ps, AF.Identity, bias=bias)
        S.act(sq_v, ps, AF.Square, bias=bias)
        ms = pg()
        ms_v = TL(ms.ap[:, 0:n], ms.keys)
        S.mm(ms_v, [(ONESH, sq_v)])
        rs = t32()
        rs_v = TL(rs.ap[:, 0:n], rs.keys)
        S.act(rs_v, ms_v, AF.Sqrt, bias=RMS_EPS)
        S.recip(rs_v, rs_v)
        S.stt("dve", qf_v, qf_v, gain, rs_v, ALU.mult, ALU.mult)
        if not rope:
            S.copy("dve", out, qf_v)
            return
        qb = t16()
        qb_v = TL(qb.ap[:, 0:n], qb.keys)
        S.copy("dve", qb_v, qf_v)
        rt = pg()
        rt_v = TL(rt.ap[:, 0:n], rt.keys)
        S.mm(rt_v, [(ROT, qb_v)])
        S.tt("dve", rs_v, rt_v, TL(SIN[:, 0:n], [("SIN",)]), ALU.mult)
        S.tt("dve", qf_v, qf_v, TL(COS[:, 0:n], [("COS",)]), ALU.mult)
        S.tt("dve", out, qf_v, rs_v, ALU.add)

    def load_rope(c0, n):
        S.dma("sp", TL(COS[:, 0:n], [("COS",)]), cs_d[0, :, c0:c0 + n])
        S.dma("sp", TL(SIN[:, 0:n], [("SIN",)]), cs_d[1, :, c0:c0 + n])

    def modvec(l, v, kc, src):
        return TL(ADA[:, l, v, kc, src:src + 1], [("ADA",)])

    def make_hc(l, vsh, chunk, src, halo, use_halo):
        c0, n, is_ctx, first, last, ci = chunk
        lh = halo if (use_halo and not first) else 0
        rh = halo if (use_halo and not last) else 0
        if lh:
            S.copy("pool", TL(HC[:, :, HL - lh:HL], [("HC", kc) for kc in range(KC)]),
                   TL(HS[:, :, 0:lh], [("HS",)]))
        for kc in range(KC):
            keys = [("X", kc, ci)] + ([("X", kc, ci + 1)] if rh else [])
            S.ts("pool", TL(HC[:, kc, HL:HL + n + rh], [("HC", kc)]),
                 TL(X[:, kc, c0:c0 + n + rh], keys),
                 modvec(l, vsh + 1, kc, src), modvec(l, vsh, kc, src), ALU.mult, ALU.add)
        if rh:
            S.copy("pool", TL(HS[:, :, 0:halo], [("HS",)]),
                   TL(HC[:, :, HL + n - halo:HL + n], [("HC", kc) for kc in range(KC)]))
        return lh, rh

    def ln_apply(l, gname, bname, chunk):
        c0, n, is_ctx, first, last, ci = chunk
        for j in range(KC):
            xj = Xt(j, ci, c0, c0 + n)
            rb = t16()
            rq = t16()
            rb_v = TL(rb.ap[:, 0:n], rb.keys)
            rq_v = TL(rq.ap[:, 0:n], rq.keys)
            S.copy("act", rb_v, xj)
            S.act(rq_v, xj, AF.Square)
            p0, p1 = pstat(0), pstat(1)
            S.mm(TL(p0.ap[:, 0:n], p0.keys), [(ONESD, rb_v)], start=(j == 0), stop=(j == KC - 1))
            S.mm(TL(p1.ap[:, 0:n], p1.keys), [(ONESD, rq_v)], start=(j == 0), stop=(j == KC - 1))
        mu, rs = layer_stats(n, LN_EPS)
        for j in range(KC):
            xj = Xt(j, ci, c0, c0 + n)
            t = t32()
            tv = TL(t.ap[:, 0:n], t.keys)
            S.tt("dve", tv, xj, mu, ALU.subtract)
            S.tt("dve", tv, tv, rs, ALU.mult)
            S.act(xj, tv, AF.Identity, bias=pcol(bname, l, j), scale=pcol(gname, l, j))

    out_dmas = []

    def dbg(name, tl, shape, dt):
        if not debug:
            return
        dd = nc.dram_tensor("dbg_" + name, list(shape), dt, kind="ExternalOutput").ap()
        out_dmas.append(S.dma("sp", dd, tl))

    dbg("ada", TL(ADA[:, :, :, :, :], [("ADA",)]), [128, depth, 6, KC, NS], F32)
    for b in range(nb):
        allx = [("X", j, c) for j in range(KC) for c in range(len(chunks))]
        S.dma("sp", TL(X[:, :, 0:T_LAT], [("X", j, c) for j in range(KC) for c in range(4)]),
              xT[b].rearrange("(k p) t -> p k t", p=128))
        S.dma("sp", TL(X[:, :, T_LAT:T_ALL], [("X", j, 4) for j in range(KC)]),
              ctxT[b].rearrange("(k p) t -> p k t", p=128))
        for l in range(depth):
            lastl = (l == depth - 1)
            for chunk in chunks:
                c0, n, is_ctx, first, last, ci = chunk
                src = nb if is_ctx else b
                S.tag = f"l{l}c{ci}_00kv"
                make_hc(l, 0, chunk, src, 0, False)
                hm = hc_main(n)
                if not is_ctx:
                    load_rope(c0, n)
                for hk in range(2):
                    wt = load_w("win", l, Q_END // 128 + hk)
                    ps = proj(wt, hm, n)
                    qk_norm_rope(ps, n, pcol("b_in", l, Q_END // 128 + hk), pcol("k_gain", l, 0),
                                 TL(KT[:, hk, c0:c0 + n], [("KT", hk, ci)]), not is_ctx)
                wv = [load_w("win", l, K_END // 128 + i) for i in range(2)]
                for tb in range(n // 128):
                    ps = pg()
                    for i in range(2):
                        S.mm(TL(ps.ap[:, i * 128:(i + 1) * 128], ps.keys),
                             [(TL(HC[:, kc, HL + tb * 128:HL + (tb + 1) * 128], [("HC", kc)]),
                               TL(wv[i].ap[:, kc, :], wv[i].keys)) for kc in range(KC)])
                    kb = c0 // 128 + tb
                    S.copy("dve" if tb % 2 else "act", TL(V[:, kb, :], [("V", kb)]), TL(ps.ap[:, 0:256], ps.keys))
            if b == 0 and l == 0:
                dbg("kt", TL(KT[:, :, :], [("KT", hk, c) for hk in range(2) for c in range(5)]), [128, 2, T_ALL], BF16)
                dbg("v", TL(V[:, :, :], [("V", kb) for kb in range(18)]), [128, 18, 256], BF16)
            for chunk in chunks:
                c0, n, is_ctx, first, last, ci = chunk
                if is_ctx and lastl:
                    continue
                src = nb if is_ctx else b
                S.tag = f"l{l}c{ci}_01hc"
                lh, rh = make_hc(l, 0, chunk, src, HL, True)
                hm = hc_main(n)
                if not is_ctx:
                    load_rope(c0, n)
                S.tag = f"l{l}c{ci}_02glu"
                for j in range(KC):
                    wa = load_w("win", l, V_END // 128 + j)
                    wg = load_w("win", l, V_END // 128 + 8 + j)
                    pa = proj(wa, hm, n)
                    pgt = proj(wg, hm, n)
                    ba = pcol("b_in", l, V_END // 128 + j)
                    bg = pcol("b_in", l, V_END // 128 + 8 + j)
                    sg = t32()
                    sg_v = TL(sg.ap[:, 0:n], sg.keys)
                    S.act(sg_v, pgt, AF.Sigmoid, bias=bg)
                    S.stt("dve", ar(0, j, HL, HL + n), pa, ba, sg_v, ALU.add, ALU.mult)
                    sides = []
                    if lh:
                        sides.append((0, HL, 0))
                    if rh:
                        sides.append((HL + n, HL + n + HL, 1))
                    if sides:
                        ph = pg()
                        for (a0, a1, si) in sides:
                            for (wt, off) in ((wa, 0), (wg, 32)):
                                S.mm(TL(ph.ap[:, off + si * HL: off + (si + 1) * HL], ph.keys),
                                     [(TL(wt.ap[:, kc, :], wt.keys), TL(HC[:, kc, a0:a1], [("HC", kc)]))
                                      for kc in range(KC)])
                        sh = t32()
                        for (a0, a1, si) in sides:
                            S.act(TL(sh.ap[:, si * HL:(si + 1) * HL], sh.keys),
                                  TL(ph.ap[:, 32 + si * HL:32 + (si + 1) * HL], ph.keys), AF.Sigmoid, bias=bg)
                            S.stt("dve", ar(0, j, a0, a1), TL(ph.ap[:, si * HL:(si + 1) * HL], ph.keys), ba,
                                  TL(sh.ap[:, si * HL:(si + 1) * HL], sh.keys), ALU.add, ALU.mult)
                    if not lh:
                        S.memset("pool", ar(0, j, 0, HL), 0.0)
                    if not rh:
                        S.memset("pool", ar(0, j, HL + n, HL + n + HL), 0.0)
                S.tag = f"l{l}c{ci}_03dw"
                for j in range(KC):
                    ps = pg()
                    ps_v = TL(ps.ap[:, 0:n], ps.keys)
                    taps = list(range(31))
                    for s0 in range(0, 31, 8):
                        grp = taps[s0:s0 + 8]
                        pairs = []
                        for k in grp:
                            dg = dgt()
                            if KOPT & 1:
                                S.ts("pool" if k % 2 else "dve", dg, IDENT, pcol("conv_dw_w", l, k * 8 + j), 1.0, ALU.mult, ALU.mult)
                            else:
                                S.ts("pool", dg, IDENT, pcol("conv_dw_w", l, k * 8 + j), None, ALU.mult)
                            pairs.append((dg, ar(0, j, k, k + n)))
                        S.mm(ps_v, pairs, start=(s0 == 0), stop=(s0 + 8 >= 31))
                    bd = pcol("conv_dw_b", l, j)
                    S.act(ar(1, j, 0, n), ps_v, AF.Identity, bias=bd)
                    sq = t16()
                    sq_v = TL(sq.ap[:, 0:n], sq.keys)
                    S.act(sq_v, ps_v, AF.Square, bias=bd)
                    p0, p1 = pstat(0), pstat(1)
                    S.mm(TL(p0.ap[:, 0:n], p0.keys), [(ONESD, ar(1, j, 0, n))], start=(j == 0), stop=(j == KC - 1))
                    S.mm(TL(p1.ap[:, 0:n], p1.keys), [(ONESD, sq_v)], start=(j == 0), stop=(j == KC - 1))
                S.tag = f"l{l}c{ci}_04cln"
                mu, rs = layer_stats(n, LN_EPS)
                for j in range(KC):
                    t = t32()
                    tv = TL(t.ap[:, 0:n], t.keys)
                    S.tt("dve", tv, ar(1, j, 0, n), mu, ALU.subtract)
                    S.tt("dve", tv, tv, rs, ALU.mult)
                    S.act(ar(2, j, 0, n), tv, AF.Silu, bias=pcol("conv_ln_b", l, j), scale=pcol("conv_ln_g", l, j))
                convh = [ar(2, kc, 0, n) for kc in range(KC)]
                kbs = list(range(T_LAT // 128, T_ALL // 128)) if is_ctx else list(range(T_ALL // 128))
                rope = not is_ctx
                tagp = f"l{l}c{ci}_"

                def q_chain(j):
                    p = j % 2
                    S.tag = tagp + "06q"
                    qf_v = TL(F32T[2 * p].ap[:, 0:n], F32T[2 * p].keys)
                    rs_v = TL(F32T[2 * p + 1].ap[:, 0:n], F32T[2 * p + 1].keys)
                    sq_v = TL(T16[:, p, 0:n], [("T16", p)])
                    out = ar(0, j, 0, n)
                    bias = pcol("b_in", l, j)
                    wq = load_w("win", l, j)
                    pq = proj(wq, hm, n)
                    S.act(qf_v, pq, AF.Identity, bias=bias)
                    S.act(sq_v, pq, AF.Square, bias=bias)
                    yield
                    S.tag = tagp + "06q"
                    ms = pg()
                    ms_v = TL(ms.ap[:, 0:n], ms.keys)
                    S.mm(ms_v, [(ONESH, sq_v)])
                    S.act(rs_v, ms_v, AF.Sqrt, bias=RMS_EPS)
                    S.recip(rs_v, rs_v)
                    S.stt("dve", qf_v, qf_v, pcol("q_gain", l, 0), rs_v, ALU.mult, ALU.mult)
                    if not rope:
                        S.copy("dve", out, qf_v)
                        return
                    S.copy("dve", sq_v, qf_v)
                    yield
                    S.tag = tagp + "06q"
                    rt = pg()
                    rt_v = TL(rt.ap[:, 0:n], rt.keys)
                    S.mm(rt_v, [(ROT, sq_v)])
                    S.tt("dve", rs_v, rt_v, TL(SIN[:, 0:n], [("SIN",)]), ALU.mult)
                    S.tt("dve", qf_v, qf_v, TL(COS[:, 0:n], [("COS",)]), ALU.mult)
                    S.tt("dve", out, qf_v, rs_v, ALU.add)

                def poolf_chain(g):
                    w = POOL_WINDOWS[g]
                    plh = 8 if not first else 0
                    prh = 8 if not last else 0
                    gp = g % 2
                    for jj in range(2):
                        S.tag = tagp + "05poolf"
                        jc = 2 * g + jj
                        wp = load_w("win", l, CONV_END // 128 + jc)
                        bp = pcol("b_in", l, CONV_END // 128 + jc)
                        pu = proj(wp, hm, n)
                        u = TL(T32[:, 4, :], [("T32", 4)])
                        S.act(TL(u.ap[:, 8:8 + n], u.keys), pu, AF.Identity, bias=bp)
                        hs = []
                        if plh:
                            hs.append((HL - 8, HL, 0, 0))
                        if prh:
                            hs.append((HL + n, HL + n + 8, 8 + n, 1))
                        if hs:
                            yield
                            S.tag = tagp + "05poolf"
                            ph = pg()
                            for (a0, a1, d0, si) in hs:
                                S.mm(TL(ph.ap[:, si * 8:(si + 1) * 8], ph.keys),
                                     [(TL(wp.ap[:, kc, :], wp.keys), TL(HC[:, kc, a0:a1], [("HC", kc)]))
                                      for kc in range(KC)])
                            for (a0, a1, d0, si) in hs:
                                S.act(TL(u.ap[:, d0:d0 + 8], u.keys), TL(ph.ap[:, si * 8:(si + 1) * 8], ph.keys),
                                      AF.Identity, bias=bp)
                        if not plh:
                            S.memset("pool", TL(u.ap[:, 0:8], u.keys), 0.0)
                        if not prh:
                            S.memset("pool", TL(u.ap[:, 8 + n:16 + n], u.keys), 0.0)
                        ln_ = 16 + n
                        cur = u
                        step = 1
                        ab = [TL(T32[:, 5, :], [("T32", 5)]), TL(T32[:, 6, :], [("T32", 6)])]
                        si_ = 0
                        while step < w:
                            nx = ab[si_ % 2]
                            si_ += 1
                            S.tt("dve", TL(nx.ap[:, 0:ln_ - step], nx.keys),
                                 TL(cur.ap[:, 0:ln_ - step], cur.keys), TL(cur.ap[:, step:ln_], cur.keys), ALU.add)
                            ln_ -= step
                            cur = nx
                            step *= 2
                        o0 = 8 - w // 2
                        mean = ab[si_ % 2]
                        S.ts("dve", TL(mean.ap[:, 0:n], mean.keys), TL(cur.ap[:, o0:o0 + n], cur.keys),
                             1.0 / w, None, ALU.mult)
                        if first:
                            S.tt("dve", TL(mean.ap[:, 0:8], mean.keys), TL(mean.ap[:, 0:8], mean.keys),
                                 TL(ET[:, g, 0, :], [("ET",)]), ALU.mult)
                        if last:
                            S.tt("dve", TL(mean.ap[:, n - 8:n], mean.keys), TL(mean.ap[:, n - 8:n], mean.keys),
                                 TL(ET[:, g, 1, :], [("ET",)]), ALU.mult)
                        S.tt("dve", TL(PL[:, gp, jj, 0:n], [("PL", gp, jj)]), TL(mean.ap[:, 0:n], mean.keys),
                             TL(u.ap[:, 8:8 + n], u.keys), ALU.subtract)
                        yield

                def merge_chain(j):
                    p = j % 2
                    g = j // 2
                    gp = g % 2
                    S.tag = tagp + "08merge"
                    po = pd(2 * p)
                    pden = pd(2 * p + 1)
                    po_v = TL(po.ap[:, 0:n], po.keys)
                    pden_v = TL(pden.ap[:, 0:n], pden.keys)
                    rd_v = TL(T32[:, 2 * p, 0:n], [("T32", 2 * p)])
                    sg_v = TL(T32[:, 2 * p + 1, 0:n], [("T32", 2 * p + 1)])
                    S.recip(rd_v, pden_v)
                    S.tt("dve", rd_v, po_v, rd_v, ALU.mult)
                    yield
                    S.tag = tagp + "08merge"
                    wg0 = load_w("win", l, POOL_END // 128 + j)
                    pz = proj(wg0, hm, n)
                    S.act(sg_v, pz, AF.Tanh, bias=gbcol(l, j), scale=0.5)
                    S.ts("dve", sg_v, sg_v, 0.5, 0.5, ALU.mult, ALU.add)
                    S.stt("dve", rd_v, rd_v, pcol("b_in", l, K_END // 128 + j // 4), sg_v, ALU.add, ALU.mult)
                    yield
                    S.tag = tagp + "08merge"
                    wg1 = load_w("win", l, POOL_END // 128 + 8 + j)
                    pz = proj(wg1, hm, n)
                    sg1 = sg_v
                    S.act(sg1, pz, AF.Tanh, bias=gbcol(l, 8 + j), scale=0.5)
                    S.ts("dve", sg1, sg1, 0.5, 0.5, ALU.mult, ALU.add)
                    yield
                    S.tag = tagp + "08merge"
                    wpw = load_w("pw", l, j)
                    pc = proj(wpw, convh, n)
                    S.stt("dve", sg1, pc, pcol("conv_pw_b", l, j), sg1, ALU.add, ALU.mult)
                    S.tt("dve", rd_v, rd_v, sg1, ALU.add)
                    yield
                    S.tag = tagp + "08merge"
                    wg2 = load_w("win", l, POOL_END // 128 + 16 + j)
                    pz = proj(wg2, hm, n)
                    S.act(sg_v, pz, AF.Tanh, bias=gbcol(l, 16 + j), scale=0.5)
                    S.ts("dve", sg_v, sg_v, 0.5, 0.5, ALU.mult, ALU.add)
                    yield
                    S.tag = tagp + "08merge"
                    i, pwt = wslot(plw_b[l, g].rearrange("p i o -> p (i o)"), ("plw", l, g), 512)
                    plw_t = TL(WS[:, i, 0:512].rearrange("p (i o) -> p i o", i=2), pwt.keys)
                    ppo = pg()
                    ppo_v = TL(ppo.ap[:, 0:n], ppo.keys)
                    oc = j % 2
                    S.mm(ppo_v, [(TL(plw_t.ap[:, ic, oc * 128:(oc + 1) * 128], plw_t.keys),
                                  TL(PL[:, gp, ic, 0:n], [("PL", gp, ic)])) for ic in range(2)])
                    S.stt("dve", sg_v, ppo_v, pcol("pool_scale", l, j), sg_v, ALU.mult, ALU.mult)
                    S.tt("dve", ar(3, j, 0, n), rd_v, sg_v, ALU.add)

                def drain(tasks):
                    for t in tasks:
                        for _ in t:
                            pass

                def attention(j, tasks):
                    kv = j // 4
                    p = j % 2
                    qt_v = ar(0, j, 0, n)
                    po = pd(2 * p)
                    pden = pd(2 * p + 1)
                    po_v = TL(po.ap[:, 0:n], po.keys)
                    pden_v = TL(pden.ap[:, 0:n], pden.keys)
                    sts = {}
                    live = list(tasks)

                    def issue_st(ki_):
                        kb_ = kbs[ki_]
                        stp = pg()
                        sv = TL(stp.ap[:, 0:n], stp.keys)
                        S.mm(sv, [(TL(KT[:, kv, kb_ * 128:(kb_ + 1) * 128], [("KT", kv, min(kb_ // 4, 4))]), qt_v)])
                        sts[ki_] = sv
                    S.tag = tagp + "07attn"
                    issue_st(0)
                    rr = 0
                    for ki, kb in enumerate(kbs):
                        S.tag = tagp + "07attn"
                        if ki + 1 < len(kbs):
                            issue_st(ki + 1)
                        st_v = sts.pop(ki)
                        pt = ptile()
                        pt_v = TL(pt.ap[:, 0:n], pt.keys)
                        S.act(pt_v, st_v, AF.Exp, scale=ATTN_SCALE)
                        S.mm(po_v, [(TL(V[:, kb, kv * 128:(kv + 1) * 128], [("V", kb)]), pt_v)],
                             start=(ki == 0), stop=(ki == len(kbs) - 1))
                        S.mm(pden_v, [(ONES1, pt_v)], start=(ki == 0), stop=(ki == len(kbs) - 1))
                        if live:
                            t = live[rr % len(live)]
                            try:
                                next(t)
                                rr += 1
                            except StopIteration:
                                live.remove(t)
                    drain(live)

                drain([q_chain(0)])
                for j in range(KC):
                    tasks = []
                    if j + 1 < KC:
                        tasks.append(q_chain(j + 1))
                    if j >= 1:
                        tasks.append(merge_chain(j - 1))
                    if j == 0:
                        tasks.append(poolf_chain(0))
                    elif j % 2 == 1 and j + 1 < KC:
                        tasks.append(poolf_chain((j + 1) // 2))
                    attention(j, tasks)
                drain([merge_chain(KC - 1)])
                if b == 0 and l == 0 and ci in (0, 1):
                    dbg("arena%d" % ci, TL(ARENA[:, :, :, :], [("AR", bb, jj) for bb in range(4) for jj in range(8)]), [128, 4, 8, EXT], BF16)
                S.tag = f"l{l}c{ci}_09wout"
                mb = [ar(3, kc, 0, n) for kc in range(KC)]
                for j in range(KC):
                    wo = load_w("wo", l, j)
                    py = proj(wo, mb, n)
                    t = t32()
                    tv = TL(t.ap[:, 0:n], t.keys)
                    S.ts("dve", tv, py, pcol("b_out", l, j), modvec(l, 2, j, src), ALU.add, ALU.mult)
                    xj = Xt(j, ci, c0, c0 + n)
                    S.stt("dve", xj, xj, ALPHA, tv, ALU.mult, ALU.add)
                S.tag = f"l{l}c{ci}_10ln1"
                ln_apply(l, "ln1_g", "ln1_b", chunk)
            if b == 0 and l == 0:
                dbg("xmix", TL(X[:, :, :], allx), [128, KC, T_ALL], F32)
            for chunk in chunks:
                c0, n, is_ctx, first, last, ci = chunk
                if is_ctx and lastl:
                    continue
                src = nb if is_ctx else b
                S.tag = f"l{l}c{ci}_11fhc"
                lh, rh = make_hc(l, 3, chunk, src, 1, True)
                hm = hc_main(n)
                S.tag = f"l{l}c{ci}_12fup"
                def ffn_a(j):
                    wa = load_w("wup", l, j)
                    pa = proj(wa, hm, n)
                    a = t32()
                    S.copy("act", TL(a.ap[:, 1:1 + n], a.keys), pa)
                    hs = []
                    if lh:
                        hs.append((HL - 1, HL, 0, 0))
                    if rh:
                        hs.append((HL + n, HL + n + 1, n + 1, 1))
                    if hs:
                        ph = pg()
                        for (a0, a1, d0, si) in hs:
                            S.mm(TL(ph.ap[:, si:si + 1], ph.keys),
                                 [(TL(wa.ap[:, kc, :], wa.keys), TL(HC[:, kc, a0:a1], [("HC", kc)]))
                                  for kc in range(KC)])
                        for (a0, a1, d0, si) in hs:
                            S.copy("act", TL(a.ap[:, d0:d0 + 1], a.keys), TL(ph.ap[:, si:si + 1], ph.keys))
                    if not lh:
                        S.memset("pool", TL(a.ap[:, 0:1], a.keys), 0.0)
                    if not rh:
                        S.memset("pool", TL(a.ap[:, n + 1:n + 2], a.keys), 0.0)
                    return a

                def ffn_b(j, a):
                    t = t32()
                    tv = TL(t.ap[:, 0:n], t.keys)
                    S.ts("dve", tv, TL(a.ap[:, 1:1 + n], a.keys), pcol("ffn_dw_w", l, 1 * NJF + j),
                         pcol("ffn_dw_b", l, j), ALU.mult, ALU.add)
                    S.stt("dve", tv, TL(a.ap[:, 0:n], a.keys), pcol("ffn_dw_w", l, 0 * NJF + j), tv, ALU.mult, ALU.add)
                    S.stt("dve", tv, TL(a.ap[:, 2:2 + n], a.keys), pcol("ffn_dw_w", l, 2 * NJF + j), tv, ALU.mult, ALU.add)
                    S.act(tv, tv, AF.Silu)
                    wu = load_w("wup", l, NJF + j)
                    pu = proj(wu, hm, n)
                    S.tt("dve", ar(j // 8, j % 8, 0, n), tv, pu, ALU.mult)

                a_next = ffn_a(0)
                for j in range(NJF):
                    a_cur = a_next
                    if j + 1 < NJF:
                        a_next = ffn_a(j + 1)
                    ffn_b(j, a_cur)
                S.tag = f"l{l}c{ci}_13fdown"
                for hf in range(2):
                    for j in range(NJF):
                        i, wt = wslot(wd_b[l, hf, j], ("wd", l, hf, j), 512)
                        for i4 in range(4):
                            acc = pd(i4)
                            S.mm(TL(acc.ap[:, 0:n], acc.keys),
                                 [(TL(WS[:, i, i4 * 128:(i4 + 1) * 128], wt.keys), ar(j // 8, j % 8, 0, n))],
                                 start=(j == 0), stop=(j == NJF - 1))
                    for i4 in range(4):
                        jo = hf * 4 + i4
                        acc = pd(i4)
                        t = t32()
                        tv = TL(t.ap[:, 0:n], t.keys)
                        S.ts("dve", tv, TL(acc.ap[:, 0:n], acc.keys), modvec(l, 5, jo, src), None, ALU.mult)
                        xj = Xt(jo, ci, c0, c0 + n)
                        S.stt("dve", xj, xj, ALPHA, tv, ALU.mult, ALU.add)
                S.tag = f"l{l}c{ci}_14ln2"
                ln_apply(l, "ln2_g", "ln2_b", chunk)
                if lastl and not is_ctx:
                    d = S.dma("pool", outT[b, :, c0:c0 + n].rearrange("(k p) t -> p k t", p=128),
                              TL(X[:, :, c0:c0 + n], [("X", j, ci) for j in range(KC)]))
                    out_dmas.append(d)

    S.emit(out_dmas)
    st.close()
    return nc


_CACHE = {}


def _get_program(nb, depth, debug=False):
    key = (nb, depth, debug)
    if key not in _CACHE:
        _CACHE[key] = build_program(nb, depth, debug)
    return _CACHE[key]


def make_in_maps(inputs, ncores, nb, depth):
    f = lambda a: np.ascontiguousarray(np.asarray(a, dtype=np.float32))
    x = f(inputs["x"])
    ctx = f(inputs["ctx"])
    c = f(inputs["c"])
    c_ctx = f(inputs["c_ctx"])
    pp = pack_params(inputs, depth)
    cb, cs, et = const_tables()
    bvb = np.ascontiguousarray(np.broadcast_to(
        f(inputs["b_in"])[None, :depth, K_END:V_END], (128, depth, 256)))
    shared = {
        "pp": pp, "bvb": bvb, "cb": cb, "cs": cs, "et": et,
        "w_ada": f(inputs["w_ada"])[:depth], "w_in": f(inputs["w_in"])[:depth],
        "conv_pw_w": f(inputs["conv_pw_w"])[:depth], "pool_w": f(inputs["pool_w"])[:depth],
        "w_out": f(inputs["w_out"])[:depth], "w_up": f(inputs["w_up"])[:depth],
        "w_down": f(inputs["w_down"])[:depth],
    }
    maps = []
    for i in range(ncores):
        sl = slice(i * nb, (i + 1) * nb)
        cc = np.concatenate([c[sl], c_ctx[None, :]], axis=0)
        csT = np.ascontiguousarray(cc.reshape(nb + 1, KC, 128).transpose(2, 1, 0))
        m = dict(shared)
        m["xT"] = np.ascontiguousarray(x[sl].transpose(0, 2, 1))
        m["ctxT"] = np.ascontiguousarray(ctx[sl].transpose(0, 2, 1))
        m["csT"] = csT
        maps.append(m)
    return maps


def run(inputs, ncores, nb, depth, debug=False):
    nc = _get_program(nb, depth, debug)
    maps = make_in_maps(inputs, ncores, nb, depth)
    res = run_bass_kernel_spmd(nc, maps, core_ids=list(range(ncores)))
    if debug:
        global DBG
        DBG = {k: np.asarray(v) for k, v in res.results[0].items() if k.startswith("dbg_")}
    outs = [np.asarray(r["outT"]).transpose(0, 2, 1) for r in res.results]
    return np.ascontiguousarray(np.concatenate(outs, axis=0).astype(np.float32))


def kernel(**inputs):
    return run(inputs, 8, BATCH // 8, DEPTH)
```
